# Optimizing a Trainium2 kernel written in Bass

```python
import math
import jax, jax.numpy as jnp
from jax import lax
import numpy as np

D_MODEL = 1024
BATCH = 4
SEQ = 4096
DEPTH = 2
DEC_BATCH = 8
DEC_SEQ = 64
PAST_LEN = 1024

CHUNK = 64
Q_BLOCK = 128
H_G = 4
DV_G = D_MODEL // (2 * H_G)
DK_G = DV_G // 2
GATE_RANK = 16
GATE_NORMALIZER = 16.0
H_D = 4
DH_D = D_MODEL // (4 * H_D)
NUM_BUCKETS = 32
MAX_DISTANCE = 128
D_FF = 2816
CONV_W = 3
EPS = 1e-6

PROJ_SPLITS = (H_G * DK_G, H_G * DK_G, H_G * DV_G, H_G * DV_G, GATE_RANK,
               H_D * 2 * DH_D, H_D * 2 * DH_D, H_D * 2 * DH_D)
D_PROJ = sum(PROJ_SPLITS)
MIX_WIDTH = H_G * DV_G + H_D * 2 * DH_D

kernel_name = "hybrid_gla_diffattn_streaming_step"


def rms_norm(x, w):
    xf = x.astype(jnp.float32)
    y = xf * lax.rsqrt(jnp.mean(xf * xf, axis=-1, keepdims=True) + EPS)
    return (y * w.astype(jnp.float32)).astype(x.dtype)


def lambda_init(layer):
    return 0.8 - 0.6 * math.exp(-0.3 * layer)


def t5_bias(qpos, kpos, table):
    rel = kpos[None, :] - qpos[:, None]
    nb = NUM_BUCKETS // 2
    max_exact = nb // 2
    ret = (rel > 0).astype(jnp.int32) * nb
    n = jnp.abs(rel)
    nf = jnp.maximum(n, 1).astype(jnp.float32)
    large = max_exact + (jnp.log(nf / max_exact) / math.log(MAX_DISTANCE / max_exact)
                         * (nb - max_exact)).astype(jnp.int32)
    large = jnp.minimum(large, nb - 1)
    bucket = ret + jnp.where(n < max_exact, n, large)
    return jnp.transpose(table[bucket].astype(jnp.float32), (2, 0, 1))


def gla_chunked(q, k, v, log_a, s0):
    B, T, H, _ = q.shape
    DV = v.shape[-1]
    C = min(CHUNK, T)
    N = T // C

    def blocks(t):
        return t.reshape(B, N, C, H, t.shape[-1]).transpose(1, 0, 3, 2, 4)

    q, k, v, log_a = blocks(q), blocks(k), blocks(v), blocks(log_a)
    G = jnp.cumsum(log_a, axis=-2)
    G_last = G[..., -1:, :]
    q_dec = q * jnp.exp(G)
    k_inv = k * jnp.exp(-G)
    k_end = k * jnp.exp(G_last - G)
    causal = jnp.tril(jnp.ones((C, C), dtype=bool))
    A = jnp.where(causal, jnp.einsum('nbhid,nbhjd->nbhij', q_dec, k_inv), 0.0)
    o_intra = jnp.einsum('nbhij,nbhjv->nbhiv', A, v)
    upd = jnp.einsum('nbhjd,nbhjv->nbhdv', k_end, v)
    decay = jnp.exp(G_last[..., 0, :])

    def step(S, inp):
        dec, u, qd = inp
        o = jnp.einsum('bhid,bhdv->bhiv', qd, S)
        return dec[..., None] * S + u, o

    s_final, o_inter = lax.scan(step, s0, (decay, upd, q_dec))
    o = (o_intra + o_inter).transpose(1, 0, 3, 2, 4).reshape(B, T, H, DV)
    return o, s_final


def diff_attention(q, k, v, pos0, lam, table):
    B, H, T, _, DH = q.shape
    K = k.shape[2]
    QB = min(Q_BLOCK, T)
    nblk = T // QB
    scale = DH ** -0.5
    kpos = jnp.arange(K, dtype=jnp.int32)
    qb_all = q.reshape(B, H, nblk, QB, 2, DH).transpose(2, 0, 1, 3, 4, 5)
    starts = pos0 + jnp.arange(nblk, dtype=jnp.int32) * QB

    def one(args):
        qb, start = args
        qpos = start + jnp.arange(QB, dtype=jnp.int32)
        mask = (kpos // CHUNK)[None, :] <= (qpos // CHUNK)[:, None]
        bias = t5_bias(qpos, kpos, table)
        s = jnp.einsum('bhqmd,bhkmd->mbhqk', qb, k) * scale + bias
        s = jnp.where(mask, s, -jnp.inf)
        p = jax.nn.softmax(s, axis=-1)
        a = p[0] - lam * p[1]
        return jnp.einsum('bhqk,bhkv->bhqv', a, v)

    out = lax.map(one, (qb_all, starts))
    return out.transpose(1, 2, 0, 3, 4).reshape(B, H, T, 2 * DH)


def trunk_layer(x, layer, past_k, past_v, gla_state, conv_state, t5_table,
                w_in, w_gate_up, b_gate_up, gla_norm, lam_params, diff_norm, w_out,
                ln_mix_pre, ln_mix_post, ln_ffn_pre, ln_ffn_post,
                w_ffn_up, conv_w, conv_b, w_ffn_down):
    f32 = jnp.float32
    B, T, _ = x.shape
    pos0 = past_k.shape[2]
    h = rms_norm(x, ln_mix_pre)
    proj = h @ w_in
    idx = np.cumsum(PROJ_SPLITS)[:-1].tolist()
    qg, kg, vg, rg, gd, qd, kd, vd = jnp.split(proj, idx, axis=-1)
    qg = qg.reshape(B, T, H_G, DK_G).astype(f32) * DK_G ** -0.5
    kg = kg.reshape(B, T, H_G, DK_G).astype(f32)
    vg = vg.reshape(B, T, H_G, DV_G).astype(f32)
    log_a = jax.nn.log_sigmoid((gd @ w_gate_up + b_gate_up).astype(f32)) / GATE_NORMALIZER
    log_a = log_a.reshape(B, T, H_G, DK_G)
    og, gla_new = gla_chunked(qg, kg, vg, log_a, gla_state.astype(f32))
    og = rms_norm(og, gla_norm).reshape(B, T, H_G * DV_G) * jax.nn.silu(rg.astype(f32))
    qd = qd.reshape(B, T, H_D, 2, DH_D).transpose(0, 2, 1, 3, 4).astype(f32)
    kd = kd.reshape(B, T, H_D, 2 * DH_D).transpose(0, 2, 1, 3)
    vd = vd.reshape(B, T, H_D, 2 * DH_D).transpose(0, 2, 1, 3)
    k_all = jnp.concatenate([past_k.astype(kd.dtype), kd], axis=2).astype(f32)
    k_all = k_all.reshape(B, H_D, pos0 + T, 2, DH_D)
    v_all = jnp.concatenate([past_v.astype(vd.dtype), vd], axis=2).astype(f32)
    lp = lam_params.astype(f32)
    lam_0 = lambda_init(layer)
    lam = jnp.exp(jnp.sum(lp[0] * lp[1])) - jnp.exp(jnp.sum(lp[2] * lp[3])) + lam_0
    od = diff_attention(qd, k_all, v_all, pos0, lam, t5_table)
    od = rms_norm(od, diff_norm) * (1.0 - lam_0)
    od = od.transpose(0, 2, 1, 3).reshape(B, T, H_D * 2 * DH_D)
    mix = jnp.concatenate([og, od], axis=-1).astype(x.dtype) @ w_out
    x = x + rms_norm(mix, ln_mix_post)
    h = rms_norm(x, ln_ffn_pre)
    up = h @ w_ffn_up
    ext = jnp.concatenate([conv_state.astype(up.dtype), up], axis=1)
    c = conv_b
    for i in range(CONV_W):
        c = c + conv_w[i] * ext[:, i:i + T]
    g, u = jnp.split(c, 2, axis=-1)
    y = jax.nn.gelu(g, approximate=True) * u
    x = x + rms_norm(y @ w_ffn_down, ln_ffn_post)
    return (x, kd, vd, gla_new.astype(gla_state.dtype), ext[:, -(CONV_W - 1):])


def setup_inputs(seed: int = 0) -> dict:
    key = jax.random.key(seed)
    ks = jax.random.split(key, 24)
    nrm = lambda k, shape, s=1.0: jax.random.normal(k, shape, jnp.float32) * s
    F2 = 2 * D_FF
    return {
        "x_prompt": nrm(ks[0], (BATCH, SEQ, D_MODEL)),
        "x_sample": nrm(ks[1], (DEC_BATCH, DEC_SEQ, D_MODEL)),
        "cache_k": nrm(ks[2], (DEPTH, DEC_BATCH, H_D, PAST_LEN, 2 * DH_D)),
        "cache_v": nrm(ks[3], (DEPTH, DEC_BATCH, H_D, PAST_LEN, 2 * DH_D)),
        "state_gla": nrm(ks[4], (DEPTH, DEC_BATCH, H_G, DK_G, DV_G)),
        "state_conv": nrm(ks[5], (DEPTH, DEC_BATCH, CONV_W - 1, F2)),
        "t5_table": nrm(ks[6], (NUM_BUCKETS, H_D), 0.5),
        "w_in": nrm(ks[7], (DEPTH, D_MODEL, D_PROJ), D_MODEL ** -0.5),
        "w_gate_up": nrm(ks[8], (DEPTH, GATE_RANK, H_G * DK_G), GATE_RANK ** -0.5),
        "b_gate_up": nrm(ks[9], (DEPTH, H_G * DK_G), 0.1),
        "gla_norm": 1.0 + nrm(ks[10], (DEPTH, DV_G), 0.02),
        "lam_params": nrm(ks[11], (DEPTH, 4, DH_D), 0.1),
        "diff_norm": 1.0 + nrm(ks[12], (DEPTH, 2 * DH_D), 0.02),
        "w_out": nrm(ks[13], (DEPTH, MIX_WIDTH, D_MODEL), MIX_WIDTH ** -0.5),
        "ln_mix_pre": 1.0 + nrm(ks[14], (DEPTH, D_MODEL), 0.02),
        "ln_mix_post": 1.0 + nrm(ks[15], (DEPTH, D_MODEL), 0.02),
        "ln_ffn_pre": 1.0 + nrm(ks[16], (DEPTH, D_MODEL), 0.02),
        "ln_ffn_post": 1.0 + nrm(ks[17], (DEPTH, D_MODEL), 0.02),
        "w_ffn_up": nrm(ks[18], (DEPTH, D_MODEL, F2), D_MODEL ** -0.5),
        "conv_w": nrm(ks[19], (DEPTH, CONV_W, F2), CONV_W ** -0.5),
        "conv_b": nrm(ks[20], (DEPTH, F2), 0.01),
        "w_ffn_down": nrm(ks[21], (DEPTH, D_FF, D_MODEL), D_FF ** -0.5),
    }


def reference(x_prompt, x_sample, cache_k, cache_v, state_gla, state_conv, t5_table,
              w_in, w_gate_up, b_gate_up, gla_norm, lam_params, diff_norm, w_out,
              ln_mix_pre, ln_mix_post, ln_ffn_pre, ln_ffn_post,
              w_ffn_up, conv_w, conv_b, w_ffn_down):
    dt = x_prompt.dtype

    def layer_weights(l):
        return (t5_table, w_in[l], w_gate_up[l], b_gate_up[l], gla_norm[l], lam_params[l],
                diff_norm[l], w_out[l], ln_mix_pre[l], ln_mix_post[l], ln_ffn_pre[l],
                ln_ffn_post[l], w_ffn_up[l], conv_w[l], conv_b[l], w_ffn_down[l])

    y = x_prompt
    kp, vp, gp, cp = [], [], [], []
    for l in range(DEPTH):
        empty_k = jnp.zeros((BATCH, H_D, 0, 2 * DH_D), dt)
        zero_s = jnp.zeros((BATCH, H_G, DK_G, DV_G), jnp.float32)
        zero_c = jnp.zeros((BATCH, CONV_W - 1, 2 * D_FF), dt)
        y, k_new, v_new, s_new, c_new = trunk_layer(y, l, empty_k, empty_k, zero_s, zero_c,
                                                   *layer_weights(l))
        kp.append(k_new); vp.append(v_new); gp.append(s_new); cp.append(c_new)
    y_prompt = y

    y = x_sample
    ksm, vsm, gsm, csm = [], [], [], []
    for l in range(DEPTH):
        y, k_new, v_new, s_new, c_new = trunk_layer(y, l, cache_k[l], cache_v[l], state_gla[l],
                                                   state_conv[l], *layer_weights(l))
        ksm.append(k_new); vsm.append(v_new); gsm.append(s_new); csm.append(c_new)
    y_sample = y

    return (y_prompt, y_sample,
            jnp.stack(kp), jnp.stack(vp), jnp.stack(gp), jnp.stack(cp),
            jnp.stack(ksm), jnp.stack(vsm), jnp.stack(gsm), jnp.stack(csm))
```

```python
import math
from contextlib import ExitStack

import numpy as np
import concourse.bass as bass
import concourse.mybir as mybir
from concourse.bass_utils import run_bass_kernel_spmd

F32 = mybir.dt.float32
BF16 = mybir.dt.bfloat16
AF = mybir.ActivationFunctionType
ALU = mybir.AluOpType

D = 1024
NCORES = 8
DPROJ = 3088
F2 = 5632
DFF = 2816
EPS = 1e-6


class Buf:
    __slots__ = ("name", "w", "r", "excl")

    def __init__(self, name, excl=False):
        self.name = name
        self.w = None
        self.r = {}
        self.excl = excl


class Eng:
    def __init__(self, name, be, sem):
        self.name = name
        self.be = be
        self.sem = sem
        self.count = 0
        self.pending = False
        self.seen = {}
        self.ops = []
        self.dma_n = 0


class FW:
    def __init__(self, nc, stack, n_dma_sems=12):
        self.nc = nc
        self.sems = {}
        self.E = {}
        for name, be in (("pe", nc.tensor), ("act", nc.scalar), ("dve", nc.vector),
                         ("pool", nc.gpsimd), ("sp", nc.sync)):
            self.sems["s_" + name] = stack.enter_context(nc.semaphore("s_" + name))
            self.E[name] = Eng(name, be, "s_" + name)
        self.dma_sems = {}
        for q in ("sp", "pool", "act"):
            lst = []
            for i in range(n_dma_sems):
                k = "d_%s%d" % (q, i)
                self.sems[k] = stack.enter_context(nc.semaphore(k))
                lst.append(k)
            self.dma_sems[q] = lst
        self.nd = n_dma_sems
        self.semowner = {e.sem: e for e in self.E.values()}
        self.ninst = 0

    def _deps(self, eng, reads, writes):
        need = {}
        for b in reads:
            if b.w is not None:
                k, v = b.w
                if need.get(k, 0) < v:
                    need[k] = v
            if b.excl:
                for k, v in b.r.items():
                    if k != eng.sem and need.get(k, 0) < v:
                        need[k] = v
        for b in writes:
            if b.w is not None:
                k, v = b.w
                if need.get(k, 0) < v:
                    need[k] = v
            for k, v in b.r.items():
                if need.get(k, 0) < v:
                    need[k] = v
        out = []
        for k, v in need.items():
            if k == eng.sem and eng.name == "pe":
                continue
            if eng.seen.get(k, 0) >= v:
                continue
            ow = self.semowner.get(k)
            if ow is not None:
                assert v <= ow.count, ("pending tick", eng.name, k, v, ow.count)
            eng.seen[k] = v
            out.append((k, v))
        return out

    def _mark(self, tick, reads, writes):
        k, v = tick
        for b in reads:
            if b.r.get(k, 0) < v:
                b.r[k] = v
        for b in writes:
            b.w = tick
            b.r = {}

    def op(self, en, fn, reads=(), writes=(), inc=True):
        eng = self.E[en]
        waits = self._deps(eng, reads, writes)
        if inc:
            eng.count += 1
            eng.pending = False
            tick = (eng.sem, eng.count)
        else:
            eng.pending = True
            tick = (eng.sem, eng.count + 1)
        eng.ops.append((waits, fn, eng.sem if inc else None, 1))
        self._mark(tick, reads, writes)
        self.ninst += 1

    def dma(self, q, out, in_, reads=(), writes=()):
        eng = self.E[q]
        n = eng.dma_n
        eng.dma_n += 1
        sk = self.dma_sems[q][n % self.nd]
        gen = n // self.nd
        waits = self._deps(eng, reads, writes)
        if gen > 0 and eng.seen.get(sk, 0) < 16 * gen:
            eng.seen[sk] = 16 * gen
            waits.append((sk, 16 * gen))
        tick = (sk, 16 * (gen + 1))
        eng.ops.append((waits, (lambda be: be.dma_start(out=out, in_=in_)), sk, 16))
        self._mark(tick, reads, writes)
        self.ninst += 1

    def finish(self, final_bufs):
        sp = self.E["sp"]
        waits = self._deps(sp, list(final_bufs), [])
        for q, lst in self.dma_sems.items():
            n = self.E[q].dma_n
            for i, sk in enumerate(lst):
                cnt = (n - i + self.nd - 1) // self.nd if n > i else 0
                if cnt > 0 and sp.seen.get(sk, 0) < 16 * cnt:
                    sp.seen[sk] = 16 * cnt
                    waits.append((sk, 16 * cnt))
        sp.ops.append((waits, None, None, 0))
        for e in self.E.values():
            assert not e.pending, e.name

    def emit(self):
        nc = self.nc
        sems = self.sems
        with nc.Block() as block:
            def run(eng):
                def body(be):
                    for waits, fn, sk, inc in eng.ops:
                        for k, v in waits:
                            be.wait_ge(sems[k], v)
                        if fn is None:
                            continue
                        ins = fn(be)
                        if sk is not None:
                            ins.then_inc(sems[sk], inc)
                return body
            block.tensor(run(self.E["pe"]))
            block.scalar(run(self.E["act"]))
            block.vector(run(self.E["dve"]))
            block.gpsimd(run(self.E["pool"]))
            block.sync(run(self.E["sp"]))


def _bucket_of(rel):
    rel = np.asarray(rel)
    n = np.abs(rel)
    nf = np.maximum(n, 1).astype(np.float32)
    large = 8 + (np.log(nf / np.float32(8)) / np.float32(math.log(16.0)) * np.float32(8)).astype(np.int32)
    large = np.minimum(large, 15)
    return (rel > 0).astype(np.int32) * 16 + np.where(n < 8, n, large)


C_ID, C_TRI, C_SCAN, C_OHR, C_ONES, C_J, NCW = 0, 128, 192, 1216, 1728, 1856, 1984


def _consts():
    c = np.zeros((128, NCW), np.float32)
    c[:, C_ID:C_ID + 128] = np.eye(128, dtype=np.float32)
    j = np.arange(64)[:, None]
    i = np.arange(64)[None, :]
    c[0:64, C_TRI:C_TRI + 64] = (j <= i).astype(np.float32)
    m = np.ones(1024, np.float32)
    m[::64] = 0.0
    c[0:64, C_SCAN:C_SCAN + 1024] = m[None, :]
    idx = np.arange(512)
    b = _bucket_of(127 - idx)
    oh = np.zeros((32, 512), np.float32)
    oh[b, idx] = 1.0
    c[0:32, C_OHR:C_OHR + 512] = oh
    c[:, C_ONES:C_ONES + 128] = 1.0
    c[:, C_J:C_J + 128] = np.eye(128, dtype=np.float32)[::-1]
    return c


def lam_init(l):
    return 0.8 - 0.6 * math.exp(-0.3 * l)


class Seg:
    pass


DBG = {"segs": "sp", "stop": 99}


def build_nc(TP=4096, TS=64, PS=1024, TTP=256):
    nc = bass.Bass("TRN2", target_bir_lowering=False)
    st = ExitStack()
    with st:
        _build(nc, st, TP, TS, PS, TTP)
    return nc


def _build(nc, st, TP, TS, PS, TTP):
    fw = FW(nc, st)
    TTM = max(TTP, TS)

    def din(n, s):
        return nc.dram_tensor(n, list(s), F32, kind="ExternalInput")

    def dout(n, s):
        return nc.dram_tensor(n, list(s), F32, kind="ExternalOutput")

    d_xp = din("xp", (TP, D)); d_xs = din("xs", (TS, D))
    d_ck = din("ck", (2, 4, PS, 128)); d_cv = din("cv", (2, 4, PS, 128))
    d_sg = din("sg", (2, 4, 64, 128)); d_sc = din("sc", (2, 2, F2))
    d_t5 = din("t5", (32, 4))
    d_win = din("w_in", (2, D, DPROJ)); d_wgu = din("w_gate_up", (2, 16, 256)); d_bgu = din("b_gate_up", (2, 256))
    d_gn = din("gla_norm", (2, 128)); d_lp = din("lam_params", (2, 4, 64)); d_dn = din("diff_norm", (2, 128))
    d_wout = din("w_out", (2, D, D))
    d_ln = [din(n, (2, D)) for n in ("ln_mix_pre", "ln_mix_post", "ln_ffn_pre", "ln_ffn_post")]
    d_wup = din("w_ffn_up", (2, D, F2)); d_cw = din("conv_w", (2, 3, F2)); d_cb = din("conv_b", (2, F2))
    d_wdn = din("w_ffn_down", (2, DFF, D))
    d_consts = din("consts", (128, NCW))

    o_yp = dout("yp", (TP, D)); o_ys = dout("ys", (TS, D))
    o_kp = dout("kp", (2, 4, TP, 128)); o_vp = dout("vp", (2, 4, TP, 128))
    o_gp = dout("gp", (2, 4, 64, 128)); o_cp = dout("cp", (2, 2, F2))
    o_ks = dout("ks", (2, 4, TS, 128)); o_vs = dout("vs", (2, 4, TS, 128))
    o_gs = dout("gs", (2, 4, 64, 128)); o_cs = dout("cs", (2, 2, F2))

    def dscr(n, s, dt):
        return nc.dram_tensor(n, list(s), dt)

    wb_in = dscr("wb_in", (2, D, DPROJ), BF16); wb_out = dscr("wb_out", (2, D, D), BF16)
    wb_up = dscr("wb_up", (2, D, F2), BF16); wb_dn = dscr("wb_dn", (2, DFF, D), BF16)
    ktp = dscr("ktp", (2, 4, 128, TP), BF16); vsp = dscr("vsp", (2, 4, TP, 128), BF16)
    kts = dscr("kts", (2, 4, 128, PS + TS), BF16); vss = dscr("vss", (2, 4, PS + TS, 128), BF16)
    f3d = dscr("f3d", (4, 512), F32)

    def sb(n, s, dt=F32):
        return st.enter_context(nc.sbuf_tensor("sb_" + n, list(s), dt))

    bufs = {}

    def B(name):
        if name not in bufs:
            bufs[name] = Buf(name)
        return bufs[name]

    consts = sb("consts", (128, NCW))
    identb = sb("identb", (128, 128), BF16); onesb = sb("onesb", (128, 128), BF16)
    lnv = sb("lnv", (128, 64))
    convw = sb("convw", (128, 264))
    convb = sb("convb", (128, 88))
    glan = sb("glan", (128, 2)); difn = sb("difn", (128, 2))
    nbg = sb("nbg", (64, 8))
    lpt = sb("lpt", (64, 8)); lpp = sb("lpp", (64, 4)); lame = sb("lame", (128, 4)); nlam = sb("nlam", (128, 2))
    wgu = sb("wgu", (16, 512))
    cbias = sb("cbias", (128, 4)); t5sb = sb("t5sb", (32, 4)); f3sb = sb("f3sb", (4, 512))
    trr = sb("trr", (128, 384)); strip = sb("strip", (128, 4 * 384))
    vstage = sb("vstage", (128, 128))

    xin = sb("xin", (128, 2 * 1024)); xT = sb("xT", (128, 8 * TTM))
    hT = sb("hT", (128, 8 * TTM), BF16); mixT = sb("mixT", (128, 8 * TTM), BF16)
    mo = sb("mo", (128, 8 * TTM)); sqb = [sb("sqb%d" % i, (128, TTM), BF16) for i in range(2)]
    rs = sb("rs", (128, TTM)); lnt = sb("lnt", (128, TTM))
    NSLAB = 3
    wslab = [sb("wslab%d" % i, (128, 5632), BF16) for i in range(NSLAB)]
    gdT = sb("gdT", (16, TTM))
    G = sb("G", (64, 4 * TTM)); Gc = sb("Gc", (64, 4 * TTM)); tA = sb("tA", (64, 4 * TTM)); tB = sb("tB", (64, 4 * TTM))
    qdec = sb("qdec", (64, 4 * TTM), BF16); kinv = sb("kinv", (64, 4 * TTM), BF16); kend = sb("kend", (64, 4 * TTM), BF16)
    NCH = TTM // 64
    dec = sb("dec", (64, 4 * NCH))
    kendtok = sb("kendtok", (64, NCH * 256), BF16)
    vgtok = sb("vgtok", (64, NCH * 512), BF16)
    ATsb = sb("ATsb", (64, NCH * 256), BF16)
    S = [sb("S%d" % l, (64, 512)) for l in range(2)]
    Sb = [sb("Sb%d" % l, (64, 512), BF16) for l in range(2)]
    rgs = sb("rgs", (128, 4 * TTM))
    sq4 = sb("sq4", (128, 4 * TTM), BF16)
    qpad = sb("qpad", (128, 8 * TTM), BF16); ktst = sb("ktst", (128, 4 * TTM), BF16)
    vst = sb("vst", (128, 2 * 512), BF16); kvo = sb("kvo", (128, 2 * 2 * 512))
    NG = 3
    KTg = [sb("KTg%d" % i, (128, 1024), BF16) for i in range(NG)]
    Vg = [sb("Vg%d" % i, (128, 1024), BF16) for i in range(NG)]
    PT = [sb("PT%d" % i, (128, 2 * TTM), BF16) for i in range(3)]
    PTr = [sb("PTr%d" % i, (128, 2 * TTM), BF16) for i in range(3)]
    SBK = [0, 4, 1]
    t01 = sb("t01", (128, 2 * TTM)); odr = sb("odr", (128, TTM))
    r01 = sb("r01", (128, 2 * TTM))
    NE = 3
    extg = [sb("extg%d" % i, (128, TTM + 2)) for i in range(NE)]
    extu = [sb("extu%d" % i, (128, TTM + 2)) for i in range(NE)]
    NC_ = 5
    cg = [sb("cg%d" % i, (128, TTM)) for i in range(NC_)]; cu = [sb("cu%d" % i, (128, TTM)) for i in range(NC_)]
    f1 = [sb("f1%d" % i, (128, TTM)) for i in range(2)]; f2 = [sb("f2%d" % i, (128, TTM)) for i in range(2)]
    yT = sb("yT", (128, 22 * TTM), BF16)
    hist = [sb("hist%d" % l, (128, 88)) for l in range(2)]
    ckst = kvo[:, 0:1024]; ktc = vst[:, 0:1024]
    og4 = mo[:, 0:4 * TTM]; rs4 = xin[:, 0:4 * TTM]

    ps01 = st.enter_context(nc.psum_tensor("ps01", [128, 1024], F32))
    pb23 = [st.enter_context(nc.psum_tensor("pb%d" % i, [128, 512], F32)) for i in (2, 3)]
    ps45 = st.enter_context(nc.psum_tensor("ps45", [128, 1024], F32))
    pb67 = [st.enter_context(nc.psum_tensor("pb%d" % i, [128, 512], F32)) for i in (6, 7)]
    pbank = [ps01[:, 0:512], ps01[:, 512:1024], pb23[0], pb23[1], ps45[:, 0:512], ps45[:, 512:1024], pb67[0], pb67[1]]
    psS = [ps01, ps45]
    for i in range(8):
        bufs["pb%d" % i] = Buf("pb%d" % i, excl=True)
    pbuf = [[B("pb%d" % i)] * 2 for i in range(8)]

    def PB(i):
        return [pbuf[i][0]]

    def mm(out, lhsT, rhs, start, stop, reads, writes, inc=None):
        fw.op("pe", lambda e: e.matmul(out, lhsT=lhsT, rhs=rhs, start=start, stop=stop),
              reads, writes, inc=(stop if inc is None else inc))

    def tr(out, in_, ident, reads, writes, inc=True):
        fw.op("pe", lambda e: e.transpose(out=out, in_=in_, identity=ident), reads, writes, inc=inc)

    def act(out, in_, func, reads, writes, scale=None, bias=None):
        kw = {}
        if scale is not None:
            kw["scale"] = scale
        if bias is not None:
            kw["bias"] = bias
        fw.op("act", lambda e: e.activation(out=out, in_=in_, func=func, **kw), reads, writes)

    def tt(en, out, in0, in1, op, reads, writes):
        fw.op(en, lambda e: e.tensor_tensor(out=out, in0=in0, in1=in1, op=op), reads, writes)

    def ts(en, out, in0, s1, s2, op0, op1, reads, writes):
        if op1 is None:
            fw.op(en, lambda e: e.tensor_scalar(out=out, in0=in0, scalar1=s1, scalar2=None, op0=op0), reads, writes)
        else:
            fw.op(en, lambda e: e.tensor_scalar(out=out, in0=in0, scalar1=s1, scalar2=s2, op0=op0, op1=op1), reads, writes)

    def stt(out, in0, scalar, in1, op0, op1, reads, writes):
        fw.op("dve", lambda e: e.scalar_tensor_tensor(out=out, in0=in0, scalar=scalar, in1=in1, op0=op0, op1=op1),
              reads, writes)

    def cp(en, out, in_, reads, writes):
        fw.op(en, lambda e: e.tensor_copy(out=out, in_=in_), reads, writes)

    def v3(ap, a):
        return ap.rearrange("p (a b) -> p a b", a=a)

    ident = consts[:, C_ID:C_ID + 128]
    onesf = consts[:, C_ONES:C_ONES + 128]
    Bc = B("consts")

    fw.dma("sp", consts[:], d_consts.ap(), writes=[Bc])
    cp("dve", identb[:], ident, [Bc], [B("identb")])
    cp("dve", onesb[:], onesf, [Bc], [B("onesb")])
    Bi, Bo = B("identb"), B("onesb")

    conv_order = [("in", d_win, wb_in, D), ("out", d_wout, wb_out, D), ("up", d_wup, wb_up, D), ("dn", d_wdn, wb_dn, DFF)]
    pool_wait_layer = {}
    for l in range(2):
        for name, src, dst, rows in conv_order:
            r0 = 0
            while r0 < rows:
                nr = min(256, rows - r0)
                fw.dma("pool", dst.ap()[l, r0:r0 + nr, :], src.ap()[l, r0:r0 + nr, :], writes=[])
                r0 += nr
            pw = []
            for i, sk in enumerate(fw.dma_sems["pool"]):
                n = fw.E["pool"].dma_n
                cnt = (n - i + fw.nd - 1) // fw.nd if n > i else 0
                if cnt > 0:
                    pw.append((sk, 16 * cnt))
            pool_wait_layer[(name, l)] = pw
    weights_waited = set()

    def wait_weights(l):
        if l in weights_waited:
            return
        weights_waited.add(l)
        sp = fw.E["sp"]
        w0 = []
        for k, v in pool_wait_layer[l]:
            if sp.seen.get(k, 0) < v:
                sp.seen[k] = v
                w0.append((k, v))
        sp.ops.append((w0, None, None, 0))

    def load_cols(src_ap, R, W, dst_ap, dstbuf, post=None):
        fw.dma("sp", vstage[0:R, 0:W], src_ap, writes=[B("vstage")])
        tr(pbank[7][0:W, 0:R], vstage[0:R, 0:W], consts[0:R, C_ID:C_ID + R], [B("vstage"), Bc], PB(7))
        if post is None:
            cp("dve", dst_ap, pbank[7][0:W, 0:R], PB(7), dstbuf if isinstance(dstbuf, list) else [dstbuf])
        else:
            post(pbank[7][0:W, 0:R])

    for kind in range(4):
        load_cols(d_ln[kind].ap().rearrange("l (c p) -> (l c) p", p=128), 16, 128,
                  lnv[:, kind * 16:(kind + 1) * 16], B("lnv"))
    for l in range(2):
        for tp in range(3):
            load_cols(d_cw.ap()[l, tp, :].rearrange("(c p) -> c p", p=128), 44, 128,
                      convw[:, (l * 3 + tp) * 44:(l * 3 + tp + 1) * 44], B("convw"))
        load_cols(d_cb.ap()[l, :].rearrange("(c p) -> c p", p=128), 44, 128, convb[:, l * 44:(l + 1) * 44], B("convb"))
    load_cols(d_gn.ap(), 2, 128, glan[:], B("glan"))
    load_cols(d_dn.ap(), 2, 128, difn[:], B("difn"))
    for l in range(2):
        ts("dve", difn[:, l:l + 1], difn[:, l:l + 1], 1.0 - lam_init(l), None, ALU.mult, None, [B("difn")], [B("difn")])
    load_cols(d_bgu.ap().rearrange("l (h k) -> (l h) k", k=64), 8, 64, nbg[:], B("nbg"),
              post=lambda ps: ts("dve", nbg[:], ps, -1.0, None, ALU.mult, None, PB(7), [B("nbg")]))
    load_cols(d_lp.ap().rearrange("l i k -> (l i) k"), 8, 64, lpt[:], B("lpt"))
    fw.dma("sp", v3(wgu[:], 2), d_wgu.ap().rearrange("l r c -> r l c"), writes=[B("wgu")])
    lp3 = v3(lpt[:], 4)
    tt("dve", lpp[:], lp3[:, :, 0], lp3[:, :, 1], ALU.mult, [B("lpt")], [B("lpp")])
    mm(pbank[7][:, 0:4], consts[0:64, C_ONES:C_ONES + 128], lpp[:], True, True, [Bc, B("lpp")], PB(7))
    act(lame[:], pbank[7][:, 0:4], AF.Exp, PB(7), [B("lame")])
    for l in range(2):
        tt("dve", nlam[:, l:l + 1], lame[:, 2 * l + 1:2 * l + 2], lame[:, 2 * l:2 * l + 1], ALU.subtract, [B("lame")], [B("nlam")])
        ts("dve", nlam[:, l:l + 1], nlam[:, l:l + 1], -lam_init(l), None, ALU.add, None, [B("nlam")], [B("nlam")])
    fw.dma("sp", t5sb[:], d_t5.ap(), writes=[B("t5sb")])
    fw.dma("sp", cbias[0:32, :], bass.AP(d_t5, 15 * 4, [[0, 32], [1, 4]]), writes=[B("cbias")])
    tt("dve", t5sb[:], t5sb[:], cbias[0:32, :], ALU.subtract, [B("t5sb"), B("cbias")], [B("t5sb")])
    mm(pbank[7][0:4, 0:512], t5sb[:], consts[0:32, C_OHR:C_OHR + 512], True, True, [B("t5sb"), Bc], PB(7))
    act(f3sb[:], pbank[7][0:4, 0:512], AF.Exp, PB(7), [B("f3sb")])
    fw.dma("pool", f3d.ap(), f3sb[:], reads=[B("f3sb")], writes=[B("f3d")])
    for h in range(4):
        fw.dma("sp", trr[:], bass.AP(f3d, h * 512, [[1, 128], [1, 384]]), reads=[B("f3d")], writes=[B("trr")])
        mm(pbank[7][:, 0:384], consts[:, C_J:C_J + 128], trr[:], True, True, [Bc, B("trr")], PB(7))
        cp("dve", strip[:, h * 384:(h + 1) * 384], pbank[7][:, 0:384], PB(7), [B("strip")])
        fw.op("dve", (lambda hh: (lambda e: e.memset(strip[64:128, hh * 384:hh * 384 + 64], 0.0)))(h), [], [B("strip")])

    slab_seq = []

    class SlabStream:
        def __init__(self):
            self.plan = []
            self.loaded = 0
            self.used = 0

        def add(self, pieces, wkey):
            self.plan.append((pieces, wkey))

        def ensure(self, upto):
            while self.loaded < min(upto, len(self.plan)):
                i = self.loaded
                pieces, wkey = self.plan[i]
                wait_weights(wkey)
                sl = wslab[i % NSLAB]
                for col0, c, n, src in pieces:
                    dst = sl[:, col0:col0 + c * n].rearrange("p (c n) -> p c n", c=c)
                    fw.dma("sp", dst, src, reads=[], writes=[B("wslab%d" % (i % NSLAB))])
                self.loaded += 1

        def get(self):
            i = self.used
            self.ensure(i + NSLAB)
            self.used += 1
            return wslab[i % NSLAB], B("wslab%d" % (i % NSLAB))

    ss = SlabStream()

    def w_pieces_in(l, c0, n):
        return [(0, 8, n, wb_in.ap()[l, :, c0:c0 + n].rearrange("(c p) n -> p c n", p=128))]

    IN_SLABS = [(1536, 16), (512, 512), (0, 512), (1024, 512), (1552, 512), (2064, 512), (2576, 512)]

    def plan_tile_layer(l):
        for c0, n in IN_SLABS:
            ss.add(w_pieces_in(l, c0, n), ("in", l))
        for c0 in (0, 512):
            ss.add([(0, 8, 512, wb_out.ap()[l, :, c0:c0 + 512].rearrange("(c p) n -> p c n", p=128))], ("out", l))
        for s in range(11):
            ss.add([(0, 8, 256, wb_up.ap()[l, :, s * 256:(s + 1) * 256].rearrange("(c p) n -> p c n", p=128)),
                    (2048, 8, 256, wb_up.ap()[l, :, DFF + s * 256:DFF + (s + 1) * 256].rearrange("(c p) n -> p c n", p=128))],
                   ("up", l))
        for s in range(4):
            ss.add([(0, 22, 256, wb_dn.ap()[l, :, s * 256:(s + 1) * 256].rearrange("(c p) n -> p c n", p=128))], ("dn", l))

    rot = {"sq": 0, "nt": 0, "acc": 0}
    ACC = [(0, 0), (1, 0), (4, 0), (5, 0)]

    def acc_next():
        b, hf = ACC[rot["acc"] % 4]
        rot["acc"] += 1
        return b, hf

    def pslice(b, hf, p, n):
        return pbank[b][0:p, hf * 256:hf * 256 + n]

    def rms_stats(chunks, nparts_feat, TT, tagbufs, bank=6):
        n = len(chunks)
        for i, (ap, rb) in enumerate(chunks):
            q = sqb[rot["sq"] % 2]
            qb = B("sqb%d" % (rot["sq"] % 2))
            rot["sq"] += 1
            act(q[:, 0:TT], ap, AF.Square, rb, [qb])
            mm(pbank[bank][:, 0:TT], onesb[:], q[:, 0:TT], i == 0, i == n - 1, [Bo, qb], PB(bank), inc=True)
        act(lnt[:, 0:TT], pbank[bank][:, 0:TT], AF.Ln, PB(bank), [B("lnt")], scale=1.0 / nparts_feat, bias=EPS)
        act(rs[:, 0:TT], lnt[:, 0:TT], AF.Exp, [B("lnt")], [B("rs")], scale=-0.5)

    def x3(t, TT):
        return t[:, 0:8 * TT].rearrange("p (c t) -> p c t", c=8)

    def run_segment(sg):
        T, TT, P = sg.T, sg.TT, sg.P
        ntile = T // TT
        nblk = max(1, TT // 128)
        bt = min(128, TT)
        nch = TT // 64
        xT3 = x3(xT, TT); hT3 = x3(hT, TT); mix3 = x3(mixT, TT); mo3 = x3(mo, TT)
        G3 = v3(G[:, 0:4 * TT], 4); Gc3 = v3(Gc[:, 0:4 * TT], 4)
        qd3 = v3(qdec[:, 0:4 * TT], 4); ki3 = v3(kinv[:, 0:4 * TT], 4); ke3 = v3(kend[:, 0:4 * TT], 4)
        rgs3 = v3(rgs[:, 0:4 * TT], 4); ktst3 = v3(ktst[:, 0:4 * TT], 4)
        qdT3 = qpad[:, 0:8 * TT].rearrange("p (m h t) -> p m h t", m=2, h=4)
        fw.op("pool", lambda e: e.memset(qpad[:], 0.0), [], [B("qdT")])
        yT3 = yT[:, 0:22 * TT].rearrange("p (c t) -> p c t", c=22)

        for l in range(2):
            sg.init_state(l)

        for it in range(ntile):
            tok0 = it * TT
            def load_x(it_):
                if it_ >= ntile:
                    return
                for bl_ in range(nblk):
                    fw.dma("sp", xin[0:bt, bl_ * 1024:(bl_ + 1) * 1024],
                           sg.x.ap()[it_ * TT + bl_ * bt:it_ * TT + (bl_ + 1) * bt, :], writes=[B("xin")])
            if it == 0:
                load_x(0)
            sg.prefetch = (lambda it_=it: load_x(it_ + 1))
            for bl in range(nblk):
                for c0 in (0, 4):
                    bk = 2 + (c0 // 4)
                    for c in range(c0, c0 + 4):
                        tr(pbank[bk][:, (c - c0) * 128:(c - c0) * 128 + bt], xin[0:bt, bl * 1024 + c * 128:bl * 1024 + (c + 1) * 128],
                           consts[0:bt, C_ID:C_ID + bt], [B("xin"), Bc], PB(bk), inc=(c == c0 + 3))
                    fw.op("act", (lambda c0_, bl_, bk_: (lambda e: e.activation(
                        out=xT3[:, c0_:c0_ + 4, bl_ * bt:(bl_ + 1) * bt],
                        in_=pbank[bk_][:, 0:512].rearrange("p (c t) -> p c t", c=4)[:, :, 0:bt], func=AF.Copy)))(c0, bl, bk),
                        PB(bk), [B("xT")])

            for l in range(2):
                layer_tile(sg, l, it, T, TT, P, nblk, bt, nch, xT3, hT3, mix3, mo3, G3, Gc3, qd3, ki3, ke3, rgs3, qdT3,
                           ktst3, yT3)

            for bl in range(nblk):
                for c0 in (0, 4):
                    bk = 2 + (c0 // 4)
                    for c in range(c0, c0 + 4):
                        tr(pbank[bk][0:bt, (c - c0) * 128:(c - c0 + 1) * 128], xT3[:, c, bl * bt:(bl + 1) * bt],
                           ident, [B("xT"), Bc], PB(bk), inc=(c == c0 + 3))
                    act(kvo[0:bt, bl * 1024 + c0 * 128:bl * 1024 + (c0 + 4) * 128], pbank[bk][0:bt, 0:512], AF.Copy,
                        PB(bk), [B("kvoK" if bl == 0 else "kvoV")])
            for bl in range(nblk):
                fw.dma("pool", sg.y.ap()[tok0 + bl * bt:tok0 + (bl + 1) * bt, :], kvo[0:bt, bl * 1024:(bl + 1) * 1024],
                       reads=[B("kvoK" if bl == 0 else "kvoV")], writes=[])

        for l in range(2):
            fw.dma("pool", sg.gout.ap()[l].rearrange("h k v -> k h v"), v3(S[l][:], 4), reads=[B("S%d" % l)], writes=[])
            tr(pbank[7][0:88, 0:128], hist[l][:], ident, [B("hist%d_%d" % (l, jj)) for jj in range(44)] + [Bc], PB(7))
            cp("dve", vstage[0:88, :], pbank[7][0:88, 0:128], PB(7), [B("vstage")])
            fw.dma("pool", sg.cout.ap()[l].rearrange("r (c p) -> (r c) p", p=128), vstage[0:88, :],
                   reads=[B("vstage")], writes=[])

    def layer_tile(sg, l, it, T, TT, P, nblk, bt, nch, xT3, hT3, mix3, mo3, G3, Gc3, qd3, ki3, ke3, rgs3, qdT3,
                   ktst3, yT3):
        tok0 = it * TT
        Bx, Bh, Bmix, Bmo = B("xT"), B("hT"), B("mixT"), B("mo")
        BS, BSb = B("S%d" % l), B("Sb%d" % l)

        def lnp(kind, c):
            i = kind * 16 + l * 8 + c
            return lnv[:, i:i + 1]

        if l == 1:
            act(lnt[:, 0:TT], pbank[7][:, 0:TT], AF.Ln, PB(7), [B("lnt")], scale=1.0 / D, bias=EPS)
            act(rs[:, 0:TT], lnt[:, 0:TT], AF.Exp, [B("lnt")], [B("rs")], scale=-0.5)
        else:
            rms_stats([(xT3[:, c, :], [Bx]) for c in range(8)], D, TT, None)
        for c in range(8):
            stt(hT3[:, c, :], xT3[:, c, :], lnp(0, c), rs[:, 0:TT], ALU.mult, ALU.mult, [Bx, B("lnv"), B("rs")], [Bh])

        if DBG["stop"] <= 1:
            return
        def proj_fm(slab, sbuf_, col0, M, kparts=128):
            b, hf = acc_next()
            sl3 = slab[:, 0:8 * slab_n[0]].rearrange("p (c n) -> p c n", c=8)
            for kc in range(8):
                mm(pslice(b, hf, M, TT), sl3[:, kc, col0:col0 + M], hT3[:, kc, :], kc == 0, kc == 7,
                   [sbuf_, Bh], [pbuf[b][hf]])
            return b, hf

        slab_n = [16]
        slab3, sb3 = ss.get()
        b, hf = proj_fm(slab3, sb3, 0, 16)
        act(gdT[:, 0:TT], pslice(b, hf, 16, TT), AF.Copy, [pbuf[b][hf]], [B("gdT")])
        for h in range(4):
            bk, hf = 6 + h // 2, h % 2
            mm(pbank[bk][0:64, hf * 256:hf * 256 + TT], wgu[:, l * 256 + h * 64:l * 256 + (h + 1) * 64], gdT[:, 0:TT],
               True, True, [B("wgu"), B("gdT")], [pbuf[bk][hf]])
        for h in range(4):
            bk, hf = 6 + h // 2, h % 2
            act(G3[:, h, :], pbank[bk][0:64, hf * 256:hf * 256 + TT], AF.Exp, [pbuf[bk][hf], B("nbg")], [B("G")],
                scale=-1.0, bias=nbg[:, l * 4 + h:l * 4 + h + 1])
        act(G[:, 0:4 * TT], G[:, 0:4 * TT], AF.Ln, [B("G")], [B("G")], bias=1.0)
        slab1, sb1 = ss.get()
        sl3_1 = slab1[:, 0:8 * 512].rearrange("p (c n) -> p c n", c=8)
        for c in range(nch):
            bk = 6 + c % 2
            for kc in range(8):
                mm(pbank[bk][0:64, 0:512], hT3[:, kc, c * 64:(c + 1) * 64], sl3_1[:, kc, :], kc == 0, kc == 7,
                   [sb1, Bh], PB(bk))
            act(vgtok[:, c * 512:(c + 1) * 512], pbank[bk][0:64, 0:512], AF.Copy, PB(bk), [B("vgtok")])
        fw.op("dve", lambda e: e.tensor_tensor_scan(out=Gc[:, 0:4 * TT], data0=consts[0:64, C_SCAN:C_SCAN + 4 * TT],
                                                    data1=G[:, 0:4 * TT], initial=0.0, op0=ALU.mult, op1=ALU.add),
              [B("G"), Bc], [B("Gc")])
        act(tA[:, 0:4 * TT], Gc[:, 0:4 * TT], AF.Exp, [B("Gc")], [B("tA")], scale=-1.0 / 16.0)
        act(tB[:, 0:4 * TT], Gc[:, 0:4 * TT], AF.Exp, [B("Gc")], [B("tB")], scale=1.0 / 16.0)
        tA3 = v3(tA[:, 0:4 * TT], 4); tB3 = v3(tB[:, 0:4 * TT], 4); tC3 = v3(G[:, 0:4 * TT], 4)
        dec3 = dec[:, 0:4 * nch].rearrange("p (h c) -> p h c", h=4)
        act(dec3, bass.AP(Gc, 63, [[4 * TTM, 64], [TT, 4], [64, nch]]),
            AF.Exp, [B("Gc")], [B("dec")], scale=-1.0 / 16.0)
        for h in range(4):
            tt("dve", tC3[:, h, :].rearrange("p (c t) -> p c t", t=64), tB3[:, h, :].rearrange("p (c t) -> p c t", t=64),
               bass.AP(dec, h * nch, [[4 * NCH, 64], [1, nch], [0, 64]]), ALU.mult, [B("tB"), B("dec"), B("G")], [B("G")])
        if DBG["stop"] <= 1.2:
            return
        slab0, sb0 = ss.get()
        sl3_0 = slab0[:, 0:8 * 512].rearrange("p (c n) -> p c n", c=8)
        for which in range(2):
            for h in range(4):
                b, hf = acc_next()
                for kc in range(8):
                    mm(pslice(b, hf, 64, TT), sl3_0[:, kc, which * 256 + h * 64:which * 256 + (h + 1) * 64],
                       hT3[:, kc, :], kc == 0, kc == 7, [sb0, Bh], [pbuf[b][hf]])
                if which == 0:
                    stt(qd3[:, h, :], pslice(b, hf, 64, TT), 0.125, tA3[:, h, :], ALU.mult, ALU.mult,
                        [pbuf[b][hf], B("tA")], [B("qdec")])
                else:
                    tt("dve", ki3[:, h, :], pslice(b, hf, 64, TT), tB3[:, h, :], ALU.mult, [pbuf[b][hf], B("tB")], [B("kinv")])
                    tt("dve", ke3[:, h, :], pslice(b, hf, 64, TT), tC3[:, h, :], ALU.mult, [pbuf[b][hf], B("G")], [B("kend")])
        pb7b = pbank[7][:].bitcast(BF16)
        for c in range(nch):
            for h in range(4):
                tr(pb7b[0:64, (c * 4 + h) * 64:(c * 4 + h + 1) * 64], ke3[:, h, c * 64:(c + 1) * 64], identb[0:64, 0:64],
                   [B("kend"), Bi], PB(7), inc=(h == 3 and c == nch - 1))
        cp("dve", kendtok[:, 0:nch * 256], pb7b[0:64, 0:nch * 256], PB(7), [B("kendtok")])
        if DBG["stop"] <= 1.4:
            return
        kvo4 = kvo[:].rearrange("p (w b n) -> p w b n", w=2, b=2)
        vst3 = v3(vst[:], 2)
        Bkt = [B(sg.name + "kt%d_%d" % (l, h_)) for h_ in range(4)]
        Bvs = [B(sg.name + "vs%d_%d" % (l, b_)) for b_ in range(2)]
        Bvs_r = Bvs + sg.extra.get(l, [])

        def slab_rg():
            slab2, sb2 = ss.get()
            slab_n[0] = 512
            for h in range(4):
                b, hf = proj_fm(slab2, sb2, h * 128, 128)
                act(rgs3[:, h, :], pslice(b, hf, 128, TT), AF.Silu, [pbuf[b][hf]], [B("rgs")])

        def slab_qd():
            slab4, sb4 = ss.get()
            slab_n[0] = 512
            for h in range(4):
                b, hf = proj_fm(slab4, sb4, h * 128, 128)
                for m in range(2):
                    act(qdT3[m * 64:(m + 1) * 64, m, h, :], pbank[b][m * 64:(m + 1) * 64, hf * 256:hf * 256 + TT], AF.Copy,
                        [pbuf[b][hf]], [B("qdT")])

        def slab_kd():
            slab5, sb5 = ss.get()
            slab_n[0] = 512
            sl3_5 = slab5[:, 0:8 * 512].rearrange("p (c n) -> p c n", c=8)
            for h in range(4):
                b, hf = proj_fm(slab5, sb5, h * 128, 128)
                act(ktst3[:, h, :], pslice(b, hf, 128, TT), AF.Copy, [pbuf[b][hf]], [B("ktst")])
            for bl in range(nblk):
                bk, _hf = acc_next()
                for kc in range(8):
                    mm(pbank[bk][0:bt, 0:512], hT3[:, kc, bl * bt:(bl + 1) * bt], sl3_5[:, kc, :], kc == 0, kc == 7,
                       [sb5, Bh], PB(bk))
                act(kvo4[0:bt, 0, bl, :], pbank[bk][0:bt, 0:512], AF.Copy, PB(bk), [B("kvoK")])
            for h in range(4):
                fw.dma("act", sg.kt.ap()[l, h, :, P + tok0:P + tok0 + TT], ktst3[:, h, :], reads=[B("ktst")], writes=[Bkt[h]])
            for bl in range(nblk):
                fw.dma("act", sg.kout.ap()[l, :, tok0 + bl * bt:tok0 + (bl + 1) * bt, :].rearrange("h t d -> t h d"),
                       kvo4[0:bt, 0, bl, :].rearrange("p (h d) -> p h d", h=4), reads=[B("kvoK")], writes=[])

        def slab_vd():
            slab6, sb6 = ss.get()
            sl3_6 = slab6[:, 0:8 * 512].rearrange("p (c n) -> p c n", c=8)
            for bl in range(nblk):
                bk, _hf = acc_next()
                for kc in range(8):
                    mm(pbank[bk][0:bt, 0:512], hT3[:, kc, bl * bt:(bl + 1) * bt], sl3_6[:, kc, :], kc == 0, kc == 7,
                       [sb6, Bh], PB(bk))
                act(kvo4[0:bt, 1, bl, :], pbank[bk][0:bt, 0:512], AF.Copy, PB(bk), [B("kvoV")])
                cp("dve", vst3[0:bt, bl, :], kvo4[0:bt, 1, bl, :], [B("kvoV")], [B("vst")])
            for bl in range(nblk):
                r0 = P + tok0 + bl * bt
                fw.dma("act", sg.vsc.ap()[l, :, r0:r0 + bt, :].rearrange("h t d -> t h d"),
                       vst3[0:bt, bl, :].rearrange("p (h d) -> p h d", h=4), reads=[B("vst")], writes=[Bvs[bl % 2]])
                fw.dma("act", sg.vout.ap()[l, :, tok0 + bl * bt:tok0 + (bl + 1) * bt, :].rearrange("h t d -> t h d"),
                       kvo4[0:bt, 1, bl, :].rearrange("p (h d) -> p h d", h=4), reads=[B("kvoV")], writes=[])

        late_slabs = [slab_rg, slab_qd, slab_kd, slab_vd]
        if DBG["stop"] <= 2:
            while late_slabs:
                late_slabs.pop(0)()
            return
        S3 = v3(S[l][:], 4); Sb3 = v3(Sb[l][:], 4)
        at4 = ATsb[:, 0:nch * 256].rearrange("p (c h t) -> p c h t", c=nch, h=4)
        kt4 = kendtok[:, 0:nch * 256].rearrange("p (c h k) -> p c h k", c=nch, h=4)
        for c in range(nch):
            cs = slice(c * 64, (c + 1) * 64)
            for h in range(4):
                mm(pbank[6][0:64, h * 64:(h + 1) * 64], ki3[:, h, cs], qd3[:, h, cs], True, True,
                   [B("kinv"), B("qdec")], [pbuf[6][0]], inc=(h == 3))
            tt("dve", at4[:, c, :, :], pbank[6][0:64, 0:256].rearrange("p (h t) -> p h t", h=4),
               bass.AP(consts, C_TRI, [[NCW, 64], [0, 4], [1, 64]]), ALU.mult, [pbuf[6][0], Bc], [B("ATsb")])
            if late_slabs:
                late_slabs.pop(0)()
            for h in range(4):
                bk, hf = 2 + h // 2, h % 2
                o_ap = pbank[bk][:, hf * 256 + c * 64:hf * 256 + (c + 1) * 64]
                mm(o_ap, vgtok[:, c * 512 + h * 128:c * 512 + (h + 1) * 128], at4[:, c, h, :], True, False,
                   [B("vgtok"), B("ATsb")], [pbuf[bk][hf]])
                mm(o_ap, Sb3[:, h, :], qd3[:, h, cs], False, True, [BSb, B("qdec")], [pbuf[bk][hf]])
            for h in range(4):
                mm(pbank[7][0:64, h * 128:(h + 1) * 128], kt4[:, c, h, :], vgtok[:, c * 512 + h * 128:c * 512 + (h + 1) * 128],
                   True, True, [B("kendtok"), B("vgtok")], PB(7), inc=(h == 3))
            tt("dve", S3, S3, bass.AP(dec, c, [[4 * NCH, 64], [nch, 4], [0, 128]]), ALU.mult, [BS, B("dec")], [BS])
            tt("dve", S[l][:], S[l][:], pbank[7][0:64, 0:512], ALU.add, [BS] + PB(7), [BS])
            act(Sb[l][:], S[l][:], AF.Copy, [BS], [BSb])
        while late_slabs:
            late_slabs.pop(0)()
        og3 = v3(og4[:, 0:4 * TT], 4)
        for hp in range(2):
            src = pbank[2 + hp][:, 0:512].rearrange("p (h t) -> p h t", h=2)[:, :, 0:TT]
            if hp == 0:
                act(og3[:, 0:2, :], src, AF.Copy, PB(2), [B("mo")])
            else:
                cp("dve", og3[:, 2:4, :], src, PB(3), [B("mo")])

        def gla_epilogue():
            act(sq4[:, 0:4 * TT], og4[:, 0:4 * TT], AF.Square, [B("mo")], [B("sq4")])
            sq3 = v3(sq4[:, 0:4 * TT], 4)
            banks = []
            for hp in range(2):
                bk = (3, 5)[hp]
                banks.append(bk)
                for hh in range(2):
                    mm(pbank[bk][:, hh * 256:hh * 256 + TT], onesb[:], sq3[:, hp * 2 + hh, :], True, True, [Bo, B("sq4")], PB(bk),
                       inc=(hh == 1))
            rs3 = v3(rs4[:, 0:4 * TT], 4)
            for hp in range(2):
                bk = banks[hp]
                act(rs3[:, hp * 2:hp * 2 + 2, :], pbank[bk][:, 0:512].rearrange("p (h t) -> p h t", h=2)[:, :, 0:TT], AF.Ln,
                    PB(bk), [B("xin")], scale=1.0 / 128, bias=EPS)
            act(rs4[:, 0:4 * TT], rs4[:, 0:4 * TT], AF.Exp, [B("xin")], [B("xin")], scale=-0.5)
            stt(og4[:, 0:4 * TT], og4[:, 0:4 * TT], glan[:, l:l + 1], rs4[:, 0:4 * TT], ALU.mult, ALU.mult,
                [B("mo"), B("glan"), B("xin")], [B("mo")])
            tt("dve", mixT[:, 0:4 * TT], og4[:, 0:4 * TT], rgs[:, 0:4 * TT], ALU.mult, [B("mo"), B("rgs")], [Bmix])

        if DBG["stop"] <= 3:
            gla_epilogue()

        if DBG["stop"] <= 3:
            return
        q0 = P + tok0
        nkeys = P + tok0 + TT
        nkb = (nkeys + 127) // 128
        ngrp = (nkb + 7) // 8
        deferred = [gla_epilogue]
        for h in range(4):
            hs = strip[:, h * 384:(h + 1) * 384]
            first = True
            pend = []
            nblk_done = 0
            for g in range(ngrp):
                gi = rot.setdefault("grp", 0)
                rot["grp"] += 1
                ktg, vg_ = KTg[gi % NG], Vg[gi % NG]
                Bkg, Bvg = B("KTg%d" % (gi % NG)), B("Vg%d" % (gi % NG))
                k0g = g * 1024
                nkg = min(1024, nkeys - k0g)
                fw.dma("sp", ktg[:, 0:nkg], sg.kt.ap()[l, h, :, k0g:k0g + nkg], reads=[Bkt[h]], writes=[Bkg])
                nfull = nkg // 128
                if nfull:
                    fw.dma("sp", vg_[:, 0:nfull * 128].rearrange("p (b d) -> p b d", d=128),
                           sg.vsc.ap()[l, h, k0g:k0g + nfull * 128, :].rearrange("(b p) d -> p b d", p=128),
                           reads=Bvs_r, writes=[Bvg])
                if nkg % 128:
                    rem = nkg % 128
                    fw.dma("sp", vg_[0:rem, nfull * 128:(nfull + 1) * 128],
                           sg.vsc.ap()[l, h, k0g + nfull * 128:k0g + nkg, :], reads=Bvs_r, writes=[Bvg])
                nb_g = (nkg + 127) // 128
                for kb in range(nb_g):
                    k0 = k0g + kb * 128
                    nk = min(128, nkeys - k0)
                    j = (k0 - q0) // 128
                    c0 = max(0, 128 * j)
                    ncol = TT - c0
                    gb = rot.setdefault("pt", 0)
                    rot["pt"] += 1
                    last = (g == ngrp - 1 and kb == nb_g - 1)
                    sbuf_i = gb % 3
                    ptb = PT[sbuf_i][:, 0:2 * TT].rearrange("p (m t) -> p m t", m=2)
                    ptr = PTr[sbuf_i][:, 0:2 * TT].rearrange("p (m t) -> p m t", m=2)
                    Bpt, Bptr = B("PT%d" % sbuf_i), B("PTr%d" % sbuf_i)
                    near = j >= -1
                    bk0 = SBK[sbuf_i]
                    s3 = pbank[bk0][0:nk, 0:2 * TT].rearrange("p (m t) -> p m t", m=2)[:, :, c0:TT]
                    if c0 == 0:
                        mm(pbank[bk0][0:nk, 0:2 * TT], ktg[:, kb * 128:kb * 128 + nk], qdT3[:, :, h, :], True, True,
                           [Bkg, B("qdT")], PB(bk0))
                    else:
                        for m in range(2):
                            mm(pbank[bk0][0:nk, m * TT + c0:(m + 1) * TT], ktg[:, kb * 128:kb * 128 + nk], qdT3[:, m, h, c0:TT],
                               True, True, [Bkg, B("qdT")], PB(bk0), inc=(m == 1))
                    if len(pend) >= 2:
                        pend.pop(0)()
                    if near:
                        act(ptr[0:nk, :, c0:TT], s3, AF.Exp, PB(bk0), [Bptr], scale=0.125)
                        tt("dve", ptb[0:nk, :, c0:TT], ptr[0:nk, :, c0:TT],
                           bass.AP(strip, h * 384 + c0 - 128 * j, [[4 * 384, nk], [0, 2], [1, TT - c0]]), ALU.mult,
                           [Bptr, B("strip")], [Bpt])
                    else:
                        act(ptb[0:nk, :, c0:TT], s3, AF.Exp, PB(bk0), [Bpt], scale=0.125)

                    def pv(ptb=ptb, Bpt=Bpt, nk=nk, c0=c0, kb=kb, vg_=vg_, Bvg=Bvg, first=first, last=last):
                        o3_ = pbank[2][:, 0:2 * TT].rearrange("p (m t) -> p m t", m=2)[:, :, c0:TT]
                        z3_ = pbank[6][:, 0:2 * TT].rearrange("p (m t) -> p m t", m=2)[:, :, c0:TT]
                        if c0 == 0:
                            mm(pbank[2][:, 0:2 * TT], vg_[0:nk, kb * 128:(kb + 1) * 128], ptb[0:nk, :, :], first, last,
                               [Bvg, Bpt], PB(2), inc=False)
                            mm(pbank[6][:, 0:2 * TT], onesb[0:nk, :], ptb[0:nk, :, :], first, last, [Bo, Bpt], PB(6), inc=True)
                        else:
                            for m in range(2):
                                mm(pbank[2][:, m * TT + c0:(m + 1) * TT], vg_[0:nk, kb * 128:(kb + 1) * 128], ptb[0:nk, m, c0:TT],
                                   first, last and m == 1, [Bvg, Bpt], PB(2), inc=False)
                            for m in range(2):
                                mm(pbank[6][:, m * TT + c0:(m + 1) * TT], onesb[0:nk, :], ptb[0:nk, m, c0:TT],
                                   first, last and m == 1, [Bo, Bpt], PB(6), inc=(m == 1))
                    pend.append(pv)
                    nblk_done += 1
                    if nblk_done == 2 and deferred:
                        deferred.pop(0)()
                    first = False
            while pend:
                pend.pop(0)()
            def att_part1():
                act(r01[:, 0:2 * TT], pbank[6][:, 0:2 * TT], AF.Ln, PB(6), [B("r01")])
                act(r01[:, 0:2 * TT], r01[:, 0:2 * TT], AF.Exp, [B("r01")], [B("r01")], scale=-1.0)
                tt("dve", t01[:, 0:2 * TT], pbank[2][:, 0:2 * TT], r01[:, 0:2 * TT], ALU.mult, PB(2) + [B("r01")], [B("t01")])
                stt(odr[:, 0:TT], t01[:, TT:2 * TT], nlam[:, l:l + 1], t01[:, 0:TT], ALU.mult, ALU.add,
                    [B("t01"), B("nlam")], [B("odr")])
            defer_p1 = (h < 3 and nkb >= 6)
            if not defer_p1:
                att_part1()

            def att_epilogue(h=h, defer_p1=defer_p1, att_part1=att_part1):
                if defer_p1:
                    att_part1()
                rms_stats([(odr[:, 0:TT], [B("odr")])], 128, TT, None, bank=3)
                stt(mix3[:, 4 + h, :], odr[:, 0:TT], difn[:, l:l + 1], rs[:, 0:TT], ALU.mult, ALU.mult,
                    [B("odr"), B("difn"), B("rs")], [Bmix])
            deferred.append(att_epilogue)
        while deferred:
            deferred.pop(0)()

        if DBG["stop"] <= 4:
            return
        def sq_chunk(ap, rb):
            q = sqb[rot["sq"] % 2]
            qb = B("sqb%d" % (rot["sq"] % 2))
            rot["sq"] += 1
            act(q[:, 0:TT], ap, AF.Square, rb, [qb])
            return q, qb

        def stat_mm(bank, q, qb, i, n):
            mm(pbank[bank][:, 0:TT], onesb[:], q[:, 0:TT], i == 0, i == n - 1, [Bo, qb], PB(bank), inc=True)

        def dense_fm(nslab, mper, kchunks, src3, Bsrc, kcols):
            pend = None
            for s in range(nslab):
                slab, sbf = ss.get()
                sl3 = slab[:, 0:kchunks * kcols].rearrange("p (c n) -> p c n", c=kchunks)
                for mi in range(mper):
                    mch = s * mper + mi
                    b, hf = acc_next()
                    for kc in range(kchunks):
                        mm(pslice(b, hf, 128, TT), sl3[:, kc, mi * 128:(mi + 1) * 128], src3[:, kc, :], kc == 0,
                           kc == kchunks - 1, [sbf, Bsrc], [pbuf[b][hf]])
                    if pend is not None:
                        stat_mm(6, pend[0], pend[1], pend[2], 8)
                    Bm_ = Bmo if mch < 5 else B("mo2")
                    act(mo3[:, mch, :], pslice(b, hf, 128, TT), AF.Copy, [pbuf[b][hf]], [Bm_])
                    q, qb = sq_chunk(mo3[:, mch, :], [Bm_])
                    pend = (q, qb, mch)
            stat_mm(6, pend[0], pend[1], pend[2], 8)

        def post_norm_residual(kind, fuse_next):
            act(lnt[:, 0:TT], pbank[6][:, 0:TT], AF.Ln, PB(6), [B("lnt")], scale=1.0 / D, bias=EPS)
            act(rs[:, 0:TT], lnt[:, 0:TT], AF.Exp, [B("lnt")], [B("rs")], scale=-0.5)
            rsb = bass.AP(rs, 0, [[TTM, 128], [0, 5], [1, TT]])
            tt("dve", mo3[:, 0:5, :], mo3[:, 0:5, :], rsb, ALU.mult, [Bmo, B("rs")], [Bmo])
            rsb2 = bass.AP(rs, 0, [[TTM, 128], [0, 3], [1, TT]])
            tt("pool", mo3[:, 5:8, :], mo3[:, 5:8, :], rsb2, ALU.mult, [B("mo2"), B("rs")], [B("mo2")])
            pend = None
            for c in range(8):
                stt(xT3[:, c, :], mo3[:, c, :], lnp(kind, c), xT3[:, c, :], ALU.mult, ALU.add,
                    [Bmo if c < 5 else B("mo2"), B("lnv"), Bx], [Bx])
                if fuse_next:
                    if pend is not None:
                        stat_mm(7, pend[0], pend[1], pend[2], 8)
                    q, qb = sq_chunk(xT3[:, c, :], [Bx])
                    pend = (q, qb, c)
            if fuse_next:
                stat_mm(7, pend[0], pend[1], pend[2], 8)

        def pre_norm(kind, have_stats):
            if have_stats:
                act(lnt[:, 0:TT], pbank[7][:, 0:TT], AF.Ln, PB(7), [B("lnt")], scale=1.0 / D, bias=EPS)
                act(rs[:, 0:TT], lnt[:, 0:TT], AF.Exp, [B("lnt")], [B("rs")], scale=-0.5)
            else:
                rms_stats([(xT3[:, c, :], [Bx]) for c in range(8)], D, TT, None)
            for c in range(8):
                stt(hT3[:, c, :], xT3[:, c, :], lnp(kind, c), rs[:, 0:TT], ALU.mult, ALU.mult, [Bx, B("lnv"), B("rs")], [Bh])

        dense_fm(2, 4, 8, mix3, Bmix, 512)
        post_norm_residual(1, True)

        if DBG["stop"] <= 5:
            return
        pre_norm(2, True)
        if l == 1:
            sg.prefetch()
        hist3 = v3(hist[l][:], 2)

        def Bh_(jj):
            return B("hist%d_%d" % (l, jj))

        def cw(tp, j):
            i = (l * 3 + tp) * 44 + j
            return convw[:, i:i + 1]

        ffn_slabs = {}

        def stage_a(p):
            s_ = p // 2
            pi = p % 2
            if pi == 0:
                ffn_slabs[s_] = ss.get()
            slab, sbf = ffn_slabs[s_]
            sl3 = slab[:, 0:4096].rearrange("p (c n) -> p c n", c=16)
            e = p % NE
            for gu, (ex, nm, jj) in enumerate(((extg[e], "extg", p), (extu[e], "extu", 22 + p))):
                b, hf = acc_next()
                for kc in range(8):
                    mm(pslice(b, hf, 128, TT), sl3[:, gu * 8 + kc, pi * 128:(pi + 1) * 128], hT3[:, kc, :], kc == 0, kc == 7,
                       [sbf, Bh], [pbuf[b][hf]])
                cp("pool", ex[:, 0:2], hist3[:, :, jj], [Bh_(jj)], [B("%sh%d" % (nm, e))])
                act(ex[:, 2:2 + TT], pslice(b, hf, 128, TT), AF.Copy, [pbuf[b][hf]], [B("%sm%d" % (nm, e))])

        def stage_b1(p):
            e, c_ = p % NE, p % NC_
            for (ex, nm, jj, ct, cn) in ((extg[e], "extg", p, cg[c_], "cg"), (extu[e], "extu", 22 + p, cu[c_], "cu")):
                Bexh, Bexm, Bct = B("%sh%d" % (nm, e)), B("%sm%d" % (nm, e)), B("%s%d" % (cn, c_))
                act(ct[:, 0:TT], ex[:, 0:TT], AF.Identity, [Bexh, Bexm, B("convw"), B("convb")], [Bct], scale=cw(0, jj),
                    bias=convb[:, l * 44 + jj:l * 44 + jj + 1])

        def stage_b2(p):
            e, c_ = p % NE, p % NC_
            for (ex, nm, jj, ct, cn) in ((extg[e], "extg", p, cg[c_], "cg"), (extu[e], "extu", 22 + p, cu[c_], "cu")):
                Bexh, Bexm, Bct = B("%sh%d" % (nm, e)), B("%sm%d" % (nm, e)), B("%s%d" % (cn, c_))
                stt(ct[:, 0:TT], ex[:, 1:1 + TT], cw(1, jj), ct[:, 0:TT], ALU.mult, ALU.add, [Bexh, Bexm, Bct, B("convw")], [Bct])
                stt(ct[:, 0:TT], ex[:, 2:2 + TT], cw(2, jj), ct[:, 0:TT], ALU.mult, ALU.add, [Bexm, Bct, B("convw")], [Bct])
                cp("pool", hist3[:, :, jj], ex[:, TT:TT + 2], [Bexm], [Bh_(jj)])

        def stage_c1(p):
            c_, k2 = p % NC_, p % 2
            act(f1[k2][:, 0:TT], cg[c_][:, 0:TT], AF.Square, [B("cg%d" % c_)], [B("f1%d" % k2)])
            ts("pool", f1[k2][:, 0:TT], f1[k2][:, 0:TT], 0.044715, 1.0, ALU.mult, ALU.add, [B("f1%d" % k2)], [B("f1%d" % k2)])
            tt("pool", f1[k2][:, 0:TT], f1[k2][:, 0:TT], cg[c_][:, 0:TT], ALU.mult, [B("f1%d" % k2), B("cg%d" % c_)], [B("f1%d" % k2)])

        def stage_c2(p):
            c_, k2 = p % NC_, p % 2
            act(f2[k2][:, 0:TT], f1[k2][:, 0:TT], AF.Sigmoid, [B("f1%d" % k2)], [B("f2%d" % k2)], scale=1.5957691216057308)
            tt("pool" if p % 2 == 0 else "dve", f2[k2][:, 0:TT], f2[k2][:, 0:TT], cu[c_][:, 0:TT], ALU.mult,
               [B("f2%d" % k2), B("cu%d" % c_)], [B("f2%d" % k2)])

        def stage_c3(p):
            c_, k2 = p % NC_, p % 2
            tt("dve", yT3[:, p, :], f2[k2][:, 0:TT], cg[c_][:, 0:TT], ALU.mult, [B("f2%d" % k2), B("cg%d" % c_)], [B("yT")])

        stages = (stage_a, stage_b1, stage_b2, stage_c1, stage_c2, stage_c3)
        for it_ in range(22 + len(stages) - 1):
            for si_, fn_ in enumerate(stages):
                p_ = it_ - si_
                if 0 <= p_ < 22:
                    fn_(p_)
        dense_fm(4, 2, 22, yT3, B("yT"), 256)
        post_norm_residual(3, l == 0)

    def make_seg(name, T, TT, P, x, y, kout, vout, gout, cout, kt, vsc):
        sg = Seg()
        sg.name, sg.T, sg.TT, sg.P = name, T, TT, P
        sg.x, sg.y, sg.kout, sg.vout, sg.gout, sg.cout, sg.kt, sg.vsc = x, y, kout, vout, gout, cout, kt, vsc
        sg.extra = {}
        return sg

    segp = make_seg("p", TP, TTP, 0, d_xp, o_yp, o_kp, o_vp, o_gp, o_cp, ktp, vsp)
    segs = make_seg("s", TS, TS, PS, d_xs, o_ys, o_ks, o_vs, o_gs, o_cs, kts, vss)

    def init_prompt(l):
        fw.op("dve", lambda e: e.memset(S[l][:], 0.0), [], [B("S%d" % l)])
        fw.op("dve", lambda e: e.memset(Sb[l][:], 0.0), [], [B("Sb%d" % l)])
        fw.op("dve", lambda e: e.memset(hist[l][:], 0.0), [], [B("hist%d_%d" % (l, jj)) for jj in range(44)])

    def init_sample(l):
        fw.dma("sp", v3(S[l][:], 4), d_sg.ap()[l].rearrange("h k v -> k h v"), writes=[B("S%d" % l)])
        cp("dve", Sb[l][:], S[l][:], [B("S%d" % l)], [B("Sb%d" % l)])
        load_cols(d_sc.ap()[l].rearrange("r (c p) -> (r c) p", p=128), 88, 128, hist[l][:], [B("hist%d_%d" % (l, jj)) for jj in range(44)])
        Bkt = [B("skt%d_%d" % (l, h_)) for h_ in range(4)]
        Bvs = [B("svs%d_%d" % (l, b_)) for b_ in range(2)]
        nb = PS // 128
        for h in range(4):
            fw.dma("sp", ckst[:, 0:nb * 128].rearrange("p (b d) -> p b d", d=128),
                   d_ck.ap()[l, h].rearrange("(b p) d -> p b d", p=128), writes=[B("kvoK")])
            for b0 in range(0, nb, 4):
                for b_ in range(b0, min(nb, b0 + 4)):
                    tr(pbank[7][:, (b_ - b0) * 128:(b_ - b0 + 1) * 128], ckst[:, b_ * 128:(b_ + 1) * 128], ident,
                       [B("kvoK"), Bc], PB(7), inc=(b_ == min(nb, b0 + 4) - 1))
                nn = min(nb, b0 + 4) - b0
                act(ktc[:, b0 * 128:(b0 + nn) * 128], pbank[7][:, 0:nn * 128], AF.Copy, PB(7), [B("vst")])
            fw.dma("pool", kts.ap()[l, h, :, 0:PS], ktc[:, 0:PS], reads=[B("vst")], writes=[Bkt[h]])
            for r0_ in range(0, PS, 256):
                bx_ = B("scv%d_%d_%d" % (l, h, r0_))
                segs.extra.setdefault(l, []).append(bx_)
                fw.dma("pool", vss.ap()[l, h, r0_:r0_ + 256, :], d_cv.ap()[l, h, r0_:r0_ + 256, :], writes=[bx_])

    segp.init_state = init_prompt
    segs.init_state = init_sample

    for sg_ in (segs, segp):
        for it_ in range(sg_.T // sg_.TT):
            for l_ in range(2):
                plan_tile_layer(l_)
    if "s" in DBG["segs"]:
        run_segment(segs)
    if "p" in DBG["segs"]:
        run_segment(segp)

    fw.finish([B("outs")])
    fw.emit()
    build_nc.ninst = fw.ninst


_NC_CACHE = {}


def kernel(x_prompt, x_sample, cache_k, cache_v, state_gla, state_conv, t5_table,
           w_in, w_gate_up, b_gate_up, gla_norm, lam_params, diff_norm, w_out,
           ln_mix_pre, ln_mix_post, ln_ffn_pre, ln_ffn_post,
           w_ffn_up, conv_w, conv_b, w_ffn_down):
    f = lambda a: np.ascontiguousarray(np.asarray(a, dtype=np.float32))
    x_prompt, x_sample = f(x_prompt), f(x_sample)
    BATCH, TP = x_prompt.shape[0], x_prompt.shape[1]
    NS, TS = x_sample.shape[0], x_sample.shape[1]
    PS = cache_k.shape[3]
    key = (TP, TS, PS)
    if key not in _NC_CACHE:
        _NC_CACHE[key] = build_nc(TP=TP, TS=TS, PS=PS)
    nc = _NC_CACHE[key]
    shared = {
        "t5": f(t5_table), "w_in": f(w_in), "w_gate_up": f(w_gate_up), "b_gate_up": f(b_gate_up),
        "gla_norm": f(gla_norm), "lam_params": f(lam_params), "diff_norm": f(diff_norm), "w_out": f(w_out),
        "ln_mix_pre": f(ln_mix_pre), "ln_mix_post": f(ln_mix_post), "ln_ffn_pre": f(ln_ffn_pre),
        "ln_ffn_post": f(ln_ffn_post), "w_ffn_up": f(w_ffn_up), "conv_w": f(conv_w), "conv_b": f(conv_b),
        "w_ffn_down": f(w_ffn_down), "consts": _consts(),
    }
    cache_k, cache_v, state_gla, state_conv = f(cache_k), f(cache_v), f(state_gla), f(state_conv)
    in_maps = []
    for c in range(NCORES):
        bp = c % BATCH
        bs = c % NS
        m = dict(shared)
        m["xp"] = x_prompt[bp]
        m["xs"] = x_sample[bs]
        m["ck"] = np.ascontiguousarray(cache_k[:, bs]); m["cv"] = np.ascontiguousarray(cache_v[:, bs])
        m["sg"] = np.ascontiguousarray(state_gla[:, bs]); m["sc"] = np.ascontiguousarray(state_conv[:, bs])
        in_maps.append(m)
    res = run_bass_kernel_spmd(nc, in_maps, core_ids=list(range(NCORES)))
    R = res.results
    y_prompt = np.stack([R[b]["yp"] for b in range(BATCH)])
    y_sample = np.stack([R[b]["ys"] for b in range(NS)])
    stk = lambda name, n: np.stack([R[b][name] for b in range(n)], axis=1)
    return (y_prompt, y_sample, stk("kp", BATCH), stk("vp", BATCH), stk("gp", BATCH), stk("cp", BATCH),
            stk("ks", NS), stk("vs", NS), stk("gs", NS), stk("cs", NS))
```

```python
import math
from contextlib import ExitStack

import numpy as np
import concourse.bass as bass
import concourse.mybir as mybir
from concourse.bass_utils import run_bass_kernel_spmd

F32 = mybir.dt.float32
BF16 = mybir.dt.bfloat16
AF = mybir.ActivationFunctionType
ALU = mybir.AluOpType

D = 1024
NCORES = 8
DPROJ = 3088
F2 = 5632
DFF = 2816
EPS = 1e-6


class Buf:
    __slots__ = ("name", "w", "r", "excl")

    def __init__(self, name, excl=False):
        self.name = name
        self.w = None
        self.r = {}
        self.excl = excl


class Eng:
    def __init__(self, name, be, sem):
        self.name = name
        self.be = be
        self.sem = sem
        self.count = 0
        self.pending = False
        self.seen = {}
        self.ops = []
        self.dma_n = 0


class FW:
    def __init__(self, nc, stack, n_dma_sems=12):
        self.nc = nc
        self.sems = {}
        self.E = {}
        for name, be in (("pe", nc.tensor), ("act", nc.scalar), ("dve", nc.vector),
                         ("pool", nc.gpsimd), ("sp", nc.sync)):
            self.sems["s_" + name] = stack.enter_context(nc.semaphore("s_" + name))
            self.E[name] = Eng(name, be, "s_" + name)
        self.dma_sems = {}
        for q in ("sp", "pool", "act"):
            lst = []
            for i in range(n_dma_sems):
                k = "d_%s%d" % (q, i)
                self.sems[k] = stack.enter_context(nc.semaphore(k))
                lst.append(k)
            self.dma_sems[q] = lst
        self.nd = n_dma_sems
        self.semowner = {e.sem: e for e in self.E.values()}
        self.ninst = 0

    def _deps(self, eng, reads, writes):
        need = {}
        for b in reads:
            if b.w is not None:
                k, v = b.w
                if need.get(k, 0) < v:
                    need[k] = v
            if b.excl:
                for k, v in b.r.items():
                    if k != eng.sem and need.get(k, 0) < v:
                        need[k] = v
        for b in writes:
            if b.w is not None:
                k, v = b.w
                if need.get(k, 0) < v:
                    need[k] = v
            for k, v in b.r.items():
                if need.get(k, 0) < v:
                    need[k] = v
        out = []
        for k, v in need.items():
            if k == eng.sem and eng.name == "pe":
                continue
            if eng.seen.get(k, 0) >= v:
                continue
            ow = self.semowner.get(k)
            if ow is not None:
                assert v <= ow.count, ("pending tick", eng.name, k, v, ow.count)
            eng.seen[k] = v
            out.append((k, v))
        return out

    def _mark(self, tick, reads, writes):
        k, v = tick
        for b in reads:
            if b.r.get(k, 0) < v:
                b.r[k] = v
        for b in writes:
            b.w = tick
            b.r = {}

    def op(self, en, fn, reads=(), writes=(), inc=True):
        eng = self.E[en]
        waits = self._deps(eng, reads, writes)
        if inc:
            eng.count += 1
            eng.pending = False
            tick = (eng.sem, eng.count)
        else:
            eng.pending = True
            tick = (eng.sem, eng.count + 1)
        eng.ops.append((waits, fn, eng.sem if inc else None, 1))
        self._mark(tick, reads, writes)
        self.ninst += 1

    def dma(self, q, out, in_, reads=(), writes=()):
        eng = self.E[q]
        n = eng.dma_n
        eng.dma_n += 1
        sk = self.dma_sems[q][n % self.nd]
        gen = n // self.nd
        waits = self._deps(eng, reads, writes)
        if gen > 0 and eng.seen.get(sk, 0) < 16 * gen:
            eng.seen[sk] = 16 * gen
            waits.append((sk, 16 * gen))
        tick = (sk, 16 * (gen + 1))
        eng.ops.append((waits, (lambda be: be.dma_start(out=out, in_=in_)), sk, 16))
        self._mark(tick, reads, writes)
        self.ninst += 1

    def finish(self, final_bufs):
        sp = self.E["sp"]
        waits = self._deps(sp, list(final_bufs), [])
        for q, lst in self.dma_sems.items():
            n = self.E[q].dma_n
            for i, sk in enumerate(lst):
                cnt = (n - i + self.nd - 1) // self.nd if n > i else 0
                if cnt > 0 and sp.seen.get(sk, 0) < 16 * cnt:
                    sp.seen[sk] = 16 * cnt
                    waits.append((sk, 16 * cnt))
        sp.ops.append((waits, None, None, 0))
        for e in self.E.values():
            assert not e.pending, e.name

    def emit(self):
        nc = self.nc
        sems = self.sems
        with nc.Block() as block:
            def run(eng):
                def body(be):
                    for waits, fn, sk, inc in eng.ops:
                        for k, v in waits:
                            be.wait_ge(sems[k], v)
                        if fn is None:
                            continue
                        ins = fn(be)
                        if sk is not None:
                            ins.then_inc(sems[sk], inc)
                return body
            block.tensor(run(self.E["pe"]))
            block.scalar(run(self.E["act"]))
            block.vector(run(self.E["dve"]))
            block.gpsimd(run(self.E["pool"]))
            block.sync(run(self.E["sp"]))


def _bucket_of(rel):
    rel = np.asarray(rel)
    n = np.abs(rel)
    nf = np.maximum(n, 1).astype(np.float32)
    large = 8 + (np.log(nf / np.float32(8)) / np.float32(math.log(16.0)) * np.float32(8)).astype(np.int32)
    large = np.minimum(large, 15)
    return (rel > 0).astype(np.int32) * 16 + np.where(n < 8, n, large)


C_ID, C_TRI, C_SCAN, C_OHR, C_ONES, C_J, NCW = 0, 128, 192, 1216, 1728, 1856, 1984


def _consts():
    c = np.zeros((128, NCW), np.float32)
    c[:, C_ID:C_ID + 128] = np.eye(128, dtype=np.float32)
    j = np.arange(64)[:, None]
    i = np.arange(64)[None, :]
    c[0:64, C_TRI:C_TRI + 64] = (j <= i).astype(np.float32)
    m = np.ones(1024, np.float32)
    m[::64] = 0.0
    c[0:64, C_SCAN:C_SCAN + 1024] = m[None, :]
    idx = np.arange(512)
    b = _bucket_of(127 - idx)
    oh = np.zeros((32, 512), np.float32)
    oh[b, idx] = 1.0
    c[0:32, C_OHR:C_OHR + 512] = oh
    c[:, C_ONES:C_ONES + 128] = 1.0
    c[:, C_J:C_J + 128] = np.eye(128, dtype=np.float32)[::-1]
    return c


def lam_init(l):
    return 0.8 - 0.6 * math.exp(-0.3 * l)


class Seg:
    pass


DBG = {"segs": "sp", "stop": 99}


def build_nc(TP=4096, TS=64, PS=1024, TTP=256):
    nc = bass.Bass("TRN2", target_bir_lowering=False)
    st = ExitStack()
    with st:
        _build(nc, st, TP, TS, PS, TTP)
    return nc


def _build(nc, st, TP, TS, PS, TTP):
    fw = FW(nc, st)
    TTM = max(TTP, TS)

    def din(n, s):
        return nc.dram_tensor(n, list(s), F32, kind="ExternalInput")

    def dout(n, s):
        return nc.dram_tensor(n, list(s), F32, kind="ExternalOutput")

    d_xp = din("xp", (TP, D)); d_xs = din("xs", (TS, D))
    d_ck = din("ck", (2, 4, PS, 128)); d_cv = din("cv", (2, 4, PS, 128))
    d_sg = din("sg", (2, 4, 64, 128)); d_sc = din("sc", (2, 2, F2))
    d_t5 = din("t5", (32, 4))
    d_win = din("w_in", (2, D, DPROJ)); d_wgu = din("w_gate_up", (2, 16, 256)); d_bgu = din("b_gate_up", (2, 256))
    d_gn = din("gla_norm", (2, 128)); d_lp = din("lam_params", (2, 4, 64)); d_dn = din("diff_norm", (2, 128))
    d_wout = din("w_out", (2, D, D))
    d_ln = [din(n, (2, D)) for n in ("ln_mix_pre", "ln_mix_post", "ln_ffn_pre", "ln_ffn_post")]
    d_wup = din("w_ffn_up", (2, D, F2)); d_cw = din("conv_w", (2, 3, F2)); d_cb = din("conv_b", (2, F2))
    d_wdn = din("w_ffn_down", (2, DFF, D))
    d_consts = din("consts", (128, NCW))

    o_yp = dout("yp", (TP, D)); o_ys = dout("ys", (TS, D))
    o_kp = dout("kp", (2, 4, TP, 128)); o_vp = dout("vp", (2, 4, TP, 128))
    o_gp = dout("gp", (2, 4, 64, 128)); o_cp = dout("cp", (2, 2, F2))
    o_ks = dout("ks", (2, 4, TS, 128)); o_vs = dout("vs", (2, 4, TS, 128))
    o_gs = dout("gs", (2, 4, 64, 128)); o_cs = dout("cs", (2, 2, F2))

    def dscr(n, s, dt):
        return nc.dram_tensor(n, list(s), dt)

    wb_in = dscr("wb_in", (2, D, DPROJ), BF16); wb_out = dscr("wb_out", (2, D, D), BF16)
    wb_up = dscr("wb_up", (2, D, F2), BF16); wb_dn = dscr("wb_dn", (2, DFF, D), BF16)
    ktp = dscr("ktp", (2, 4, 128, TP), BF16); vsp = dscr("vsp", (2, 4, TP, 128), BF16)
    kts = dscr("kts", (2, 4, 128, PS + TS), BF16); vss = dscr("vss", (2, 4, PS + TS, 128), BF16)
    f3d = dscr("f3d", (4, 512), F32)

    def sb(n, s, dt=F32):
        return st.enter_context(nc.sbuf_tensor("sb_" + n, list(s), dt))

    bufs = {}

    def B(name):
        if name not in bufs:
            bufs[name] = Buf(name)
        return bufs[name]

    consts = sb("consts", (128, NCW))
    identb = sb("identb", (128, 128), BF16); onesb = sb("onesb", (128, 128), BF16)
    lnv = sb("lnv", (128, 64))
    convw = sb("convw", (128, 264))
    convb = sb("convb", (128, 88))
    glan = sb("glan", (128, 2)); difn = sb("difn", (128, 2))
    nbg = sb("nbg", (64, 8))
    lpt = sb("lpt", (64, 8)); lpp = sb("lpp", (64, 4)); lame = sb("lame", (128, 4)); nlam = sb("nlam", (128, 2))
    wgu = sb("wgu", (16, 512))
    cbias = sb("cbias", (128, 4)); t5sb = sb("t5sb", (32, 4)); f3sb = sb("f3sb", (4, 512))
    trr = sb("trr", (128, 384)); strip = sb("strip", (128, 4 * 384))
    vstage = sb("vstage", (128, 128))

    xin = sb("xin", (128, 2 * 1024)); xT = sb("xT", (128, 8 * TTM))
    hT = sb("hT", (128, 8 * TTM), BF16); mixT = sb("mixT", (128, 8 * TTM), BF16)
    mo = sb("mo", (128, 8 * TTM)); sqb = [sb("sqb%d" % i, (128, TTM), BF16) for i in range(2)]
    rs = sb("rs", (128, TTM)); lnt = sb("lnt", (128, TTM))
    NSLAB = 3
    wslab = [sb("wslab%d" % i, (128, 5632), BF16) for i in range(NSLAB)]
    gdT = sb("gdT", (16, TTM))
    G = sb("G", (64, 4 * TTM)); Gc = sb("Gc", (64, 4 * TTM)); tA = sb("tA", (64, 4 * TTM)); tB = sb("tB", (64, 4 * TTM))
    qdec = sb("qdec", (64, 4 * TTM), BF16); kinv = sb("kinv", (64, 4 * TTM), BF16); kend = sb("kend", (64, 4 * TTM), BF16)
    NCH = TTM // 64
    dec = sb("dec", (64, 4 * NCH))
    kendtok = sb("kendtok", (64, NCH * 256), BF16)
    vgtok = sb("vgtok", (64, NCH * 512), BF16)
    ATsb = sb("ATsb", (64, NCH * 256), BF16)
    S = [sb("S%d" % l, (64, 512)) for l in range(2)]
    Sb = [sb("Sb%d" % l, (64, 512), BF16) for l in range(2)]
    rgs = sb("rgs", (128, 4 * TTM))
    sq4 = sb("sq4", (128, 4 * TTM), BF16)
    qpad = sb("qpad", (128, 8 * TTM), BF16); ktst = sb("ktst", (128, 4 * TTM), BF16)
    vst = sb("vst", (128, 2 * 512), BF16); kvo = sb("kvo", (128, 2 * 2 * 512))
    NG = 3
    KTg = [sb("KTg%d" % i, (128, 1024), BF16) for i in range(NG)]
    Vg = [sb("Vg%d" % i, (128, 1024), BF16) for i in range(NG)]
    PT = [sb("PT%d" % i, (128, 2 * TTM), BF16) for i in range(3)]
    PTr = [sb("PTr%d" % i, (128, 2 * TTM), BF16) for i in range(3)]
    SBK = [0, 4, 1]
    t01 = sb("t01", (128, 2 * TTM)); odr = sb("odr", (128, TTM))
    r01 = sb("r01", (128, 2 * TTM))
    NE = 4
    extg = [sb("extg%d" % i, (128, TTM + 2)) for i in range(NE)]
    extu = [sb("extu%d" % i, (128, TTM + 2)) for i in range(NE)]
    NC_ = 5
    cg = [sb("cg%d" % i, (128, TTM)) for i in range(NC_)]; cu = [sb("cu%d" % i, (128, TTM)) for i in range(NC_)]
    f1 = [sb("f1%d" % i, (128, TTM)) for i in range(2)]; f2 = [sb("f2%d" % i, (128, TTM)) for i in range(2)]
    yT = sb("yT", (128, 22 * TTM), BF16)
    hist = [sb("hist%d" % l, (128, 88)) for l in range(2)]
    ckst = kvo[:, 0:1024]; ktc = vst[:, 0:1024]
    og4 = mo[:, 0:4 * TTM]; rs4 = xin[:, 0:4 * TTM]

    ps01 = st.enter_context(nc.psum_tensor("ps01", [128, 1024], F32))
    pb23 = [st.enter_context(nc.psum_tensor("pb%d" % i, [128, 512], F32)) for i in (2, 3)]
    ps45 = st.enter_context(nc.psum_tensor("ps45", [128, 1024], F32))
    pb67 = [st.enter_context(nc.psum_tensor("pb%d" % i, [128, 512], F32)) for i in (6, 7)]
    pbank = [ps01[:, 0:512], ps01[:, 512:1024], pb23[0], pb23[1], ps45[:, 0:512], ps45[:, 512:1024], pb67[0], pb67[1]]
    psS = [ps01, ps45]
    for i in range(8):
        bufs["pb%d" % i] = Buf("pb%d" % i, excl=True)
    pbuf = [[B("pb%d" % i)] * 2 for i in range(8)]

    def PB(i):
        return [pbuf[i][0]]

    def mm(out, lhsT, rhs, start, stop, reads, writes, inc=None):
        fw.op("pe", lambda e: e.matmul(out, lhsT=lhsT, rhs=rhs, start=start, stop=stop),
              reads, writes, inc=(stop if inc is None else inc))

    def tr(out, in_, ident, reads, writes, inc=True):
        fw.op("pe", lambda e: e.transpose(out=out, in_=in_, identity=ident), reads, writes, inc=inc)

    def act(out, in_, func, reads, writes, scale=None, bias=None):
        kw = {}
        if scale is not None:
            kw["scale"] = scale
        if bias is not None:
            kw["bias"] = bias
        fw.op("act", lambda e: e.activation(out=out, in_=in_, func=func, **kw), reads, writes)

    def tt(en, out, in0, in1, op, reads, writes):
        fw.op(en, lambda e: e.tensor_tensor(out=out, in0=in0, in1=in1, op=op), reads, writes)

    def ts(en, out, in0, s1, s2, op0, op1, reads, writes):
        if op1 is None:
            fw.op(en, lambda e: e.tensor_scalar(out=out, in0=in0, scalar1=s1, scalar2=None, op0=op0), reads, writes)
        else:
            fw.op(en, lambda e: e.tensor_scalar(out=out, in0=in0, scalar1=s1, scalar2=s2, op0=op0, op1=op1), reads, writes)

    def stt(out, in0, scalar, in1, op0, op1, reads, writes):
        fw.op("dve", lambda e: e.scalar_tensor_tensor(out=out, in0=in0, scalar=scalar, in1=in1, op0=op0, op1=op1),
              reads, writes)

    def cp(en, out, in_, reads, writes):
        fw.op(en, lambda e: e.tensor_copy(out=out, in_=in_), reads, writes)

    def v3(ap, a):
        return ap.rearrange("p (a b) -> p a b", a=a)

    ident = consts[:, C_ID:C_ID + 128]
    onesf = consts[:, C_ONES:C_ONES + 128]
    Bc = B("consts")

    fw.dma("sp", consts[:], d_consts.ap(), writes=[Bc])
    cp("dve", identb[:], ident, [Bc], [B("identb")])
    cp("dve", onesb[:], onesf, [Bc], [B("onesb")])
    Bi, Bo = B("identb"), B("onesb")

    conv_order = [("in", d_win, wb_in, D), ("out", d_wout, wb_out, D), ("up", d_wup, wb_up, D), ("dn", d_wdn, wb_dn, DFF)]
    pool_wait_layer = {}
    for l in range(2):
        for name, src, dst, rows in conv_order:
            r0 = 0
            while r0 < rows:
                nr = min(256, rows - r0)
                fw.dma("pool", dst.ap()[l, r0:r0 + nr, :], src.ap()[l, r0:r0 + nr, :], writes=[])
                r0 += nr
            pw = []
            for i, sk in enumerate(fw.dma_sems["pool"]):
                n = fw.E["pool"].dma_n
                cnt = (n - i + fw.nd - 1) // fw.nd if n > i else 0
                if cnt > 0:
                    pw.append((sk, 16 * cnt))
            pool_wait_layer[(name, l)] = pw
    weights_waited = set()

    def wait_weights(l):
        if l in weights_waited:
            return
        weights_waited.add(l)
        sp = fw.E["sp"]
        w0 = []
        for k, v in pool_wait_layer[l]:
            if sp.seen.get(k, 0) < v:
                sp.seen[k] = v
                w0.append((k, v))
        sp.ops.append((w0, None, None, 0))

    def load_cols(src_ap, R, W, dst_ap, dstbuf, post=None):
        fw.dma("sp", vstage[0:R, 0:W], src_ap, writes=[B("vstage")])
        tr(pbank[7][0:W, 0:R], vstage[0:R, 0:W], consts[0:R, C_ID:C_ID + R], [B("vstage"), Bc], PB(7))
        if post is None:
            cp("dve", dst_ap, pbank[7][0:W, 0:R], PB(7), dstbuf if isinstance(dstbuf, list) else [dstbuf])
        else:
            post(pbank[7][0:W, 0:R])

    for kind in range(4):
        load_cols(d_ln[kind].ap().rearrange("l (c p) -> (l c) p", p=128), 16, 128,
                  lnv[:, kind * 16:(kind + 1) * 16], B("lnv"))
    for l in range(2):
        for tp in range(3):
            load_cols(d_cw.ap()[l, tp, :].rearrange("(c p) -> c p", p=128), 44, 128,
                      convw[:, (l * 3 + tp) * 44:(l * 3 + tp + 1) * 44], B("convw"))
        load_cols(d_cb.ap()[l, :].rearrange("(c p) -> c p", p=128), 44, 128, convb[:, l * 44:(l + 1) * 44], B("convb"))
    load_cols(d_gn.ap(), 2, 128, glan[:], B("glan"))
    load_cols(d_dn.ap(), 2, 128, difn[:], B("difn"))
    for l in range(2):
        ts("dve", difn[:, l:l + 1], difn[:, l:l + 1], 1.0 - lam_init(l), None, ALU.mult, None, [B("difn")], [B("difn")])
    load_cols(d_bgu.ap().rearrange("l (h k) -> (l h) k", k=64), 8, 64, nbg[:], B("nbg"),
              post=lambda ps: ts("dve", nbg[:], ps, -1.0, None, ALU.mult, None, PB(7), [B("nbg")]))
    load_cols(d_lp.ap().rearrange("l i k -> (l i) k"), 8, 64, lpt[:], B("lpt"))
    fw.dma("sp", v3(wgu[:], 2), d_wgu.ap().rearrange("l r c -> r l c"), writes=[B("wgu")])
    lp3 = v3(lpt[:], 4)
    tt("dve", lpp[:], lp3[:, :, 0], lp3[:, :, 1], ALU.mult, [B("lpt")], [B("lpp")])
    mm(pbank[7][:, 0:4], consts[0:64, C_ONES:C_ONES + 128], lpp[:], True, True, [Bc, B("lpp")], PB(7))
    act(lame[:], pbank[7][:, 0:4], AF.Exp, PB(7), [B("lame")])
    for l in range(2):
        tt("dve", nlam[:, l:l + 1], lame[:, 2 * l + 1:2 * l + 2], lame[:, 2 * l:2 * l + 1], ALU.subtract, [B("lame")], [B("nlam")])
        ts("dve", nlam[:, l:l + 1], nlam[:, l:l + 1], -lam_init(l), None, ALU.add, None, [B("nlam")], [B("nlam")])
    fw.dma("sp", t5sb[:], d_t5.ap(), writes=[B("t5sb")])
    fw.dma("sp", cbias[0:32, :], bass.AP(d_t5, 15 * 4, [[0, 32], [1, 4]]), writes=[B("cbias")])
    tt("dve", t5sb[:], t5sb[:], cbias[0:32, :], ALU.subtract, [B("t5sb"), B("cbias")], [B("t5sb")])
    mm(pbank[7][0:4, 0:512], t5sb[:], consts[0:32, C_OHR:C_OHR + 512], True, True, [B("t5sb"), Bc], PB(7))
    act(f3sb[:], pbank[7][0:4, 0:512], AF.Exp, PB(7), [B("f3sb")])
    fw.dma("pool", f3d.ap(), f3sb[:], reads=[B("f3sb")], writes=[B("f3d")])
    for h in range(4):
        fw.dma("sp", trr[:], bass.AP(f3d, h * 512, [[1, 128], [1, 384]]), reads=[B("f3d")], writes=[B("trr")])
        mm(pbank[7][:, 0:384], consts[:, C_J:C_J + 128], trr[:], True, True, [Bc, B("trr")], PB(7))
        cp("dve", strip[:, h * 384:(h + 1) * 384], pbank[7][:, 0:384], PB(7), [B("strip")])
        fw.op("dve", (lambda hh: (lambda e: e.memset(strip[64:128, hh * 384:hh * 384 + 64], 0.0)))(h), [], [B("strip")])

    slab_seq = []

    class SlabStream:
        def __init__(self):
            self.plan = []
            self.loaded = 0
            self.used = 0

        def add(self, pieces, wkey):
            self.plan.append((pieces, wkey))

        def ensure(self, upto):
            while self.loaded < min(upto, len(self.plan)):
                i = self.loaded
                pieces, wkey = self.plan[i]
                wait_weights(wkey)
                sl = wslab[i % NSLAB]
                for col0, c, n, src in pieces:
                    dst = sl[:, col0:col0 + c * n].rearrange("p (c n) -> p c n", c=c)
                    fw.dma("sp", dst, src, reads=[], writes=[B("wslab%d" % (i % NSLAB))])
                self.loaded += 1

        def get(self):
            i = self.used
            self.ensure(i + NSLAB)
            self.used += 1
            return wslab[i % NSLAB], B("wslab%d" % (i % NSLAB))

    ss = SlabStream()

    def w_pieces_in(l, c0, n):
        return [(0, 8, n, wb_in.ap()[l, :, c0:c0 + n].rearrange("(c p) n -> p c n", p=128))]

    IN_SLABS = [(1536, 16), (512, 512), (0, 512), (1024, 512), (1552, 512), (2064, 512), (2576, 512)]

    def plan_tile_layer(l):
        for c0, n in IN_SLABS:
            ss.add(w_pieces_in(l, c0, n), ("in", l))
        for c0 in (0, 512):
            ss.add([(0, 8, 512, wb_out.ap()[l, :, c0:c0 + 512].rearrange("(c p) n -> p c n", p=128))], ("out", l))
        for s in range(11):
            ss.add([(0, 8, 256, wb_up.ap()[l, :, s * 256:(s + 1) * 256].rearrange("(c p) n -> p c n", p=128)),
                    (2048, 8, 256, wb_up.ap()[l, :, DFF + s * 256:DFF + (s + 1) * 256].rearrange("(c p) n -> p c n", p=128))],
                   ("up", l))
        for s in range(4):
            ss.add([(0, 22, 256, wb_dn.ap()[l, :, s * 256:(s + 1) * 256].rearrange("(c p) n -> p c n", p=128))], ("dn", l))

    rot = {"sq": 0, "nt": 0, "acc": 0}
    ACC = [(0, 0), (1, 0), (4, 0), (5, 0)]

    def acc_next():
        b, hf = ACC[rot["acc"] % 4]
        rot["acc"] += 1
        return b, hf

    def pslice(b, hf, p, n):
        return pbank[b][0:p, hf * 256:hf * 256 + n]

    def rms_stats(chunks, nparts_feat, TT, tagbufs, bank=6):
        n = len(chunks)
        for i, (ap, rb) in enumerate(chunks):
            q = sqb[rot["sq"] % 2]
            qb = B("sqb%d" % (rot["sq"] % 2))
            rot["sq"] += 1
            act(q[:, 0:TT], ap, AF.Square, rb, [qb])
            mm(pbank[bank][:, 0:TT], onesb[:], q[:, 0:TT], i == 0, i == n - 1, [Bo, qb], PB(bank), inc=True)
        act(lnt[:, 0:TT], pbank[bank][:, 0:TT], AF.Ln, PB(bank), [B("lnt")], scale=1.0 / nparts_feat, bias=EPS)
        act(rs[:, 0:TT], lnt[:, 0:TT], AF.Exp, [B("lnt")], [B("rs")], scale=-0.5)

    def x3(t, TT):
        return t[:, 0:8 * TT].rearrange("p (c t) -> p c t", c=8)

    def run_segment(sg):
        T, TT, P = sg.T, sg.TT, sg.P
        ntile = T // TT
        nblk = max(1, TT // 128)
        bt = min(128, TT)
        nch = TT // 64
        xT3 = x3(xT, TT); hT3 = x3(hT, TT); mix3 = x3(mixT, TT); mo3 = x3(mo, TT)
        G3 = v3(G[:, 0:4 * TT], 4); Gc3 = v3(Gc[:, 0:4 * TT], 4)
        qd3 = v3(qdec[:, 0:4 * TT], 4); ki3 = v3(kinv[:, 0:4 * TT], 4); ke3 = v3(kend[:, 0:4 * TT], 4)
        rgs3 = v3(rgs[:, 0:4 * TT], 4); ktst3 = v3(ktst[:, 0:4 * TT], 4)
        qdT3 = qpad[:, 0:8 * TT].rearrange("p (m h t) -> p m h t", m=2, h=4)
        fw.op("pool", lambda e: e.memset(qpad[:], 0.0), [], [B("qdT")])
        yT3 = yT[:, 0:22 * TT].rearrange("p (c t) -> p c t", c=22)

        for l in range(2):
            sg.init_state(l)

        for it in range(ntile):
            tok0 = it * TT
            def load_x(it_):
                if it_ >= ntile:
                    return
                for bl_ in range(nblk):
                    fw.dma("sp", xin[0:bt, bl_ * 1024:(bl_ + 1) * 1024],
                           sg.x.ap()[it_ * TT + bl_ * bt:it_ * TT + (bl_ + 1) * bt, :], writes=[B("xin")])
            if it == 0:
                load_x(0)
            sg.prefetch = (lambda it_=it: load_x(it_ + 1))
            for bl in range(nblk):
                for c0 in (0, 4):
                    bk = 2 + (c0 // 4)
                    for c in range(c0, c0 + 4):
                        tr(pbank[bk][:, (c - c0) * 128:(c - c0) * 128 + bt], xin[0:bt, bl * 1024 + c * 128:bl * 1024 + (c + 1) * 128],
                           consts[0:bt, C_ID:C_ID + bt], [B("xin"), Bc], PB(bk), inc=(c == c0 + 3))
                    fw.op("act", (lambda c0_, bl_, bk_: (lambda e: e.activation(
                        out=xT3[:, c0_:c0_ + 4, bl_ * bt:(bl_ + 1) * bt],
                        in_=pbank[bk_][:, 0:512].rearrange("p (c t) -> p c t", c=4)[:, :, 0:bt], func=AF.Copy)))(c0, bl, bk),
                        PB(bk), [B("xT")])

            for l in range(2):
                layer_tile(sg, l, it, T, TT, P, nblk, bt, nch, xT3, hT3, mix3, mo3, G3, Gc3, qd3, ki3, ke3, rgs3, qdT3,
                           ktst3, yT3)

            for bl in range(nblk):
                for c0 in (0, 4):
                    bk = 2 + (c0 // 4)
                    for c in range(c0, c0 + 4):
                        tr(pbank[bk][0:bt, (c - c0) * 128:(c - c0 + 1) * 128], xT3[:, c, bl * bt:(bl + 1) * bt],
                           ident, [B("xT"), Bc], PB(bk), inc=(c == c0 + 3))
                    act(kvo[0:bt, bl * 1024 + c0 * 128:bl * 1024 + (c0 + 4) * 128], pbank[bk][0:bt, 0:512], AF.Copy,
                        PB(bk), [B("kvoK" if bl == 0 else "kvoV")])
            for bl in range(nblk):
                fw.dma("pool", sg.y.ap()[tok0 + bl * bt:tok0 + (bl + 1) * bt, :], kvo[0:bt, bl * 1024:(bl + 1) * 1024],
                       reads=[B("kvoK" if bl == 0 else "kvoV")], writes=[])

        for l in range(2):
            fw.dma("pool", sg.gout.ap()[l].rearrange("h k v -> k h v"), v3(S[l][:], 4), reads=[B("S%d" % l)], writes=[])
            tr(pbank[7][0:88, 0:128], hist[l][:], ident, [B("hist%d_%d" % (l, jj)) for jj in range(44)] + [Bc], PB(7))
            cp("dve", vstage[0:88, :], pbank[7][0:88, 0:128], PB(7), [B("vstage")])
            fw.dma("pool", sg.cout.ap()[l].rearrange("r (c p) -> (r c) p", p=128), vstage[0:88, :],
                   reads=[B("vstage")], writes=[])

    def layer_tile(sg, l, it, T, TT, P, nblk, bt, nch, xT3, hT3, mix3, mo3, G3, Gc3, qd3, ki3, ke3, rgs3, qdT3,
                   ktst3, yT3):
        tok0 = it * TT
        Bx, Bh, Bmix, Bmo = B("xT"), B("hT"), B("mixT"), B("mo")
        BS, BSb = B("S%d" % l), B("Sb%d" % l)

        def lnp(kind, c):
            i = kind * 16 + l * 8 + c
            return lnv[:, i:i + 1]

        if l == 1:
            act(lnt[:, 0:TT], pbank[7][:, 0:TT], AF.Ln, PB(7), [B("lnt")], scale=1.0 / D, bias=EPS)
            act(rs[:, 0:TT], lnt[:, 0:TT], AF.Exp, [B("lnt")], [B("rs")], scale=-0.5)
        else:
            rms_stats([(xT3[:, c, :], [Bx]) for c in range(8)], D, TT, None)
        for c in range(8):
            stt(hT3[:, c, :], xT3[:, c, :], lnp(0, c), rs[:, 0:TT], ALU.mult, ALU.mult, [Bx, B("lnv"), B("rs")], [Bh])

        if DBG["stop"] <= 1:
            return
        def proj_fm(slab, sbuf_, col0, M, kparts=128):
            b, hf = acc_next()
            sl3 = slab[:, 0:8 * slab_n[0]].rearrange("p (c n) -> p c n", c=8)
            for kc in range(8):
                mm(pslice(b, hf, M, TT), sl3[:, kc, col0:col0 + M], hT3[:, kc, :], kc == 0, kc == 7,
                   [sbuf_, Bh], [pbuf[b][hf]])
            return b, hf

        slab_n = [16]
        slab3, sb3 = ss.get()
        b, hf = proj_fm(slab3, sb3, 0, 16)
        act(gdT[:, 0:TT], pslice(b, hf, 16, TT), AF.Copy, [pbuf[b][hf]], [B("gdT")])
        for h in range(4):
            bk, hf = 6 + h // 2, h % 2
            mm(pbank[bk][0:64, hf * 256:hf * 256 + TT], wgu[:, l * 256 + h * 64:l * 256 + (h + 1) * 64], gdT[:, 0:TT],
               True, True, [B("wgu"), B("gdT")], [pbuf[bk][hf]])
        for h in range(4):
            bk, hf = 6 + h // 2, h % 2
            act(G3[:, h, :], pbank[bk][0:64, hf * 256:hf * 256 + TT], AF.Exp, [pbuf[bk][hf], B("nbg")], [B("G")],
                scale=-1.0, bias=nbg[:, l * 4 + h:l * 4 + h + 1])
        act(G[:, 0:4 * TT], G[:, 0:4 * TT], AF.Ln, [B("G")], [B("G")], bias=1.0)
        slab1, sb1 = ss.get()
        sl3_1 = slab1[:, 0:8 * 512].rearrange("p (c n) -> p c n", c=8)
        for c in range(nch):
            bk = 6 + c % 2
            for kc in range(8):
                mm(pbank[bk][0:64, 0:512], hT3[:, kc, c * 64:(c + 1) * 64], sl3_1[:, kc, :], kc == 0, kc == 7,
                   [sb1, Bh], PB(bk))
            act(vgtok[:, c * 512:(c + 1) * 512], pbank[bk][0:64, 0:512], AF.Copy, PB(bk), [B("vgtok")])
        fw.op("dve", lambda e: e.tensor_tensor_scan(out=Gc[:, 0:4 * TT], data0=consts[0:64, C_SCAN:C_SCAN + 4 * TT],
                                                    data1=G[:, 0:4 * TT], initial=0.0, op0=ALU.mult, op1=ALU.add),
              [B("G"), Bc], [B("Gc")])
        act(tA[:, 0:4 * TT], Gc[:, 0:4 * TT], AF.Exp, [B("Gc")], [B("tA")], scale=-1.0 / 16.0)
        act(tB[:, 0:4 * TT], Gc[:, 0:4 * TT], AF.Exp, [B("Gc")], [B("tB")], scale=1.0 / 16.0)
        tA3 = v3(tA[:, 0:4 * TT], 4); tB3 = v3(tB[:, 0:4 * TT], 4); tC3 = v3(G[:, 0:4 * TT], 4)
        dec3 = dec[:, 0:4 * nch].rearrange("p (h c) -> p h c", h=4)
        act(dec3, bass.AP(Gc, 63, [[4 * TTM, 64], [TT, 4], [64, nch]]),
            AF.Exp, [B("Gc")], [B("dec")], scale=-1.0 / 16.0)
        for h in range(4):
            tt("dve", tC3[:, h, :].rearrange("p (c t) -> p c t", t=64), tB3[:, h, :].rearrange("p (c t) -> p c t", t=64),
               bass.AP(dec, h * nch, [[4 * NCH, 64], [1, nch], [0, 64]]), ALU.mult, [B("tB"), B("dec"), B("G")], [B("G")])
        if DBG["stop"] <= 1.2:
            return
        slab0, sb0 = ss.get()
        sl3_0 = slab0[:, 0:8 * 512].rearrange("p (c n) -> p c n", c=8)
        for which in range(2):
            for h in range(4):
                b, hf = acc_next()
                for kc in range(8):
                    mm(pslice(b, hf, 64, TT), sl3_0[:, kc, which * 256 + h * 64:which * 256 + (h + 1) * 64],
                       hT3[:, kc, :], kc == 0, kc == 7, [sb0, Bh], [pbuf[b][hf]])
                if which == 0:
                    stt(qd3[:, h, :], pslice(b, hf, 64, TT), 0.125, tA3[:, h, :], ALU.mult, ALU.mult,
                        [pbuf[b][hf], B("tA")], [B("qdec")])
                else:
                    tt("dve", ki3[:, h, :], pslice(b, hf, 64, TT), tB3[:, h, :], ALU.mult, [pbuf[b][hf], B("tB")], [B("kinv")])
                    tt("dve", ke3[:, h, :], pslice(b, hf, 64, TT), tC3[:, h, :], ALU.mult, [pbuf[b][hf], B("G")], [B("kend")])
        pb7b = pbank[7][:].bitcast(BF16)
        for c in range(nch):
            for h in range(4):
                tr(pb7b[0:64, (c * 4 + h) * 64:(c * 4 + h + 1) * 64], ke3[:, h, c * 64:(c + 1) * 64], identb[0:64, 0:64],
                   [B("kend"), Bi], PB(7), inc=(h == 3 and c == nch - 1))
        cp("dve", kendtok[:, 0:nch * 256], pb7b[0:64, 0:nch * 256], PB(7), [B("kendtok")])
        if DBG["stop"] <= 1.4:
            return
        kvo4 = kvo[:].rearrange("p (w b n) -> p w b n", w=2, b=2)
        vst3 = v3(vst[:], 2)
        Bkt = [B(sg.name + "kt%d_%d" % (l, h_)) for h_ in range(4)]
        Bvs = [B(sg.name + "vs%d_%d" % (l, b_)) for b_ in range(2)]
        Bvs_r = Bvs + sg.extra.get(l, [])

        def slab_rg():
            slab2, sb2 = ss.get()
            slab_n[0] = 512
            for h in range(4):
                b, hf = proj_fm(slab2, sb2, h * 128, 128)
                act(rgs3[:, h, :], pslice(b, hf, 128, TT), AF.Silu, [pbuf[b][hf]], [B("rgs")])

        def slab_qd():
            slab4, sb4 = ss.get()
            slab_n[0] = 512
            for h in range(4):
                b, hf = proj_fm(slab4, sb4, h * 128, 128)
                for m in range(2):
                    act(qdT3[m * 64:(m + 1) * 64, m, h, :], pbank[b][m * 64:(m + 1) * 64, hf * 256:hf * 256 + TT], AF.Copy,
                        [pbuf[b][hf]], [B("qdT")])

        def slab_kd():
            slab5, sb5 = ss.get()
            slab_n[0] = 512
            sl3_5 = slab5[:, 0:8 * 512].rearrange("p (c n) -> p c n", c=8)
            for h in range(4):
                b, hf = proj_fm(slab5, sb5, h * 128, 128)
                act(ktst3[:, h, :], pslice(b, hf, 128, TT), AF.Copy, [pbuf[b][hf]], [B("ktst")])
            for bl in range(nblk):
                bk, _hf = acc_next()
                for kc in range(8):
                    mm(pbank[bk][0:bt, 0:512], hT3[:, kc, bl * bt:(bl + 1) * bt], sl3_5[:, kc, :], kc == 0, kc == 7,
                       [sb5, Bh], PB(bk))
                act(kvo4[0:bt, 0, bl, :], pbank[bk][0:bt, 0:512], AF.Copy, PB(bk), [B("kvoK")])
            for h in range(4):
                fw.dma("act", sg.kt.ap()[l, h, :, P + tok0:P + tok0 + TT], ktst3[:, h, :], reads=[B("ktst")], writes=[Bkt[h]])
            for bl in range(nblk):
                fw.dma("act", sg.kout.ap()[l, :, tok0 + bl * bt:tok0 + (bl + 1) * bt, :].rearrange("h t d -> t h d"),
                       kvo4[0:bt, 0, bl, :].rearrange("p (h d) -> p h d", h=4), reads=[B("kvoK")], writes=[])

        def slab_vd():
            slab6, sb6 = ss.get()
            sl3_6 = slab6[:, 0:8 * 512].rearrange("p (c n) -> p c n", c=8)
            for bl in range(nblk):
                bk, _hf = acc_next()
                for kc in range(8):
                    mm(pbank[bk][0:bt, 0:512], hT3[:, kc, bl * bt:(bl + 1) * bt], sl3_6[:, kc, :], kc == 0, kc == 7,
                       [sb6, Bh], PB(bk))
                act(kvo4[0:bt, 1, bl, :], pbank[bk][0:bt, 0:512], AF.Copy, PB(bk), [B("kvoV")])
                cp("dve", vst3[0:bt, bl, :], kvo4[0:bt, 1, bl, :], [B("kvoV")], [B("vst")])
            for bl in range(nblk):
                r0 = P + tok0 + bl * bt
                fw.dma("act", sg.vsc.ap()[l, :, r0:r0 + bt, :].rearrange("h t d -> t h d"),
                       vst3[0:bt, bl, :].rearrange("p (h d) -> p h d", h=4), reads=[B("vst")], writes=[Bvs[bl % 2]])
                fw.dma("act", sg.vout.ap()[l, :, tok0 + bl * bt:tok0 + (bl + 1) * bt, :].rearrange("h t d -> t h d"),
                       kvo4[0:bt, 1, bl, :].rearrange("p (h d) -> p h d", h=4), reads=[B("kvoV")], writes=[])

        late_slabs = [slab_rg, slab_qd, slab_kd, slab_vd]
        if DBG["stop"] <= 2:
            while late_slabs:
                late_slabs.pop(0)()
            return
        S3 = v3(S[l][:], 4); Sb3 = v3(Sb[l][:], 4)
        at4 = ATsb[:, 0:nch * 256].rearrange("p (c h t) -> p c h t", c=nch, h=4)
        kt4 = kendtok[:, 0:nch * 256].rearrange("p (c h k) -> p c h k", c=nch, h=4)
        for c in range(nch):
            cs = slice(c * 64, (c + 1) * 64)
            for h in range(4):
                mm(pbank[6][0:64, h * 64:(h + 1) * 64], ki3[:, h, cs], qd3[:, h, cs], True, True,
                   [B("kinv"), B("qdec")], [pbuf[6][0]], inc=(h == 3))
            tt("dve", at4[:, c, :, :], pbank[6][0:64, 0:256].rearrange("p (h t) -> p h t", h=4),
               bass.AP(consts, C_TRI, [[NCW, 64], [0, 4], [1, 64]]), ALU.mult, [pbuf[6][0], Bc], [B("ATsb")])
            if late_slabs:
                late_slabs.pop(0)()
            for h in range(4):
                bk, hf = 2 + h // 2, h % 2
                o_ap = pbank[bk][:, hf * 256 + c * 64:hf * 256 + (c + 1) * 64]
                mm(o_ap, vgtok[:, c * 512 + h * 128:c * 512 + (h + 1) * 128], at4[:, c, h, :], True, False,
                   [B("vgtok"), B("ATsb")], [pbuf[bk][hf]])
                mm(o_ap, Sb3[:, h, :], qd3[:, h, cs], False, True, [BSb, B("qdec")], [pbuf[bk][hf]])
            for h in range(4):
                mm(pbank[7][0:64, h * 128:(h + 1) * 128], kt4[:, c, h, :], vgtok[:, c * 512 + h * 128:c * 512 + (h + 1) * 128],
                   True, True, [B("kendtok"), B("vgtok")], PB(7), inc=(h == 3))
            tt("dve", S3, S3, bass.AP(dec, c, [[4 * NCH, 64], [nch, 4], [0, 128]]), ALU.mult, [BS, B("dec")], [BS])
            tt("dve", S[l][:], S[l][:], pbank[7][0:64, 0:512], ALU.add, [BS] + PB(7), [BS])
            act(Sb[l][:], S[l][:], AF.Copy, [BS], [BSb])
        while late_slabs:
            late_slabs.pop(0)()
        og3 = v3(og4[:, 0:4 * TT], 4)
        for hp in range(2):
            src = pbank[2 + hp][:, 0:512].rearrange("p (h t) -> p h t", h=2)[:, :, 0:TT]
            if hp == 0:
                act(og3[:, 0:2, :], src, AF.Copy, PB(2), [B("mo")])
            else:
                cp("dve", og3[:, 2:4, :], src, PB(3), [B("mo")])

        def gla_epilogue():
            act(sq4[:, 0:4 * TT], og4[:, 0:4 * TT], AF.Square, [B("mo")], [B("sq4")])
            sq3 = v3(sq4[:, 0:4 * TT], 4)
            banks = []
            for hp in range(2):
                bk = (3, 5)[hp]
                banks.append(bk)
                for hh in range(2):
                    mm(pbank[bk][:, hh * 256:hh * 256 + TT], onesb[:], sq3[:, hp * 2 + hh, :], True, True, [Bo, B("sq4")], PB(bk),
                       inc=(hh == 1))
            rs3 = v3(rs4[:, 0:4 * TT], 4)
            for hp in range(2):
                bk = banks[hp]
                act(rs3[:, hp * 2:hp * 2 + 2, :], pbank[bk][:, 0:512].rearrange("p (h t) -> p h t", h=2)[:, :, 0:TT], AF.Ln,
                    PB(bk), [B("xin")], scale=1.0 / 128, bias=EPS)
            act(rs4[:, 0:4 * TT], rs4[:, 0:4 * TT], AF.Exp, [B("xin")], [B("xin")], scale=-0.5)
            stt(og4[:, 0:4 * TT], og4[:, 0:4 * TT], glan[:, l:l + 1], rs4[:, 0:4 * TT], ALU.mult, ALU.mult,
                [B("mo"), B("glan"), B("xin")], [B("mo")])
            tt("dve", mixT[:, 0:4 * TT], og4[:, 0:4 * TT], rgs[:, 0:4 * TT], ALU.mult, [B("mo"), B("rgs")], [Bmix])

        if DBG["stop"] <= 3:
            gla_epilogue()

        if DBG["stop"] <= 3:
            return
        q0 = P + tok0
        nkeys = P + tok0 + TT
        nkb = (nkeys + 127) // 128
        ngrp = (nkb + 7) // 8
        deferred = [gla_epilogue]
        for h in range(4):
            hs = strip[:, h * 384:(h + 1) * 384]
            first = True
            pend = []
            nblk_done = 0
            for g in range(ngrp):
                gi = rot.setdefault("grp", 0)
                rot["grp"] += 1
                ktg, vg_ = KTg[gi % NG], Vg[gi % NG]
                Bkg, Bvg = B("KTg%d" % (gi % NG)), B("Vg%d" % (gi % NG))
                k0g = g * 1024
                nkg = min(1024, nkeys - k0g)
                fw.dma("sp", ktg[:, 0:nkg], sg.kt.ap()[l, h, :, k0g:k0g + nkg], reads=[Bkt[h]], writes=[Bkg])
                nfull = nkg // 128
                if nfull:
                    fw.dma("sp", vg_[:, 0:nfull * 128].rearrange("p (b d) -> p b d", d=128),
                           sg.vsc.ap()[l, h, k0g:k0g + nfull * 128, :].rearrange("(b p) d -> p b d", p=128),
                           reads=Bvs_r, writes=[Bvg])
                if nkg % 128:
                    rem = nkg % 128
                    fw.dma("sp", vg_[0:rem, nfull * 128:(nfull + 1) * 128],
                           sg.vsc.ap()[l, h, k0g + nfull * 128:k0g + nkg, :], reads=Bvs_r, writes=[Bvg])
                nb_g = (nkg + 127) // 128
                for kb in range(nb_g):
                    k0 = k0g + kb * 128
                    nk = min(128, nkeys - k0)
                    j = (k0 - q0) // 128
                    c0 = max(0, 128 * j)
                    ncol = TT - c0
                    gb = rot.setdefault("pt", 0)
                    rot["pt"] += 1
                    last = (g == ngrp - 1 and kb == nb_g - 1)
                    sbuf_i = gb % 3
                    ptb = PT[sbuf_i][:, 0:2 * TT].rearrange("p (m t) -> p m t", m=2)
                    ptr = PTr[sbuf_i][:, 0:2 * TT].rearrange("p (m t) -> p m t", m=2)
                    Bpt, Bptr = B("PT%d" % sbuf_i), B("PTr%d" % sbuf_i)
                    near = j >= -1
                    bk0 = SBK[sbuf_i]
                    s3 = pbank[bk0][0:nk, 0:2 * TT].rearrange("p (m t) -> p m t", m=2)[:, :, c0:TT]
                    if c0 == 0:
                        mm(pbank[bk0][0:nk, 0:2 * TT], ktg[:, kb * 128:kb * 128 + nk], qdT3[:, :, h, :], True, True,
                           [Bkg, B("qdT")], PB(bk0))
                    else:
                        for m in range(2):
                            mm(pbank[bk0][0:nk, m * TT + c0:(m + 1) * TT], ktg[:, kb * 128:kb * 128 + nk], qdT3[:, m, h, c0:TT],
                               True, True, [Bkg, B("qdT")], PB(bk0), inc=(m == 1))
                    if len(pend) >= 2:
                        pend.pop(0)()
                    if near:
                        act(ptr[0:nk, :, c0:TT], s3, AF.Exp, PB(bk0), [Bptr], scale=0.125)
                        tt("dve", ptb[0:nk, :, c0:TT], ptr[0:nk, :, c0:TT],
                           bass.AP(strip, h * 384 + c0 - 128 * j, [[4 * 384, nk], [0, 2], [1, TT - c0]]), ALU.mult,
                           [Bptr, B("strip")], [Bpt])
                    else:
                        act(ptb[0:nk, :, c0:TT], s3, AF.Exp, PB(bk0), [Bpt], scale=0.125)

                    def pv(ptb=ptb, Bpt=Bpt, nk=nk, c0=c0, kb=kb, vg_=vg_, Bvg=Bvg, first=first, last=last):
                        o3_ = pbank[2][:, 0:2 * TT].rearrange("p (m t) -> p m t", m=2)[:, :, c0:TT]
                        z3_ = pbank[6][:, 0:2 * TT].rearrange("p (m t) -> p m t", m=2)[:, :, c0:TT]
                        if c0 == 0:
                            mm(pbank[2][:, 0:2 * TT], vg_[0:nk, kb * 128:(kb + 1) * 128], ptb[0:nk, :, :], first, last,
                               [Bvg, Bpt], PB(2), inc=False)
                            mm(pbank[6][:, 0:2 * TT], onesb[0:nk, :], ptb[0:nk, :, :], first, last, [Bo, Bpt], PB(6), inc=True)
                        else:
                            for m in range(2):
                                mm(pbank[2][:, m * TT + c0:(m + 1) * TT], vg_[0:nk, kb * 128:(kb + 1) * 128], ptb[0:nk, m, c0:TT],
                                   first, last and m == 1, [Bvg, Bpt], PB(2), inc=False)
                            for m in range(2):
                                mm(pbank[6][:, m * TT + c0:(m + 1) * TT], onesb[0:nk, :], ptb[0:nk, m, c0:TT],
                                   first, last and m == 1, [Bo, Bpt], PB(6), inc=(m == 1))
                    pend.append(pv)
                    nblk_done += 1
                    if nblk_done == 2 and deferred:
                        deferred.pop(0)()
                    first = False
            while pend:
                pend.pop(0)()
            act(r01[:, 0:2 * TT], pbank[6][:, 0:2 * TT], AF.Ln, PB(6), [B("r01")])
            act(r01[:, 0:2 * TT], r01[:, 0:2 * TT], AF.Exp, [B("r01")], [B("r01")], scale=-1.0)
            tt("dve", t01[:, 0:2 * TT], pbank[2][:, 0:2 * TT], r01[:, 0:2 * TT], ALU.mult, PB(2) + [B("r01")], [B("t01")])
            stt(odr[:, 0:TT], t01[:, TT:2 * TT], nlam[:, l:l + 1], t01[:, 0:TT], ALU.mult, ALU.add, [B("t01"), B("nlam")], [B("odr")])

            def att_epilogue(h=h):
                rms_stats([(odr[:, 0:TT], [B("odr")])], 128, TT, None, bank=3)
                stt(mix3[:, 4 + h, :], odr[:, 0:TT], difn[:, l:l + 1], rs[:, 0:TT], ALU.mult, ALU.mult,
                    [B("odr"), B("difn"), B("rs")], [Bmix])
            deferred.append(att_epilogue)
        while deferred:
            deferred.pop(0)()

        if DBG["stop"] <= 4:
            return
        def sq_chunk(ap, rb):
            q = sqb[rot["sq"] % 2]
            qb = B("sqb%d" % (rot["sq"] % 2))
            rot["sq"] += 1
            act(q[:, 0:TT], ap, AF.Square, rb, [qb])
            return q, qb

        def stat_mm(bank, q, qb, i, n):
            mm(pbank[bank][:, 0:TT], onesb[:], q[:, 0:TT], i == 0, i == n - 1, [Bo, qb], PB(bank), inc=True)

        def dense_fm(nslab, mper, kchunks, src3, Bsrc, kcols):
            pend = None
            for s in range(nslab):
                slab, sbf = ss.get()
                sl3 = slab[:, 0:kchunks * kcols].rearrange("p (c n) -> p c n", c=kchunks)
                for mi in range(mper):
                    mch = s * mper + mi
                    b, hf = acc_next()
                    for kc in range(kchunks):
                        mm(pslice(b, hf, 128, TT), sl3[:, kc, mi * 128:(mi + 1) * 128], src3[:, kc, :], kc == 0,
                           kc == kchunks - 1, [sbf, Bsrc], [pbuf[b][hf]])
                    if pend is not None:
                        stat_mm(6, pend[0], pend[1], pend[2], 8)
                    Bm_ = Bmo if mch < 5 else B("mo2")
                    act(mo3[:, mch, :], pslice(b, hf, 128, TT), AF.Copy, [pbuf[b][hf]], [Bm_])
                    q, qb = sq_chunk(mo3[:, mch, :], [Bm_])
                    pend = (q, qb, mch)
            stat_mm(6, pend[0], pend[1], pend[2], 8)

        def post_norm_residual(kind, fuse_next):
            act(lnt[:, 0:TT], pbank[6][:, 0:TT], AF.Ln, PB(6), [B("lnt")], scale=1.0 / D, bias=EPS)
            act(rs[:, 0:TT], lnt[:, 0:TT], AF.Exp, [B("lnt")], [B("rs")], scale=-0.5)
            rsb = bass.AP(rs, 0, [[TTM, 128], [0, 5], [1, TT]])
            tt("dve", mo3[:, 0:5, :], mo3[:, 0:5, :], rsb, ALU.mult, [Bmo, B("rs")], [Bmo])
            rsb2 = bass.AP(rs, 0, [[TTM, 128], [0, 3], [1, TT]])
            tt("pool", mo3[:, 5:8, :], mo3[:, 5:8, :], rsb2, ALU.mult, [B("mo2"), B("rs")], [B("mo2")])
            pend = None
            for c in range(8):
                stt(xT3[:, c, :], mo3[:, c, :], lnp(kind, c), xT3[:, c, :], ALU.mult, ALU.add,
                    [Bmo if c < 5 else B("mo2"), B("lnv"), Bx], [Bx])
                if fuse_next:
                    if pend is not None:
                        stat_mm(7, pend[0], pend[1], pend[2], 8)
                    q, qb = sq_chunk(xT3[:, c, :], [Bx])
                    pend = (q, qb, c)
            if fuse_next:
                stat_mm(7, pend[0], pend[1], pend[2], 8)

        def pre_norm(kind, have_stats):
            if have_stats:
                act(lnt[:, 0:TT], pbank[7][:, 0:TT], AF.Ln, PB(7), [B("lnt")], scale=1.0 / D, bias=EPS)
                act(rs[:, 0:TT], lnt[:, 0:TT], AF.Exp, [B("lnt")], [B("rs")], scale=-0.5)
            else:
                rms_stats([(xT3[:, c, :], [Bx]) for c in range(8)], D, TT, None)
            for c in range(8):
                stt(hT3[:, c, :], xT3[:, c, :], lnp(kind, c), rs[:, 0:TT], ALU.mult, ALU.mult, [Bx, B("lnv"), B("rs")], [Bh])

        dense_fm(2, 4, 8, mix3, Bmix, 512)
        post_norm_residual(1, True)

        if DBG["stop"] <= 5:
            return
        pre_norm(2, True)
        if l == 1:
            sg.prefetch()
        hist3 = v3(hist[l][:], 2)

        def Bh_(jj):
            return B("hist%d_%d" % (l, jj))

        def cw(tp, j):
            i = (l * 3 + tp) * 44 + j
            return convw[:, i:i + 1]

        ffn_slabs = {}

        def stage_a(p):
            s_ = p // 2
            pi = p % 2
            if pi == 0:
                ffn_slabs[s_] = ss.get()
            slab, sbf = ffn_slabs[s_]
            sl3 = slab[:, 0:4096].rearrange("p (c n) -> p c n", c=16)
            e = p % NE
            for gu, (ex, nm, jj) in enumerate(((extg[e], "extg", p), (extu[e], "extu", 22 + p))):
                b, hf = acc_next()
                for kc in range(8):
                    mm(pslice(b, hf, 128, TT), sl3[:, gu * 8 + kc, pi * 128:(pi + 1) * 128], hT3[:, kc, :], kc == 0, kc == 7,
                       [sbf, Bh], [pbuf[b][hf]])
                cp("pool", ex[:, 0:2], hist3[:, :, jj], [Bh_(jj)], [B("%sh%d" % (nm, e))])
                act(ex[:, 2:2 + TT], pslice(b, hf, 128, TT), AF.Copy, [pbuf[b][hf]], [B("%sm%d" % (nm, e))])

        def stage_b1(p):
            e, c_ = p % NE, p % NC_
            for (ex, nm, jj, ct, cn) in ((extg[e], "extg", p, cg[c_], "cg"), (extu[e], "extu", 22 + p, cu[c_], "cu")):
                Bexh, Bexm, Bct = B("%sh%d" % (nm, e)), B("%sm%d" % (nm, e)), B("%s%d" % (cn, c_))
                act(ct[:, 0:TT], ex[:, 0:TT], AF.Identity, [Bexh, Bexm, B("convw"), B("convb")], [Bct], scale=cw(0, jj),
                    bias=convb[:, l * 44 + jj:l * 44 + jj + 1])

        def stage_b2(p):
            e, c_ = p % NE, p % NC_
            for (ex, nm, jj, ct, cn) in ((extg[e], "extg", p, cg[c_], "cg"), (extu[e], "extu", 22 + p, cu[c_], "cu")):
                Bexh, Bexm, Bct = B("%sh%d" % (nm, e)), B("%sm%d" % (nm, e)), B("%s%d" % (cn, c_))
                stt(ct[:, 0:TT], ex[:, 1:1 + TT], cw(1, jj), ct[:, 0:TT], ALU.mult, ALU.add, [Bexh, Bexm, Bct, B("convw")], [Bct])
                stt(ct[:, 0:TT], ex[:, 2:2 + TT], cw(2, jj), ct[:, 0:TT], ALU.mult, ALU.add, [Bexm, Bct, B("convw")], [Bct])
                cp("pool", hist3[:, :, jj], ex[:, TT:TT + 2], [Bexm], [Bh_(jj)])

        def stage_c1(p):
            c_, k2 = p % NC_, p % 2
            act(f1[k2][:, 0:TT], cg[c_][:, 0:TT], AF.Square, [B("cg%d" % c_)], [B("f1%d" % k2)])
            ts("pool", f1[k2][:, 0:TT], f1[k2][:, 0:TT], 0.044715, 1.0, ALU.mult, ALU.add, [B("f1%d" % k2)], [B("f1%d" % k2)])
            tt("pool", f1[k2][:, 0:TT], f1[k2][:, 0:TT], cg[c_][:, 0:TT], ALU.mult, [B("f1%d" % k2), B("cg%d" % c_)], [B("f1%d" % k2)])

        def stage_c2(p):
            c_, k2 = p % NC_, p % 2
            act(f2[k2][:, 0:TT], f1[k2][:, 0:TT], AF.Sigmoid, [B("f1%d" % k2)], [B("f2%d" % k2)], scale=1.5957691216057308)
            tt("pool" if p % 2 == 0 else "dve", f2[k2][:, 0:TT], f2[k2][:, 0:TT], cu[c_][:, 0:TT], ALU.mult,
               [B("f2%d" % k2), B("cu%d" % c_)], [B("f2%d" % k2)])

        def stage_c3(p):
            c_, k2 = p % NC_, p % 2
            tt("dve", yT3[:, p, :], f2[k2][:, 0:TT], cg[c_][:, 0:TT], ALU.mult, [B("f2%d" % k2), B("cg%d" % c_)], [B("yT")])

        stages = (stage_a, stage_b1, stage_b2, stage_c1, stage_c2, stage_c3)
        for it_ in range(22 + len(stages) - 1):
            for si_, fn_ in enumerate(stages):
                p_ = it_ - si_
                if 0 <= p_ < 22:
                    fn_(p_)
        dense_fm(4, 2, 22, yT3, B("yT"), 256)
        post_norm_residual(3, l == 0)

    def make_seg(name, T, TT, P, x, y, kout, vout, gout, cout, kt, vsc):
        sg = Seg()
        sg.name, sg.T, sg.TT, sg.P = name, T, TT, P
        sg.x, sg.y, sg.kout, sg.vout, sg.gout, sg.cout, sg.kt, sg.vsc = x, y, kout, vout, gout, cout, kt, vsc
        sg.extra = {}
        return sg

    segp = make_seg("p", TP, TTP, 0, d_xp, o_yp, o_kp, o_vp, o_gp, o_cp, ktp, vsp)
    segs = make_seg("s", TS, TS, PS, d_xs, o_ys, o_ks, o_vs, o_gs, o_cs, kts, vss)

    def init_prompt(l):
        fw.op("dve", lambda e: e.memset(S[l][:], 0.0), [], [B("S%d" % l)])
        fw.op("dve", lambda e: e.memset(Sb[l][:], 0.0), [], [B("Sb%d" % l)])
        fw.op("dve", lambda e: e.memset(hist[l][:], 0.0), [], [B("hist%d_%d" % (l, jj)) for jj in range(44)])

    def init_sample(l):
        fw.dma("sp", v3(S[l][:], 4), d_sg.ap()[l].rearrange("h k v -> k h v"), writes=[B("S%d" % l)])
        cp("dve", Sb[l][:], S[l][:], [B("S%d" % l)], [B("Sb%d" % l)])
        load_cols(d_sc.ap()[l].rearrange("r (c p) -> (r c) p", p=128), 88, 128, hist[l][:], [B("hist%d_%d" % (l, jj)) for jj in range(44)])
        Bkt = [B("skt%d_%d" % (l, h_)) for h_ in range(4)]
        Bvs = [B("svs%d_%d" % (l, b_)) for b_ in range(2)]
        nb = PS // 128
        for h in range(4):
            fw.dma("sp", ckst[:, 0:nb * 128].rearrange("p (b d) -> p b d", d=128),
                   d_ck.ap()[l, h].rearrange("(b p) d -> p b d", p=128), writes=[B("kvoK")])
            for b0 in range(0, nb, 4):
                for b_ in range(b0, min(nb, b0 + 4)):
                    tr(pbank[7][:, (b_ - b0) * 128:(b_ - b0 + 1) * 128], ckst[:, b_ * 128:(b_ + 1) * 128], ident,
                       [B("kvoK"), Bc], PB(7), inc=(b_ == min(nb, b0 + 4) - 1))
                nn = min(nb, b0 + 4) - b0
                act(ktc[:, b0 * 128:(b0 + nn) * 128], pbank[7][:, 0:nn * 128], AF.Copy, PB(7), [B("vst")])
            fw.dma("pool", kts.ap()[l, h, :, 0:PS], ktc[:, 0:PS], reads=[B("vst")], writes=[Bkt[h]])
            for r0_ in range(0, PS, 256):
                bx_ = B("scv%d_%d_%d" % (l, h, r0_))
                segs.extra.setdefault(l, []).append(bx_)
                fw.dma("pool", vss.ap()[l, h, r0_:r0_ + 256, :], d_cv.ap()[l, h, r0_:r0_ + 256, :], writes=[bx_])

    segp.init_state = init_prompt
    segs.init_state = init_sample

    for sg_ in (segs, segp):
        for it_ in range(sg_.T // sg_.TT):
            for l_ in range(2):
                plan_tile_layer(l_)
    if "s" in DBG["segs"]:
        run_segment(segs)
    if "p" in DBG["segs"]:
        run_segment(segp)

    fw.finish([B("outs")])
    fw.emit()
    build_nc.ninst = fw.ninst


_NC_CACHE = {}


def kernel(x_prompt, x_sample, cache_k, cache_v, state_gla, state_conv, t5_table,
           w_in, w_gate_up, b_gate_up, gla_norm, lam_params, diff_norm, w_out,
           ln_mix_pre, ln_mix_post, ln_ffn_pre, ln_ffn_post,
           w_ffn_up, conv_w, conv_b, w_ffn_down):
    f = lambda a: np.ascontiguousarray(np.asarray(a, dtype=np.float32))
    x_prompt, x_sample = f(x_prompt), f(x_sample)
    BATCH, TP = x_prompt.shape[0], x_prompt.shape[1]
    NS, TS = x_sample.shape[0], x_sample.shape[1]
    PS = cache_k.shape[3]
    key = (TP, TS, PS)
    if key not in _NC_CACHE:
        _NC_CACHE[key] = build_nc(TP=TP, TS=TS, PS=PS)
    nc = _NC_CACHE[key]
    shared = {
        "t5": f(t5_table), "w_in": f(w_in), "w_gate_up": f(w_gate_up), "b_gate_up": f(b_gate_up),
        "gla_norm": f(gla_norm), "lam_params": f(lam_params), "diff_norm": f(diff_norm), "w_out": f(w_out),
        "ln_mix_pre": f(ln_mix_pre), "ln_mix_post": f(ln_mix_post), "ln_ffn_pre": f(ln_ffn_pre),
        "ln_ffn_post": f(ln_ffn_post), "w_ffn_up": f(w_ffn_up), "conv_w": f(conv_w), "conv_b": f(conv_b),
        "w_ffn_down": f(w_ffn_down), "consts": _consts(),
    }
    cache_k, cache_v, state_gla, state_conv = f(cache_k), f(cache_v), f(state_gla), f(state_conv)
    in_maps = []
    for c in range(NCORES):
        bp = c % BATCH
        bs = c % NS
        m = dict(shared)
        m["xp"] = x_prompt[bp]
        m["xs"] = x_sample[bs]
        m["ck"] = np.ascontiguousarray(cache_k[:, bs]); m["cv"] = np.ascontiguousarray(cache_v[:, bs])
        m["sg"] = np.ascontiguousarray(state_gla[:, bs]); m["sc"] = np.ascontiguousarray(state_conv[:, bs])
        in_maps.append(m)
    res = run_bass_kernel_spmd(nc, in_maps, core_ids=list(range(NCORES)))
    R = res.results
    y_prompt = np.stack([R[b]["yp"] for b in range(BATCH)])
    y_sample = np.stack([R[b]["ys"] for b in range(NS)])
    stk = lambda name, n: np.stack([R[b][name] for b in range(n)], axis=1)
    return (y_prompt, y_sample, stk("kp", BATCH), stk("vp", BATCH), stk("gp", BATCH), stk("cp", BATCH),
            stk("ks", NS), stk("vs", NS), stk("gs", NS), stk("cs", NS))
```

```python
import math
from contextlib import ExitStack

import numpy as np
import concourse.bass as bass
import concourse.mybir as mybir
from concourse.bass_utils import run_bass_kernel_spmd

F32 = mybir.dt.float32
BF16 = mybir.dt.bfloat16
AF = mybir.ActivationFunctionType
ALU = mybir.AluOpType

D = 1024
NCORES = 8
DPROJ = 3088
F2 = 5632
DFF = 2816
EPS = 1e-6


class Buf:
    __slots__ = ("name", "w", "r", "excl")

    def __init__(self, name, excl=False):
        self.name = name
        self.w = None
        self.r = {}
        self.excl = excl


class Eng:
    def __init__(self, name, be, sem):
        self.name = name
        self.be = be
        self.sem = sem
        self.count = 0
        self.pending = False
        self.seen = {}
        self.ops = []
        self.dma_n = 0


class FW:
    def __init__(self, nc, stack, n_dma_sems=12):
        self.nc = nc
        self.sems = {}
        self.E = {}
        for name, be in (("pe", nc.tensor), ("act", nc.scalar), ("dve", nc.vector),
                         ("pool", nc.gpsimd), ("sp", nc.sync)):
            self.sems["s_" + name] = stack.enter_context(nc.semaphore("s_" + name))
            self.E[name] = Eng(name, be, "s_" + name)
        self.dma_sems = {}
        for q in ("sp", "pool", "act"):
            lst = []
            for i in range(n_dma_sems):
                k = "d_%s%d" % (q, i)
                self.sems[k] = stack.enter_context(nc.semaphore(k))
                lst.append(k)
            self.dma_sems[q] = lst
        self.nd = n_dma_sems
        self.semowner = {e.sem: e for e in self.E.values()}
        self.ninst = 0

    def _deps(self, eng, reads, writes):
        need = {}
        for b in reads:
            if b.w is not None:
                k, v = b.w
                if need.get(k, 0) < v:
                    need[k] = v
            if b.excl:
                for k, v in b.r.items():
                    if k != eng.sem and need.get(k, 0) < v:
                        need[k] = v
        for b in writes:
            if b.w is not None:
                k, v = b.w
                if need.get(k, 0) < v:
                    need[k] = v
            for k, v in b.r.items():
                if need.get(k, 0) < v:
                    need[k] = v
        out = []
        for k, v in need.items():
            if k == eng.sem and eng.name == "pe":
                continue
            if eng.seen.get(k, 0) >= v:
                continue
            ow = self.semowner.get(k)
            if ow is not None:
                assert v <= ow.count, ("pending tick", eng.name, k, v, ow.count)
            eng.seen[k] = v
            out.append((k, v))
        return out

    def _mark(self, tick, reads, writes):
        k, v = tick
        for b in reads:
            if b.r.get(k, 0) < v:
                b.r[k] = v
        for b in writes:
            b.w = tick
            b.r = {}

    def op(self, en, fn, reads=(), writes=(), inc=True):
        eng = self.E[en]
        waits = self._deps(eng, reads, writes)
        if inc:
            eng.count += 1
            eng.pending = False
            tick = (eng.sem, eng.count)
        else:
            eng.pending = True
            tick = (eng.sem, eng.count + 1)
        eng.ops.append((waits, fn, eng.sem if inc else None, 1))
        self._mark(tick, reads, writes)
        self.ninst += 1

    def dma(self, q, out, in_, reads=(), writes=()):
        eng = self.E[q]
        n = eng.dma_n
        eng.dma_n += 1
        sk = self.dma_sems[q][n % self.nd]
        gen = n // self.nd
        waits = self._deps(eng, reads, writes)
        if gen > 0 and eng.seen.get(sk, 0) < 16 * gen:
            eng.seen[sk] = 16 * gen
            waits.append((sk, 16 * gen))
        tick = (sk, 16 * (gen + 1))
        eng.ops.append((waits, (lambda be: be.dma_start(out=out, in_=in_)), sk, 16))
        self._mark(tick, reads, writes)
        self.ninst += 1

    def finish(self, final_bufs):
        sp = self.E["sp"]
        waits = self._deps(sp, list(final_bufs), [])
        for q, lst in self.dma_sems.items():
            n = self.E[q].dma_n
            for i, sk in enumerate(lst):
                cnt = (n - i + self.nd - 1) // self.nd if n > i else 0
                if cnt > 0 and sp.seen.get(sk, 0) < 16 * cnt:
                    sp.seen[sk] = 16 * cnt
                    waits.append((sk, 16 * cnt))
        sp.ops.append((waits, None, None, 0))
        for e in self.E.values():
            assert not e.pending, e.name

    def emit(self):
        nc = self.nc
        sems = self.sems
        with nc.Block() as block:
            def run(eng):
                def body(be):
                    for waits, fn, sk, inc in eng.ops:
                        for k, v in waits:
                            be.wait_ge(sems[k], v)
                        if fn is None:
                            continue
                        ins = fn(be)
                        if sk is not None:
                            ins.then_inc(sems[sk], inc)
                return body
            block.tensor(run(self.E["pe"]))
            block.scalar(run(self.E["act"]))
            block.vector(run(self.E["dve"]))
            block.gpsimd(run(self.E["pool"]))
            block.sync(run(self.E["sp"]))


def _bucket_of(rel):
    rel = np.asarray(rel)
    n = np.abs(rel)
    nf = np.maximum(n, 1).astype(np.float32)
    large = 8 + (np.log(nf / np.float32(8)) / np.float32(math.log(16.0)) * np.float32(8)).astype(np.int32)
    large = np.minimum(large, 15)
    return (rel > 0).astype(np.int32) * 16 + np.where(n < 8, n, large)


C_ID, C_TRI, C_SCAN, C_OHR, C_ONES, C_J, NCW = 0, 128, 192, 1216, 1728, 1856, 1984


def _consts():
    c = np.zeros((128, NCW), np.float32)
    c[:, C_ID:C_ID + 128] = np.eye(128, dtype=np.float32)
    j = np.arange(64)[:, None]
    i = np.arange(64)[None, :]
    c[0:64, C_TRI:C_TRI + 64] = (j <= i).astype(np.float32)
    m = np.ones(1024, np.float32)
    m[::64] = 0.0
    c[0:64, C_SCAN:C_SCAN + 1024] = m[None, :]
    idx = np.arange(512)
    b = _bucket_of(127 - idx)
    oh = np.zeros((32, 512), np.float32)
    oh[b, idx] = 1.0
    c[0:32, C_OHR:C_OHR + 512] = oh
    c[:, C_ONES:C_ONES + 128] = 1.0
    c[:, C_J:C_J + 128] = np.eye(128, dtype=np.float32)[::-1]
    return c


def lam_init(l):
    return 0.8 - 0.6 * math.exp(-0.3 * l)


class Seg:
    pass


DBG = {"segs": "sp", "stop": 99}


def build_nc(TP=4096, TS=64, PS=1024, TTP=256):
    nc = bass.Bass("TRN2", target_bir_lowering=False)
    st = ExitStack()
    with st:
        _build(nc, st, TP, TS, PS, TTP)
    return nc


def _build(nc, st, TP, TS, PS, TTP):
    fw = FW(nc, st)
    TTM = max(TTP, TS)

    def din(n, s):
        return nc.dram_tensor(n, list(s), F32, kind="ExternalInput")

    def dout(n, s):
        return nc.dram_tensor(n, list(s), F32, kind="ExternalOutput")

    d_xp = din("xp", (TP, D)); d_xs = din("xs", (TS, D))
    d_ck = din("ck", (2, 4, PS, 128)); d_cv = din("cv", (2, 4, PS, 128))
    d_sg = din("sg", (2, 4, 64, 128)); d_sc = din("sc", (2, 2, F2))
    d_t5 = din("t5", (32, 4))
    d_win = din("w_in", (2, D, DPROJ)); d_wgu = din("w_gate_up", (2, 16, 256)); d_bgu = din("b_gate_up", (2, 256))
    d_gn = din("gla_norm", (2, 128)); d_lp = din("lam_params", (2, 4, 64)); d_dn = din("diff_norm", (2, 128))
    d_wout = din("w_out", (2, D, D))
    d_ln = [din(n, (2, D)) for n in ("ln_mix_pre", "ln_mix_post", "ln_ffn_pre", "ln_ffn_post")]
    d_wup = din("w_ffn_up", (2, D, F2)); d_cw = din("conv_w", (2, 3, F2)); d_cb = din("conv_b", (2, F2))
    d_wdn = din("w_ffn_down", (2, DFF, D))
    d_consts = din("consts", (128, NCW))

    o_yp = dout("yp", (TP, D)); o_ys = dout("ys", (TS, D))
    o_kp = dout("kp", (2, 4, TP, 128)); o_vp = dout("vp", (2, 4, TP, 128))
    o_gp = dout("gp", (2, 4, 64, 128)); o_cp = dout("cp", (2, 2, F2))
    o_ks = dout("ks", (2, 4, TS, 128)); o_vs = dout("vs", (2, 4, TS, 128))
    o_gs = dout("gs", (2, 4, 64, 128)); o_cs = dout("cs", (2, 2, F2))

    def dscr(n, s, dt):
        return nc.dram_tensor(n, list(s), dt)

    wb_in = dscr("wb_in", (2, D, DPROJ), BF16); wb_out = dscr("wb_out", (2, D, D), BF16)
    wb_up = dscr("wb_up", (2, D, F2), BF16); wb_dn = dscr("wb_dn", (2, DFF, D), BF16)
    ktp = dscr("ktp", (2, 4, 128, TP), BF16); vsp = dscr("vsp", (2, 4, TP, 128), BF16)
    kts = dscr("kts", (2, 4, 128, PS + TS), BF16); vss = dscr("vss", (2, 4, PS + TS, 128), BF16)
    f3d = dscr("f3d", (4, 512), F32)

    def sb(n, s, dt=F32):
        return st.enter_context(nc.sbuf_tensor("sb_" + n, list(s), dt))

    bufs = {}

    def B(name):
        if name not in bufs:
            bufs[name] = Buf(name)
        return bufs[name]

    consts = sb("consts", (128, NCW))
    identb = sb("identb", (128, 128), BF16); onesb = sb("onesb", (128, 128), BF16)
    lnv = sb("lnv", (128, 64))
    convw = sb("convw", (128, 264))
    convb = sb("convb", (128, 88))
    glan = sb("glan", (128, 2)); difn = sb("difn", (128, 2))
    nbg = sb("nbg", (64, 8))
    lpt = sb("lpt", (64, 8)); lpp = sb("lpp", (64, 4)); lame = sb("lame", (128, 4)); nlam = sb("nlam", (128, 2))
    wgu = sb("wgu", (16, 512))
    cbias = sb("cbias", (128, 4)); t5sb = sb("t5sb", (32, 4)); f3sb = sb("f3sb", (4, 512))
    trr = sb("trr", (128, 384)); strip = sb("strip", (128, 4 * 384))
    vstage = sb("vstage", (128, 128))

    xin = sb("xin", (128, 2 * 1024)); xT = sb("xT", (128, 8 * TTM))
    hT = sb("hT", (128, 8 * TTM), BF16); mixT = sb("mixT", (128, 8 * TTM), BF16)
    mo = sb("mo", (128, 8 * TTM)); sqb = [sb("sqb%d" % i, (128, TTM), BF16) for i in range(2)]
    rs = sb("rs", (128, TTM)); lnt = sb("lnt", (128, TTM))
    NSLAB = 3
    wslab = [sb("wslab%d" % i, (128, 5632), BF16) for i in range(NSLAB)]
    gdT = sb("gdT", (16, TTM))
    G = sb("G", (64, 4 * TTM)); Gc = sb("Gc", (64, 4 * TTM)); tA = sb("tA", (64, 4 * TTM)); tB = sb("tB", (64, 4 * TTM))
    qdec = sb("qdec", (64, 4 * TTM), BF16); kinv = sb("kinv", (64, 4 * TTM), BF16); kend = sb("kend", (64, 4 * TTM), BF16)
    NCH = TTM // 64
    dec = sb("dec", (64, 4 * NCH))
    kendtok = sb("kendtok", (64, NCH * 256), BF16)
    vgtok = sb("vgtok", (64, NCH * 512), BF16)
    ATsb = sb("ATsb", (64, NCH * 256), BF16)
    S = [sb("S%d" % l, (64, 512)) for l in range(2)]
    Sb = [sb("Sb%d" % l, (64, 512), BF16) for l in range(2)]
    rgs = sb("rgs", (128, 4 * TTM))
    sq4 = sb("sq4", (128, 4 * TTM), BF16)
    qpad = sb("qpad", (128, 8 * TTM), BF16); ktst = sb("ktst", (128, 4 * TTM), BF16)
    vst = sb("vst", (128, 2 * 512), BF16); kvo = sb("kvo", (128, 2 * 2 * 512))
    NG = 3
    KTg = [sb("KTg%d" % i, (128, 1024), BF16) for i in range(NG)]
    Vg = [sb("Vg%d" % i, (128, 1024), BF16) for i in range(NG)]
    PT = [sb("PT%d" % i, (128, 2 * TTM), BF16) for i in range(3)]
    PTr = [sb("PTr%d" % i, (128, 2 * TTM), BF16) for i in range(3)]
    SBK = [0, 4, 1]
    t01 = sb("t01", (128, 2 * TTM)); odr = sb("odr", (128, TTM))
    r01 = sb("r01", (128, 2 * TTM))
    NE = 4
    extg = [sb("extg%d" % i, (128, TTM + 2)) for i in range(NE)]
    extu = [sb("extu%d" % i, (128, TTM + 2)) for i in range(NE)]
    NC_ = 5
    cg = [sb("cg%d" % i, (128, TTM)) for i in range(NC_)]; cu = [sb("cu%d" % i, (128, TTM)) for i in range(NC_)]
    f1 = [sb("f1%d" % i, (128, TTM)) for i in range(2)]; f2 = [sb("f2%d" % i, (128, TTM)) for i in range(2)]
    yT = sb("yT", (128, 22 * TTM), BF16)
    hist = [sb("hist%d" % l, (128, 88)) for l in range(2)]
    ckst = kvo[:, 0:1024]; ktc = vst[:, 0:1024]
    og4 = mo[:, 0:4 * TTM]; rs4 = xin[:, 0:4 * TTM]

    ps01 = st.enter_context(nc.psum_tensor("ps01", [128, 1024], F32))
    pb23 = [st.enter_context(nc.psum_tensor("pb%d" % i, [128, 512], F32)) for i in (2, 3)]
    ps45 = st.enter_context(nc.psum_tensor("ps45", [128, 1024], F32))
    pb67 = [st.enter_context(nc.psum_tensor("pb%d" % i, [128, 512], F32)) for i in (6, 7)]
    pbank = [ps01[:, 0:512], ps01[:, 512:1024], pb23[0], pb23[1], ps45[:, 0:512], ps45[:, 512:1024], pb67[0], pb67[1]]
    psS = [ps01, ps45]
    for i in range(8):
        bufs["pb%d" % i] = Buf("pb%d" % i, excl=True)
    pbuf = [[B("pb%d" % i)] * 2 for i in range(8)]

    def PB(i):
        return [pbuf[i][0]]

    def mm(out, lhsT, rhs, start, stop, reads, writes, inc=None):
        fw.op("pe", lambda e: e.matmul(out, lhsT=lhsT, rhs=rhs, start=start, stop=stop),
              reads, writes, inc=(stop if inc is None else inc))

    def tr(out, in_, ident, reads, writes, inc=True):
        fw.op("pe", lambda e: e.transpose(out=out, in_=in_, identity=ident), reads, writes, inc=inc)

    def act(out, in_, func, reads, writes, scale=None, bias=None):
        kw = {}
        if scale is not None:
            kw["scale"] = scale
        if bias is not None:
            kw["bias"] = bias
        fw.op("act", lambda e: e.activation(out=out, in_=in_, func=func, **kw), reads, writes)

    def tt(en, out, in0, in1, op, reads, writes):
        fw.op(en, lambda e: e.tensor_tensor(out=out, in0=in0, in1=in1, op=op), reads, writes)

    def ts(en, out, in0, s1, s2, op0, op1, reads, writes):
        if op1 is None:
            fw.op(en, lambda e: e.tensor_scalar(out=out, in0=in0, scalar1=s1, scalar2=None, op0=op0), reads, writes)
        else:
            fw.op(en, lambda e: e.tensor_scalar(out=out, in0=in0, scalar1=s1, scalar2=s2, op0=op0, op1=op1), reads, writes)

    def stt(out, in0, scalar, in1, op0, op1, reads, writes):
        fw.op("dve", lambda e: e.scalar_tensor_tensor(out=out, in0=in0, scalar=scalar, in1=in1, op0=op0, op1=op1),
              reads, writes)

    def cp(en, out, in_, reads, writes):
        fw.op(en, lambda e: e.tensor_copy(out=out, in_=in_), reads, writes)

    def v3(ap, a):
        return ap.rearrange("p (a b) -> p a b", a=a)

    ident = consts[:, C_ID:C_ID + 128]
    onesf = consts[:, C_ONES:C_ONES + 128]
    Bc = B("consts")

    fw.dma("sp", consts[:], d_consts.ap(), writes=[Bc])
    cp("dve", identb[:], ident, [Bc], [B("identb")])
    cp("dve", onesb[:], onesf, [Bc], [B("onesb")])
    Bi, Bo = B("identb"), B("onesb")

    conv_order = [("in", d_win, wb_in, D), ("out", d_wout, wb_out, D), ("up", d_wup, wb_up, D), ("dn", d_wdn, wb_dn, DFF)]
    pool_wait_layer = {}
    for l in range(2):
        for name, src, dst, rows in conv_order:
            r0 = 0
            while r0 < rows:
                nr = min(256, rows - r0)
                fw.dma("pool", dst.ap()[l, r0:r0 + nr, :], src.ap()[l, r0:r0 + nr, :], writes=[])
                r0 += nr
            pw = []
            for i, sk in enumerate(fw.dma_sems["pool"]):
                n = fw.E["pool"].dma_n
                cnt = (n - i + fw.nd - 1) // fw.nd if n > i else 0
                if cnt > 0:
                    pw.append((sk, 16 * cnt))
            pool_wait_layer[(name, l)] = pw
    weights_waited = set()

    def wait_weights(l):
        if l in weights_waited:
            return
        weights_waited.add(l)
        sp = fw.E["sp"]
        w0 = []
        for k, v in pool_wait_layer[l]:
            if sp.seen.get(k, 0) < v:
                sp.seen[k] = v
                w0.append((k, v))
        sp.ops.append((w0, None, None, 0))

    def load_cols(src_ap, R, W, dst_ap, dstbuf, post=None):
        fw.dma("sp", vstage[0:R, 0:W], src_ap, writes=[B("vstage")])
        tr(pbank[7][0:W, 0:R], vstage[0:R, 0:W], consts[0:R, C_ID:C_ID + R], [B("vstage"), Bc], PB(7))
        if post is None:
            cp("dve", dst_ap, pbank[7][0:W, 0:R], PB(7), dstbuf if isinstance(dstbuf, list) else [dstbuf])
        else:
            post(pbank[7][0:W, 0:R])

    for kind in range(4):
        load_cols(d_ln[kind].ap().rearrange("l (c p) -> (l c) p", p=128), 16, 128,
                  lnv[:, kind * 16:(kind + 1) * 16], B("lnv"))
    for l in range(2):
        for tp in range(3):
            load_cols(d_cw.ap()[l, tp, :].rearrange("(c p) -> c p", p=128), 44, 128,
                      convw[:, (l * 3 + tp) * 44:(l * 3 + tp + 1) * 44], B("convw"))
        load_cols(d_cb.ap()[l, :].rearrange("(c p) -> c p", p=128), 44, 128, convb[:, l * 44:(l + 1) * 44], B("convb"))
    load_cols(d_gn.ap(), 2, 128, glan[:], B("glan"))
    load_cols(d_dn.ap(), 2, 128, difn[:], B("difn"))
    for l in range(2):
        ts("dve", difn[:, l:l + 1], difn[:, l:l + 1], 1.0 - lam_init(l), None, ALU.mult, None, [B("difn")], [B("difn")])
    load_cols(d_bgu.ap().rearrange("l (h k) -> (l h) k", k=64), 8, 64, nbg[:], B("nbg"),
              post=lambda ps: ts("dve", nbg[:], ps, -1.0, None, ALU.mult, None, PB(7), [B("nbg")]))
    load_cols(d_lp.ap().rearrange("l i k -> (l i) k"), 8, 64, lpt[:], B("lpt"))
    fw.dma("sp", v3(wgu[:], 2), d_wgu.ap().rearrange("l r c -> r l c"), writes=[B("wgu")])
    lp3 = v3(lpt[:], 4)
    tt("dve", lpp[:], lp3[:, :, 0], lp3[:, :, 1], ALU.mult, [B("lpt")], [B("lpp")])
    mm(pbank[7][:, 0:4], consts[0:64, C_ONES:C_ONES + 128], lpp[:], True, True, [Bc, B("lpp")], PB(7))
    act(lame[:], pbank[7][:, 0:4], AF.Exp, PB(7), [B("lame")])
    for l in range(2):
        tt("dve", nlam[:, l:l + 1], lame[:, 2 * l + 1:2 * l + 2], lame[:, 2 * l:2 * l + 1], ALU.subtract, [B("lame")], [B("nlam")])
        ts("dve", nlam[:, l:l + 1], nlam[:, l:l + 1], -lam_init(l), None, ALU.add, None, [B("nlam")], [B("nlam")])
    fw.dma("sp", t5sb[:], d_t5.ap(), writes=[B("t5sb")])
    fw.dma("sp", cbias[0:32, :], bass.AP(d_t5, 15 * 4, [[0, 32], [1, 4]]), writes=[B("cbias")])
    tt("dve", t5sb[:], t5sb[:], cbias[0:32, :], ALU.subtract, [B("t5sb"), B("cbias")], [B("t5sb")])
    mm(pbank[7][0:4, 0:512], t5sb[:], consts[0:32, C_OHR:C_OHR + 512], True, True, [B("t5sb"), Bc], PB(7))
    act(f3sb[:], pbank[7][0:4, 0:512], AF.Exp, PB(7), [B("f3sb")])
    fw.dma("pool", f3d.ap(), f3sb[:], reads=[B("f3sb")], writes=[B("f3d")])
    for h in range(4):
        fw.dma("sp", trr[:], bass.AP(f3d, h * 512, [[1, 128], [1, 384]]), reads=[B("f3d")], writes=[B("trr")])
        mm(pbank[7][:, 0:384], consts[:, C_J:C_J + 128], trr[:], True, True, [Bc, B("trr")], PB(7))
        cp("dve", strip[:, h * 384:(h + 1) * 384], pbank[7][:, 0:384], PB(7), [B("strip")])
        fw.op("dve", (lambda hh: (lambda e: e.memset(strip[64:128, hh * 384:hh * 384 + 64], 0.0)))(h), [], [B("strip")])

    slab_seq = []

    class SlabStream:
        def __init__(self):
            self.plan = []
            self.loaded = 0
            self.used = 0

        def add(self, pieces, wkey):
            self.plan.append((pieces, wkey))

        def ensure(self, upto):
            while self.loaded < min(upto, len(self.plan)):
                i = self.loaded
                pieces, wkey = self.plan[i]
                wait_weights(wkey)
                sl = wslab[i % NSLAB]
                for col0, c, n, src in pieces:
                    dst = sl[:, col0:col0 + c * n].rearrange("p (c n) -> p c n", c=c)
                    fw.dma("sp", dst, src, reads=[], writes=[B("wslab%d" % (i % NSLAB))])
                self.loaded += 1

        def get(self):
            i = self.used
            self.ensure(i + NSLAB)
            self.used += 1
            return wslab[i % NSLAB], B("wslab%d" % (i % NSLAB))

    ss = SlabStream()

    def w_pieces_in(l, c0, n):
        return [(0, 8, n, wb_in.ap()[l, :, c0:c0 + n].rearrange("(c p) n -> p c n", p=128))]

    IN_SLABS = [(1536, 16), (512, 512), (0, 512), (1024, 512), (1552, 512), (2064, 512), (2576, 512)]

    def plan_tile_layer(l):
        for c0, n in IN_SLABS:
            ss.add(w_pieces_in(l, c0, n), ("in", l))
        for c0 in (0, 512):
            ss.add([(0, 8, 512, wb_out.ap()[l, :, c0:c0 + 512].rearrange("(c p) n -> p c n", p=128))], ("out", l))
        for s in range(11):
            ss.add([(0, 8, 256, wb_up.ap()[l, :, s * 256:(s + 1) * 256].rearrange("(c p) n -> p c n", p=128)),
                    (2048, 8, 256, wb_up.ap()[l, :, DFF + s * 256:DFF + (s + 1) * 256].rearrange("(c p) n -> p c n", p=128))],
                   ("up", l))
        for s in range(4):
            ss.add([(0, 22, 256, wb_dn.ap()[l, :, s * 256:(s + 1) * 256].rearrange("(c p) n -> p c n", p=128))], ("dn", l))

    rot = {"sq": 0, "nt": 0, "acc": 0}
    ACC = [(0, 0), (1, 0), (4, 0), (5, 0)]

    def acc_next():
        b, hf = ACC[rot["acc"] % 4]
        rot["acc"] += 1
        return b, hf

    def pslice(b, hf, p, n):
        return pbank[b][0:p, hf * 256:hf * 256 + n]

    def rms_stats(chunks, nparts_feat, TT, tagbufs, bank=6):
        n = len(chunks)
        for i, (ap, rb) in enumerate(chunks):
            q = sqb[rot["sq"] % 2]
            qb = B("sqb%d" % (rot["sq"] % 2))
            rot["sq"] += 1
            act(q[:, 0:TT], ap, AF.Square, rb, [qb])
            mm(pbank[bank][:, 0:TT], onesb[:], q[:, 0:TT], i == 0, i == n - 1, [Bo, qb], PB(bank), inc=True)
        act(lnt[:, 0:TT], pbank[bank][:, 0:TT], AF.Ln, PB(bank), [B("lnt")], scale=1.0 / nparts_feat, bias=EPS)
        act(rs[:, 0:TT], lnt[:, 0:TT], AF.Exp, [B("lnt")], [B("rs")], scale=-0.5)

    def x3(t, TT):
        return t[:, 0:8 * TT].rearrange("p (c t) -> p c t", c=8)

    def run_segment(sg):
        T, TT, P = sg.T, sg.TT, sg.P
        ntile = T // TT
        nblk = max(1, TT // 128)
        bt = min(128, TT)
        nch = TT // 64
        xT3 = x3(xT, TT); hT3 = x3(hT, TT); mix3 = x3(mixT, TT); mo3 = x3(mo, TT)
        G3 = v3(G[:, 0:4 * TT], 4); Gc3 = v3(Gc[:, 0:4 * TT], 4)
        qd3 = v3(qdec[:, 0:4 * TT], 4); ki3 = v3(kinv[:, 0:4 * TT], 4); ke3 = v3(kend[:, 0:4 * TT], 4)
        rgs3 = v3(rgs[:, 0:4 * TT], 4); ktst3 = v3(ktst[:, 0:4 * TT], 4)
        qdT3 = qpad[:, 0:8 * TT].rearrange("p (m h t) -> p m h t", m=2, h=4)
        fw.op("pool", lambda e: e.memset(qpad[:], 0.0), [], [B("qdT")])
        yT3 = yT[:, 0:22 * TT].rearrange("p (c t) -> p c t", c=22)

        for l in range(2):
            sg.init_state(l)

        for it in range(ntile):
            tok0 = it * TT
            def load_x(it_):
                if it_ >= ntile:
                    return
                for bl_ in range(nblk):
                    fw.dma("sp", xin[0:bt, bl_ * 1024:(bl_ + 1) * 1024],
                           sg.x.ap()[it_ * TT + bl_ * bt:it_ * TT + (bl_ + 1) * bt, :], writes=[B("xin")])
            if it == 0:
                load_x(0)
            sg.prefetch = (lambda it_=it: load_x(it_ + 1))
            for bl in range(nblk):
                for c0 in (0, 4):
                    bk = 2 + (c0 // 4)
                    for c in range(c0, c0 + 4):
                        tr(pbank[bk][:, (c - c0) * 128:(c - c0) * 128 + bt], xin[0:bt, bl * 1024 + c * 128:bl * 1024 + (c + 1) * 128],
                           consts[0:bt, C_ID:C_ID + bt], [B("xin"), Bc], PB(bk), inc=(c == c0 + 3))
                    fw.op("act", (lambda c0_, bl_, bk_: (lambda e: e.activation(
                        out=xT3[:, c0_:c0_ + 4, bl_ * bt:(bl_ + 1) * bt],
                        in_=pbank[bk_][:, 0:512].rearrange("p (c t) -> p c t", c=4)[:, :, 0:bt], func=AF.Copy)))(c0, bl, bk),
                        PB(bk), [B("xT")])

            for l in range(2):
                layer_tile(sg, l, it, T, TT, P, nblk, bt, nch, xT3, hT3, mix3, mo3, G3, Gc3, qd3, ki3, ke3, rgs3, qdT3,
                           ktst3, yT3)

            for bl in range(nblk):
                for c0 in (0, 4):
                    bk = 2 + (c0 // 4)
                    for c in range(c0, c0 + 4):
                        tr(pbank[bk][0:bt, (c - c0) * 128:(c - c0 + 1) * 128], xT3[:, c, bl * bt:(bl + 1) * bt],
                           ident, [B("xT"), Bc], PB(bk), inc=(c == c0 + 3))
                    act(kvo[0:bt, bl * 1024 + c0 * 128:bl * 1024 + (c0 + 4) * 128], pbank[bk][0:bt, 0:512], AF.Copy,
                        PB(bk), [B("kvoK" if bl == 0 else "kvoV")])
            for bl in range(nblk):
                fw.dma("pool", sg.y.ap()[tok0 + bl * bt:tok0 + (bl + 1) * bt, :], kvo[0:bt, bl * 1024:(bl + 1) * 1024],
                       reads=[B("kvoK" if bl == 0 else "kvoV")], writes=[])

        for l in range(2):
            fw.dma("pool", sg.gout.ap()[l].rearrange("h k v -> k h v"), v3(S[l][:], 4), reads=[B("S%d" % l)], writes=[])
            tr(pbank[7][0:88, 0:128], hist[l][:], ident, [B("hist%d_%d" % (l, jj)) for jj in range(44)] + [Bc], PB(7))
            cp("dve", vstage[0:88, :], pbank[7][0:88, 0:128], PB(7), [B("vstage")])
            fw.dma("pool", sg.cout.ap()[l].rearrange("r (c p) -> (r c) p", p=128), vstage[0:88, :],
                   reads=[B("vstage")], writes=[])

    def layer_tile(sg, l, it, T, TT, P, nblk, bt, nch, xT3, hT3, mix3, mo3, G3, Gc3, qd3, ki3, ke3, rgs3, qdT3,
                   ktst3, yT3):
        tok0 = it * TT
        Bx, Bh, Bmix, Bmo = B("xT"), B("hT"), B("mixT"), B("mo")
        BS, BSb = B("S%d" % l), B("Sb%d" % l)

        def lnp(kind, c):
            i = kind * 16 + l * 8 + c
            return lnv[:, i:i + 1]

        if l == 1:
            act(lnt[:, 0:TT], pbank[7][:, 0:TT], AF.Ln, PB(7), [B("lnt")], scale=1.0 / D, bias=EPS)
            act(rs[:, 0:TT], lnt[:, 0:TT], AF.Exp, [B("lnt")], [B("rs")], scale=-0.5)
        else:
            rms_stats([(xT3[:, c, :], [Bx]) for c in range(8)], D, TT, None)
        for c in range(8):
            stt(hT3[:, c, :], xT3[:, c, :], lnp(0, c), rs[:, 0:TT], ALU.mult, ALU.mult, [Bx, B("lnv"), B("rs")], [Bh])

        if DBG["stop"] <= 1:
            return
        def proj_fm(slab, sbuf_, col0, M, kparts=128):
            b, hf = acc_next()
            sl3 = slab[:, 0:8 * slab_n[0]].rearrange("p (c n) -> p c n", c=8)
            for kc in range(8):
                mm(pslice(b, hf, M, TT), sl3[:, kc, col0:col0 + M], hT3[:, kc, :], kc == 0, kc == 7,
                   [sbuf_, Bh], [pbuf[b][hf]])
            return b, hf

        slab_n = [16]
        slab3, sb3 = ss.get()
        b, hf = proj_fm(slab3, sb3, 0, 16)
        act(gdT[:, 0:TT], pslice(b, hf, 16, TT), AF.Copy, [pbuf[b][hf]], [B("gdT")])
        for h in range(4):
            bk, hf = 6 + h // 2, h % 2
            mm(pbank[bk][0:64, hf * 256:hf * 256 + TT], wgu[:, l * 256 + h * 64:l * 256 + (h + 1) * 64], gdT[:, 0:TT],
               True, True, [B("wgu"), B("gdT")], [pbuf[bk][hf]])
        for h in range(4):
            bk, hf = 6 + h // 2, h % 2
            act(G3[:, h, :], pbank[bk][0:64, hf * 256:hf * 256 + TT], AF.Exp, [pbuf[bk][hf], B("nbg")], [B("G")],
                scale=-1.0, bias=nbg[:, l * 4 + h:l * 4 + h + 1])
        act(G[:, 0:4 * TT], G[:, 0:4 * TT], AF.Ln, [B("G")], [B("G")], bias=1.0)
        slab1, sb1 = ss.get()
        sl3_1 = slab1[:, 0:8 * 512].rearrange("p (c n) -> p c n", c=8)
        for c in range(nch):
            bk = 6 + c % 2
            for kc in range(8):
                mm(pbank[bk][0:64, 0:512], hT3[:, kc, c * 64:(c + 1) * 64], sl3_1[:, kc, :], kc == 0, kc == 7,
                   [sb1, Bh], PB(bk))
            act(vgtok[:, c * 512:(c + 1) * 512], pbank[bk][0:64, 0:512], AF.Copy, PB(bk), [B("vgtok")])
        fw.op("dve", lambda e: e.tensor_tensor_scan(out=Gc[:, 0:4 * TT], data0=consts[0:64, C_SCAN:C_SCAN + 4 * TT],
                                                    data1=G[:, 0:4 * TT], initial=0.0, op0=ALU.mult, op1=ALU.add),
              [B("G"), Bc], [B("Gc")])
        act(tA[:, 0:4 * TT], Gc[:, 0:4 * TT], AF.Exp, [B("Gc")], [B("tA")], scale=-1.0 / 16.0)
        act(tB[:, 0:4 * TT], Gc[:, 0:4 * TT], AF.Exp, [B("Gc")], [B("tB")], scale=1.0 / 16.0)
        tA3 = v3(tA[:, 0:4 * TT], 4); tB3 = v3(tB[:, 0:4 * TT], 4); tC3 = v3(G[:, 0:4 * TT], 4)
        dec3 = dec[:, 0:4 * nch].rearrange("p (h c) -> p h c", h=4)
        act(dec3, bass.AP(Gc, 63, [[4 * TTM, 64], [TT, 4], [64, nch]]),
            AF.Exp, [B("Gc")], [B("dec")], scale=-1.0 / 16.0)
        for h in range(4):
            tt("dve", tC3[:, h, :].rearrange("p (c t) -> p c t", t=64), tB3[:, h, :].rearrange("p (c t) -> p c t", t=64),
               bass.AP(dec, h * nch, [[4 * NCH, 64], [1, nch], [0, 64]]), ALU.mult, [B("tB"), B("dec"), B("G")], [B("G")])
        if DBG["stop"] <= 1.2:
            return
        slab0, sb0 = ss.get()
        sl3_0 = slab0[:, 0:8 * 512].rearrange("p (c n) -> p c n", c=8)
        for which in range(2):
            for h in range(4):
                b, hf = acc_next()
                for kc in range(8):
                    mm(pslice(b, hf, 64, TT), sl3_0[:, kc, which * 256 + h * 64:which * 256 + (h + 1) * 64],
                       hT3[:, kc, :], kc == 0, kc == 7, [sb0, Bh], [pbuf[b][hf]])
                if which == 0:
                    stt(qd3[:, h, :], pslice(b, hf, 64, TT), 0.125, tA3[:, h, :], ALU.mult, ALU.mult,
                        [pbuf[b][hf], B("tA")], [B("qdec")])
                else:
                    tt("dve", ki3[:, h, :], pslice(b, hf, 64, TT), tB3[:, h, :], ALU.mult, [pbuf[b][hf], B("tB")], [B("kinv")])
                    tt("dve", ke3[:, h, :], pslice(b, hf, 64, TT), tC3[:, h, :], ALU.mult, [pbuf[b][hf], B("G")], [B("kend")])
        pb7b = pbank[7][:].bitcast(BF16)
        for c in range(nch):
            for h in range(4):
                tr(pb7b[0:64, (c * 4 + h) * 64:(c * 4 + h + 1) * 64], ke3[:, h, c * 64:(c + 1) * 64], identb[0:64, 0:64],
                   [B("kend"), Bi], PB(7), inc=(h == 3 and c == nch - 1))
        cp("dve", kendtok[:, 0:nch * 256], pb7b[0:64, 0:nch * 256], PB(7), [B("kendtok")])
        if DBG["stop"] <= 1.4:
            return
        kvo4 = kvo[:].rearrange("p (w b n) -> p w b n", w=2, b=2)
        vst3 = v3(vst[:], 2)
        Bkt = [B(sg.name + "kt%d_%d" % (l, h_)) for h_ in range(4)]
        Bvs = [B(sg.name + "vs%d_%d" % (l, b_)) for b_ in range(2)]
        Bvs_r = Bvs + sg.extra.get(l, [])

        def slab_rg():
            slab2, sb2 = ss.get()
            slab_n[0] = 512
            for h in range(4):
                b, hf = proj_fm(slab2, sb2, h * 128, 128)
                act(rgs3[:, h, :], pslice(b, hf, 128, TT), AF.Silu, [pbuf[b][hf]], [B("rgs")])

        def slab_qd():
            slab4, sb4 = ss.get()
            slab_n[0] = 512
            for h in range(4):
                b, hf = proj_fm(slab4, sb4, h * 128, 128)
                for m in range(2):
                    act(qdT3[m * 64:(m + 1) * 64, m, h, :], pbank[b][m * 64:(m + 1) * 64, hf * 256:hf * 256 + TT], AF.Copy,
                        [pbuf[b][hf]], [B("qdT")])

        def slab_kd():
            slab5, sb5 = ss.get()
            slab_n[0] = 512
            sl3_5 = slab5[:, 0:8 * 512].rearrange("p (c n) -> p c n", c=8)
            for h in range(4):
                b, hf = proj_fm(slab5, sb5, h * 128, 128)
                act(ktst3[:, h, :], pslice(b, hf, 128, TT), AF.Copy, [pbuf[b][hf]], [B("ktst")])
            for bl in range(nblk):
                bk, _hf = acc_next()
                for kc in range(8):
                    mm(pbank[bk][0:bt, 0:512], hT3[:, kc, bl * bt:(bl + 1) * bt], sl3_5[:, kc, :], kc == 0, kc == 7,
                       [sb5, Bh], PB(bk))
                act(kvo4[0:bt, 0, bl, :], pbank[bk][0:bt, 0:512], AF.Copy, PB(bk), [B("kvoK")])
            for h in range(4):
                fw.dma("act", sg.kt.ap()[l, h, :, P + tok0:P + tok0 + TT], ktst3[:, h, :], reads=[B("ktst")], writes=[Bkt[h]])
            for bl in range(nblk):
                fw.dma("act", sg.kout.ap()[l, :, tok0 + bl * bt:tok0 + (bl + 1) * bt, :].rearrange("h t d -> t h d"),
                       kvo4[0:bt, 0, bl, :].rearrange("p (h d) -> p h d", h=4), reads=[B("kvoK")], writes=[])

        def slab_vd():
            slab6, sb6 = ss.get()
            sl3_6 = slab6[:, 0:8 * 512].rearrange("p (c n) -> p c n", c=8)
            for bl in range(nblk):
                bk, _hf = acc_next()
                for kc in range(8):
                    mm(pbank[bk][0:bt, 0:512], hT3[:, kc, bl * bt:(bl + 1) * bt], sl3_6[:, kc, :], kc == 0, kc == 7,
                       [sb6, Bh], PB(bk))
                act(kvo4[0:bt, 1, bl, :], pbank[bk][0:bt, 0:512], AF.Copy, PB(bk), [B("kvoV")])
                cp("dve", vst3[0:bt, bl, :], kvo4[0:bt, 1, bl, :], [B("kvoV")], [B("vst")])
            for bl in range(nblk):
                r0 = P + tok0 + bl * bt
                fw.dma("act", sg.vsc.ap()[l, :, r0:r0 + bt, :].rearrange("h t d -> t h d"),
                       vst3[0:bt, bl, :].rearrange("p (h d) -> p h d", h=4), reads=[B("vst")], writes=[Bvs[bl % 2]])
                fw.dma("act", sg.vout.ap()[l, :, tok0 + bl * bt:tok0 + (bl + 1) * bt, :].rearrange("h t d -> t h d"),
                       kvo4[0:bt, 1, bl, :].rearrange("p (h d) -> p h d", h=4), reads=[B("kvoV")], writes=[])

        late_slabs = [slab_rg, slab_qd, slab_kd, slab_vd]
        if DBG["stop"] <= 2:
            while late_slabs:
                late_slabs.pop(0)()
            return
        S3 = v3(S[l][:], 4); Sb3 = v3(Sb[l][:], 4)
        at4 = ATsb[:, 0:nch * 256].rearrange("p (c h t) -> p c h t", c=nch, h=4)
        kt4 = kendtok[:, 0:nch * 256].rearrange("p (c h k) -> p c h k", c=nch, h=4)
        for c in range(nch):
            cs = slice(c * 64, (c + 1) * 64)
            for h in range(4):
                mm(pbank[6][0:64, h * 64:(h + 1) * 64], ki3[:, h, cs], qd3[:, h, cs], True, True,
                   [B("kinv"), B("qdec")], [pbuf[6][0]], inc=(h == 3))
            tt("dve", at4[:, c, :, :], pbank[6][0:64, 0:256].rearrange("p (h t) -> p h t", h=4),
               bass.AP(consts, C_TRI, [[NCW, 64], [0, 4], [1, 64]]), ALU.mult, [pbuf[6][0], Bc], [B("ATsb")])
            if late_slabs:
                late_slabs.pop(0)()
            for h in range(4):
                bk, hf = 2 + h // 2, h % 2
                o_ap = pbank[bk][:, hf * 256 + c * 64:hf * 256 + (c + 1) * 64]
                mm(o_ap, vgtok[:, c * 512 + h * 128:c * 512 + (h + 1) * 128], at4[:, c, h, :], True, False,
                   [B("vgtok"), B("ATsb")], [pbuf[bk][hf]])
                mm(o_ap, Sb3[:, h, :], qd3[:, h, cs], False, True, [BSb, B("qdec")], [pbuf[bk][hf]])
            for h in range(4):
                mm(pbank[7][0:64, h * 128:(h + 1) * 128], kt4[:, c, h, :], vgtok[:, c * 512 + h * 128:c * 512 + (h + 1) * 128],
                   True, True, [B("kendtok"), B("vgtok")], PB(7), inc=(h == 3))
            tt("dve", S3, S3, bass.AP(dec, c, [[4 * NCH, 64], [nch, 4], [0, 128]]), ALU.mult, [BS, B("dec")], [BS])
            tt("dve", S[l][:], S[l][:], pbank[7][0:64, 0:512], ALU.add, [BS] + PB(7), [BS])
            act(Sb[l][:], S[l][:], AF.Copy, [BS], [BSb])
        while late_slabs:
            late_slabs.pop(0)()
        og3 = v3(og4[:, 0:4 * TT], 4)
        for hp in range(2):
            src = pbank[2 + hp][:, 0:512].rearrange("p (h t) -> p h t", h=2)[:, :, 0:TT]
            if hp == 0:
                act(og3[:, 0:2, :], src, AF.Copy, PB(2), [B("mo")])
            else:
                cp("dve", og3[:, 2:4, :], src, PB(3), [B("mo")])

        def gla_epilogue():
            act(sq4[:, 0:4 * TT], og4[:, 0:4 * TT], AF.Square, [B("mo")], [B("sq4")])
            sq3 = v3(sq4[:, 0:4 * TT], 4)
            banks = []
            for hp in range(2):
                bk = (3, 5)[hp]
                banks.append(bk)
                for hh in range(2):
                    mm(pbank[bk][:, hh * 256:hh * 256 + TT], onesb[:], sq3[:, hp * 2 + hh, :], True, True, [Bo, B("sq4")], PB(bk),
                       inc=(hh == 1))
            rs3 = v3(rs4[:, 0:4 * TT], 4)
            for hp in range(2):
                bk = banks[hp]
                act(rs3[:, hp * 2:hp * 2 + 2, :], pbank[bk][:, 0:512].rearrange("p (h t) -> p h t", h=2)[:, :, 0:TT], AF.Ln,
                    PB(bk), [B("xin")], scale=1.0 / 128, bias=EPS)
            act(rs4[:, 0:4 * TT], rs4[:, 0:4 * TT], AF.Exp, [B("xin")], [B("xin")], scale=-0.5)
            stt(og4[:, 0:4 * TT], og4[:, 0:4 * TT], glan[:, l:l + 1], rs4[:, 0:4 * TT], ALU.mult, ALU.mult,
                [B("mo"), B("glan"), B("xin")], [B("mo")])
            tt("dve", mixT[:, 0:4 * TT], og4[:, 0:4 * TT], rgs[:, 0:4 * TT], ALU.mult, [B("mo"), B("rgs")], [Bmix])

        if DBG["stop"] <= 3:
            gla_epilogue()

        if DBG["stop"] <= 3:
            return
        q0 = P + tok0
        nkeys = P + tok0 + TT
        nkb = (nkeys + 127) // 128
        ngrp = (nkb + 7) // 8
        deferred = [gla_epilogue]
        for h in range(4):
            hs = strip[:, h * 384:(h + 1) * 384]
            first = True
            pend = []
            nblk_done = 0
            for g in range(ngrp):
                gi = rot.setdefault("grp", 0)
                rot["grp"] += 1
                ktg, vg_ = KTg[gi % NG], Vg[gi % NG]
                Bkg, Bvg = B("KTg%d" % (gi % NG)), B("Vg%d" % (gi % NG))
                k0g = g * 1024
                nkg = min(1024, nkeys - k0g)
                fw.dma("sp", ktg[:, 0:nkg], sg.kt.ap()[l, h, :, k0g:k0g + nkg], reads=[Bkt[h]], writes=[Bkg])
                nfull = nkg // 128
                if nfull:
                    fw.dma("sp", vg_[:, 0:nfull * 128].rearrange("p (b d) -> p b d", d=128),
                           sg.vsc.ap()[l, h, k0g:k0g + nfull * 128, :].rearrange("(b p) d -> p b d", p=128),
                           reads=Bvs_r, writes=[Bvg])
                if nkg % 128:
                    rem = nkg % 128
                    fw.dma("sp", vg_[0:rem, nfull * 128:(nfull + 1) * 128],
                           sg.vsc.ap()[l, h, k0g + nfull * 128:k0g + nkg, :], reads=Bvs_r, writes=[Bvg])
                nb_g = (nkg + 127) // 128
                for kb in range(nb_g):
                    k0 = k0g + kb * 128
                    nk = min(128, nkeys - k0)
                    j = (k0 - q0) // 128
                    c0 = max(0, 128 * j)
                    ncol = TT - c0
                    gb = rot.setdefault("pt", 0)
                    rot["pt"] += 1
                    last = (g == ngrp - 1 and kb == nb_g - 1)
                    sbuf_i = gb % 3
                    ptb = PT[sbuf_i][:, 0:2 * TT].rearrange("p (m t) -> p m t", m=2)
                    ptr = PTr[sbuf_i][:, 0:2 * TT].rearrange("p (m t) -> p m t", m=2)
                    Bpt, Bptr = B("PT%d" % sbuf_i), B("PTr%d" % sbuf_i)
                    near = j >= -1
                    bk0 = SBK[sbuf_i]
                    s3 = pbank[bk0][0:nk, 0:2 * TT].rearrange("p (m t) -> p m t", m=2)[:, :, c0:TT]
                    if c0 == 0:
                        mm(pbank[bk0][0:nk, 0:2 * TT], ktg[:, kb * 128:kb * 128 + nk], qdT3[:, :, h, :], True, True,
                           [Bkg, B("qdT")], PB(bk0))
                    else:
                        for m in range(2):
                            mm(pbank[bk0][0:nk, m * TT + c0:(m + 1) * TT], ktg[:, kb * 128:kb * 128 + nk], qdT3[:, m, h, c0:TT],
                               True, True, [Bkg, B("qdT")], PB(bk0), inc=(m == 1))
                    if len(pend) >= 2:
                        pend.pop(0)()
                    if near:
                        act(ptr[0:nk, :, c0:TT], s3, AF.Exp, PB(bk0), [Bptr], scale=0.125)
                        tt("dve", ptb[0:nk, :, c0:TT], ptr[0:nk, :, c0:TT],
                           bass.AP(strip, h * 384 + c0 - 128 * j, [[4 * 384, nk], [0, 2], [1, TT - c0]]), ALU.mult,
                           [Bptr, B("strip")], [Bpt])
                    else:
                        act(ptb[0:nk, :, c0:TT], s3, AF.Exp, PB(bk0), [Bpt], scale=0.125)

                    def pv(ptb=ptb, Bpt=Bpt, nk=nk, c0=c0, kb=kb, vg_=vg_, Bvg=Bvg, first=first, last=last):
                        o3_ = pbank[2][:, 0:2 * TT].rearrange("p (m t) -> p m t", m=2)[:, :, c0:TT]
                        z3_ = pbank[6][:, 0:2 * TT].rearrange("p (m t) -> p m t", m=2)[:, :, c0:TT]
                        if c0 == 0:
                            mm(pbank[2][:, 0:2 * TT], vg_[0:nk, kb * 128:(kb + 1) * 128], ptb[0:nk, :, :], first, last,
                               [Bvg, Bpt], PB(2), inc=False)
                            mm(pbank[6][:, 0:2 * TT], onesb[0:nk, :], ptb[0:nk, :, :], first, last, [Bo, Bpt], PB(6), inc=True)
                        else:
                            for m in range(2):
                                mm(pbank[2][:, m * TT + c0:(m + 1) * TT], vg_[0:nk, kb * 128:(kb + 1) * 128], ptb[0:nk, m, c0:TT],
                                   first, last and m == 1, [Bvg, Bpt], PB(2), inc=False)
                            for m in range(2):
                                mm(pbank[6][:, m * TT + c0:(m + 1) * TT], onesb[0:nk, :], ptb[0:nk, m, c0:TT],
                                   first, last and m == 1, [Bo, Bpt], PB(6), inc=(m == 1))
                    pend.append(pv)
                    nblk_done += 1
                    if nblk_done == min(3, nkb) and deferred:
                        deferred.pop(0)()
                    first = False
            while pend:
                pend.pop(0)()
            act(r01[:, 0:2 * TT], pbank[6][:, 0:2 * TT], AF.Ln, PB(6), [B("r01")])
            act(r01[:, 0:2 * TT], r01[:, 0:2 * TT], AF.Exp, [B("r01")], [B("r01")], scale=-1.0)
            tt("dve", t01[:, 0:2 * TT], pbank[2][:, 0:2 * TT], r01[:, 0:2 * TT], ALU.mult, PB(2) + [B("r01")], [B("t01")])
            stt(odr[:, 0:TT], t01[:, TT:2 * TT], nlam[:, l:l + 1], t01[:, 0:TT], ALU.mult, ALU.add, [B("t01"), B("nlam")], [B("odr")])

            def att_epilogue(h=h):
                rms_stats([(odr[:, 0:TT], [B("odr")])], 128, TT, None, bank=3)
                stt(mix3[:, 4 + h, :], odr[:, 0:TT], difn[:, l:l + 1], rs[:, 0:TT], ALU.mult, ALU.mult,
                    [B("odr"), B("difn"), B("rs")], [Bmix])
            deferred.append(att_epilogue)
        while deferred:
            deferred.pop(0)()

        if DBG["stop"] <= 4:
            return
        def sq_chunk(ap, rb):
            q = sqb[rot["sq"] % 2]
            qb = B("sqb%d" % (rot["sq"] % 2))
            rot["sq"] += 1
            act(q[:, 0:TT], ap, AF.Square, rb, [qb])
            return q, qb

        def stat_mm(bank, q, qb, i, n):
            mm(pbank[bank][:, 0:TT], onesb[:], q[:, 0:TT], i == 0, i == n - 1, [Bo, qb], PB(bank), inc=True)

        def dense_fm(nslab, mper, kchunks, src3, Bsrc, kcols):
            pend = None
            for s in range(nslab):
                slab, sbf = ss.get()
                sl3 = slab[:, 0:kchunks * kcols].rearrange("p (c n) -> p c n", c=kchunks)
                for mi in range(mper):
                    mch = s * mper + mi
                    b, hf = acc_next()
                    for kc in range(kchunks):
                        mm(pslice(b, hf, 128, TT), sl3[:, kc, mi * 128:(mi + 1) * 128], src3[:, kc, :], kc == 0,
                           kc == kchunks - 1, [sbf, Bsrc], [pbuf[b][hf]])
                    if pend is not None:
                        stat_mm(6, pend[0], pend[1], pend[2], 8)
                    Bm_ = Bmo if mch < 5 else B("mo2")
                    act(mo3[:, mch, :], pslice(b, hf, 128, TT), AF.Copy, [pbuf[b][hf]], [Bm_])
                    q, qb = sq_chunk(mo3[:, mch, :], [Bm_])
                    pend = (q, qb, mch)
            stat_mm(6, pend[0], pend[1], pend[2], 8)

        def post_norm_residual(kind, fuse_next):
            act(lnt[:, 0:TT], pbank[6][:, 0:TT], AF.Ln, PB(6), [B("lnt")], scale=1.0 / D, bias=EPS)
            act(rs[:, 0:TT], lnt[:, 0:TT], AF.Exp, [B("lnt")], [B("rs")], scale=-0.5)
            rsb = bass.AP(rs, 0, [[TTM, 128], [0, 5], [1, TT]])
            tt("dve", mo3[:, 0:5, :], mo3[:, 0:5, :], rsb, ALU.mult, [Bmo, B("rs")], [Bmo])
            rsb2 = bass.AP(rs, 0, [[TTM, 128], [0, 3], [1, TT]])
            tt("pool", mo3[:, 5:8, :], mo3[:, 5:8, :], rsb2, ALU.mult, [B("mo2"), B("rs")], [B("mo2")])
            pend = None
            for c in range(8):
                stt(xT3[:, c, :], mo3[:, c, :], lnp(kind, c), xT3[:, c, :], ALU.mult, ALU.add,
                    [Bmo if c < 5 else B("mo2"), B("lnv"), Bx], [Bx])
                if fuse_next:
                    if pend is not None:
                        stat_mm(7, pend[0], pend[1], pend[2], 8)
                    q, qb = sq_chunk(xT3[:, c, :], [Bx])
                    pend = (q, qb, c)
            if fuse_next:
                stat_mm(7, pend[0], pend[1], pend[2], 8)

        def pre_norm(kind, have_stats):
            if have_stats:
                act(lnt[:, 0:TT], pbank[7][:, 0:TT], AF.Ln, PB(7), [B("lnt")], scale=1.0 / D, bias=EPS)
                act(rs[:, 0:TT], lnt[:, 0:TT], AF.Exp, [B("lnt")], [B("rs")], scale=-0.5)
            else:
                rms_stats([(xT3[:, c, :], [Bx]) for c in range(8)], D, TT, None)
            for c in range(8):
                stt(hT3[:, c, :], xT3[:, c, :], lnp(kind, c), rs[:, 0:TT], ALU.mult, ALU.mult, [Bx, B("lnv"), B("rs")], [Bh])

        dense_fm(2, 4, 8, mix3, Bmix, 512)
        post_norm_residual(1, True)

        if DBG["stop"] <= 5:
            return
        pre_norm(2, True)
        if l == 1:
            sg.prefetch()
        hist3 = v3(hist[l][:], 2)

        def Bh_(jj):
            return B("hist%d_%d" % (l, jj))

        def cw(tp, j):
            i = (l * 3 + tp) * 44 + j
            return convw[:, i:i + 1]

        ffn_slabs = {}

        def stage_a(p):
            s_ = p // 2
            pi = p % 2
            if pi == 0:
                ffn_slabs[s_] = ss.get()
            slab, sbf = ffn_slabs[s_]
            sl3 = slab[:, 0:4096].rearrange("p (c n) -> p c n", c=16)
            e = p % NE
            for gu, (ex, nm, jj) in enumerate(((extg[e], "extg", p), (extu[e], "extu", 22 + p))):
                b, hf = acc_next()
                for kc in range(8):
                    mm(pslice(b, hf, 128, TT), sl3[:, gu * 8 + kc, pi * 128:(pi + 1) * 128], hT3[:, kc, :], kc == 0, kc == 7,
                       [sbf, Bh], [pbuf[b][hf]])
                cp("pool", ex[:, 0:2], hist3[:, :, jj], [Bh_(jj)], [B("%sh%d" % (nm, e))])
                act(ex[:, 2:2 + TT], pslice(b, hf, 128, TT), AF.Copy, [pbuf[b][hf]], [B("%sm%d" % (nm, e))])

        def stage_b1(p):
            e, c_ = p % NE, p % NC_
            for (ex, nm, jj, ct, cn) in ((extg[e], "extg", p, cg[c_], "cg"), (extu[e], "extu", 22 + p, cu[c_], "cu")):
                Bexh, Bexm, Bct = B("%sh%d" % (nm, e)), B("%sm%d" % (nm, e)), B("%s%d" % (cn, c_))
                act(ct[:, 0:TT], ex[:, 0:TT], AF.Identity, [Bexh, Bexm, B("convw"), B("convb")], [Bct], scale=cw(0, jj),
                    bias=convb[:, l * 44 + jj:l * 44 + jj + 1])

        def stage_b2(p):
            e, c_ = p % NE, p % NC_
            for (ex, nm, jj, ct, cn) in ((extg[e], "extg", p, cg[c_], "cg"), (extu[e], "extu", 22 + p, cu[c_], "cu")):
                Bexh, Bexm, Bct = B("%sh%d" % (nm, e)), B("%sm%d" % (nm, e)), B("%s%d" % (cn, c_))
                stt(ct[:, 0:TT], ex[:, 1:1 + TT], cw(1, jj), ct[:, 0:TT], ALU.mult, ALU.add, [Bexh, Bexm, Bct, B("convw")], [Bct])
                stt(ct[:, 0:TT], ex[:, 2:2 + TT], cw(2, jj), ct[:, 0:TT], ALU.mult, ALU.add, [Bexm, Bct, B("convw")], [Bct])
                cp("pool", hist3[:, :, jj], ex[:, TT:TT + 2], [Bexm], [Bh_(jj)])

        def stage_c1(p):
            c_, k2 = p % NC_, p % 2
            act(f1[k2][:, 0:TT], cg[c_][:, 0:TT], AF.Square, [B("cg%d" % c_)], [B("f1%d" % k2)])
            ts("pool", f1[k2][:, 0:TT], f1[k2][:, 0:TT], 0.044715, 1.0, ALU.mult, ALU.add, [B("f1%d" % k2)], [B("f1%d" % k2)])
            tt("pool", f1[k2][:, 0:TT], f1[k2][:, 0:TT], cg[c_][:, 0:TT], ALU.mult, [B("f1%d" % k2), B("cg%d" % c_)], [B("f1%d" % k2)])

        def stage_c2(p):
            c_, k2 = p % NC_, p % 2
            act(f2[k2][:, 0:TT], f1[k2][:, 0:TT], AF.Sigmoid, [B("f1%d" % k2)], [B("f2%d" % k2)], scale=1.5957691216057308)
            tt("pool" if p % 2 == 0 else "dve", f2[k2][:, 0:TT], f2[k2][:, 0:TT], cu[c_][:, 0:TT], ALU.mult,
               [B("f2%d" % k2), B("cu%d" % c_)], [B("f2%d" % k2)])

        def stage_c3(p):
            c_, k2 = p % NC_, p % 2
            tt("dve", yT3[:, p, :], f2[k2][:, 0:TT], cg[c_][:, 0:TT], ALU.mult, [B("f2%d" % k2), B("cg%d" % c_)], [B("yT")])

        stages = (stage_a, stage_b1, stage_b2, stage_c1, stage_c2, stage_c3)
        for it_ in range(22 + len(stages) - 1):
            for si_, fn_ in enumerate(stages):
                p_ = it_ - si_
                if 0 <= p_ < 22:
                    fn_(p_)
        dense_fm(4, 2, 22, yT3, B("yT"), 256)
        post_norm_residual(3, l == 0)

    def make_seg(name, T, TT, P, x, y, kout, vout, gout, cout, kt, vsc):
        sg = Seg()
        sg.name, sg.T, sg.TT, sg.P = name, T, TT, P
        sg.x, sg.y, sg.kout, sg.vout, sg.gout, sg.cout, sg.kt, sg.vsc = x, y, kout, vout, gout, cout, kt, vsc
        sg.extra = {}
        return sg

    segp = make_seg("p", TP, TTP, 0, d_xp, o_yp, o_kp, o_vp, o_gp, o_cp, ktp, vsp)
    segs = make_seg("s", TS, TS, PS, d_xs, o_ys, o_ks, o_vs, o_gs, o_cs, kts, vss)

    def init_prompt(l):
        fw.op("dve", lambda e: e.memset(S[l][:], 0.0), [], [B("S%d" % l)])
        fw.op("dve", lambda e: e.memset(Sb[l][:], 0.0), [], [B("Sb%d" % l)])
        fw.op("dve", lambda e: e.memset(hist[l][:], 0.0), [], [B("hist%d_%d" % (l, jj)) for jj in range(44)])

    def init_sample(l):
        fw.dma("sp", v3(S[l][:], 4), d_sg.ap()[l].rearrange("h k v -> k h v"), writes=[B("S%d" % l)])
        cp("dve", Sb[l][:], S[l][:], [B("S%d" % l)], [B("Sb%d" % l)])
        load_cols(d_sc.ap()[l].rearrange("r (c p) -> (r c) p", p=128), 88, 128, hist[l][:], [B("hist%d_%d" % (l, jj)) for jj in range(44)])
        Bkt = [B("skt%d_%d" % (l, h_)) for h_ in range(4)]
        Bvs = [B("svs%d_%d" % (l, b_)) for b_ in range(2)]
        nb = PS // 128
        for h in range(4):
            fw.dma("sp", ckst[:, 0:nb * 128].rearrange("p (b d) -> p b d", d=128),
                   d_ck.ap()[l, h].rearrange("(b p) d -> p b d", p=128), writes=[B("kvoK")])
            for b0 in range(0, nb, 4):
                for b_ in range(b0, min(nb, b0 + 4)):
                    tr(pbank[7][:, (b_ - b0) * 128:(b_ - b0 + 1) * 128], ckst[:, b_ * 128:(b_ + 1) * 128], ident,
                       [B("kvoK"), Bc], PB(7), inc=(b_ == min(nb, b0 + 4) - 1))
                nn = min(nb, b0 + 4) - b0
                act(ktc[:, b0 * 128:(b0 + nn) * 128], pbank[7][:, 0:nn * 128], AF.Copy, PB(7), [B("vst")])
            fw.dma("pool", kts.ap()[l, h, :, 0:PS], ktc[:, 0:PS], reads=[B("vst")], writes=[Bkt[h]])
            for r0_ in range(0, PS, 256):
                bx_ = B("scv%d_%d_%d" % (l, h, r0_))
                segs.extra.setdefault(l, []).append(bx_)
                fw.dma("pool", vss.ap()[l, h, r0_:r0_ + 256, :], d_cv.ap()[l, h, r0_:r0_ + 256, :], writes=[bx_])

    segp.init_state = init_prompt
    segs.init_state = init_sample

    for sg_ in (segs, segp):
        for it_ in range(sg_.T // sg_.TT):
            for l_ in range(2):
                plan_tile_layer(l_)
    if "s" in DBG["segs"]:
        run_segment(segs)
    if "p" in DBG["segs"]:
        run_segment(segp)

    fw.finish([B("outs")])
    fw.emit()
    build_nc.ninst = fw.ninst


_NC_CACHE = {}


def kernel(x_prompt, x_sample, cache_k, cache_v, state_gla, state_conv, t5_table,
           w_in, w_gate_up, b_gate_up, gla_norm, lam_params, diff_norm, w_out,
           ln_mix_pre, ln_mix_post, ln_ffn_pre, ln_ffn_post,
           w_ffn_up, conv_w, conv_b, w_ffn_down):
    f = lambda a: np.ascontiguousarray(np.asarray(a, dtype=np.float32))
    x_prompt, x_sample = f(x_prompt), f(x_sample)
    BATCH, TP = x_prompt.shape[0], x_prompt.shape[1]
    NS, TS = x_sample.shape[0], x_sample.shape[1]
    PS = cache_k.shape[3]
    key = (TP, TS, PS)
    if key not in _NC_CACHE:
        _NC_CACHE[key] = build_nc(TP=TP, TS=TS, PS=PS)
    nc = _NC_CACHE[key]
    shared = {
        "t5": f(t5_table), "w_in": f(w_in), "w_gate_up": f(w_gate_up), "b_gate_up": f(b_gate_up),
        "gla_norm": f(gla_norm), "lam_params": f(lam_params), "diff_norm": f(diff_norm), "w_out": f(w_out),
        "ln_mix_pre": f(ln_mix_pre), "ln_mix_post": f(ln_mix_post), "ln_ffn_pre": f(ln_ffn_pre),
        "ln_ffn_post": f(ln_ffn_post), "w_ffn_up": f(w_ffn_up), "conv_w": f(conv_w), "conv_b": f(conv_b),
        "w_ffn_down": f(w_ffn_down), "consts": _consts(),
    }
    cache_k, cache_v, state_gla, state_conv = f(cache_k), f(cache_v), f(state_gla), f(state_conv)
    in_maps = []
    for c in range(NCORES):
        bp = c % BATCH
        bs = c % NS
        m = dict(shared)
        m["xp"] = x_prompt[bp]
        m["xs"] = x_sample[bs]
        m["ck"] = np.ascontiguousarray(cache_k[:, bs]); m["cv"] = np.ascontiguousarray(cache_v[:, bs])
        m["sg"] = np.ascontiguousarray(state_gla[:, bs]); m["sc"] = np.ascontiguousarray(state_conv[:, bs])
        in_maps.append(m)
    res = run_bass_kernel_spmd(nc, in_maps, core_ids=list(range(NCORES)))
    R = res.results
    y_prompt = np.stack([R[b]["yp"] for b in range(BATCH)])
    y_sample = np.stack([R[b]["ys"] for b in range(NS)])
    stk = lambda name, n: np.stack([R[b][name] for b in range(n)], axis=1)
    return (y_prompt, y_sample, stk("kp", BATCH), stk("vp", BATCH), stk("gp", BATCH), stk("cp", BATCH),
            stk("ks", NS), stk("vs", NS), stk("gs", NS), stk("cs", NS))
```
